# Optimizing a Trainium2 kernel written in Bass

```python
import math
import jax, jax.numpy as jnp
from jax import lax
import numpy as np

D_MODEL = 1024
BATCH = 8
SEQ = 2048
DEPTH = 4

CHUNK = 64
N_MIXERS = 3
EPS = 1e-6
FFN_DIM = 4 * D_MODEL
CONV_KERNEL = 31
SSM_D_INNER = 2 * D_MODEL
SSM_HEAD_DIM = 64
SSM_HEADS = SSM_D_INNER // SSM_HEAD_DIM
SSM_GROUPS = 8
SSM_STATE = 128
SSM_CONV = 4
SSM_CONV_DIM = SSM_D_INNER + 2 * SSM_GROUPS * SSM_STATE
SSM_IN_DIM = SSM_D_INNER + SSM_CONV_DIM + SSM_HEADS
MLA_HEADS = D_MODEL // 64
MLA_NOPE = 64
MLA_ROPE = 32
MLA_V = 64
MLA_Q_LORA = 3 * D_MODEL // 8
MLA_KV_LORA = D_MODEL // 4
MLA_IN_DIM = MLA_Q_LORA + MLA_KV_LORA + MLA_ROPE
ROPE_BASE = 10000.0
Q_BLOCK = 128
N_CONV_LAYERS = (DEPTH + 2) // 3
N_SSM_LAYERS = (DEPTH + 1) // 3
N_MLA_LAYERS = DEPTH // 3

kernel_name = "hybrid_conformer_ssd_mla_trunk"


def rms_norm(x, w):
    xf = x.astype(jnp.float32)
    y = xf * lax.rsqrt(jnp.mean(xf * xf, axis=-1, keepdims=True) + EPS)
    return (y * w.astype(jnp.float32)).astype(x.dtype)


def layer_norm(x, g, b):
    xf = x.astype(jnp.float32)
    mu = jnp.mean(xf, axis=-1, keepdims=True)
    var = jnp.mean(jnp.square(xf - mu), axis=-1, keepdims=True)
    y = (xf - mu) * lax.rsqrt(var + EPS)
    return (y * g.astype(jnp.float32) + b.astype(jnp.float32)).astype(x.dtype)


def causal_dwconv(x, w, b):
    k = w.shape[0]
    xp = jnp.pad(x, ((0, 0), (k - 1, 0), (0, 0)))
    y = lax.conv_general_dilated(xp, w[:, None, :].astype(x.dtype), (1,), 'VALID',
                                 dimension_numbers=('NWC', 'WIO', 'NWC'),
                                 feature_group_count=x.shape[-1])
    return y + b.astype(x.dtype)


def apply_rope(x, positions):
    half = x.shape[-1] // 2
    inv = ROPE_BASE ** (-jnp.arange(half, dtype=jnp.float32) / half)
    ang = positions.astype(jnp.float32)[..., None] * inv
    cos = jnp.cos(ang)[:, :, None, :].astype(x.dtype)
    sin = jnp.sin(ang)[:, :, None, :].astype(x.dtype)
    x1, x2 = x[..., :half], x[..., half:]
    return jnp.concatenate([x1 * cos - x2 * sin, x2 * cos + x1 * sin], axis=-1)


def conformer_conv(h, w_pw1, b_pw1, w_dw, b_dw, ln_g, ln_b, w_pw2, b_pw2):
    u = h @ w_pw1 + b_pw1
    u = u[..., :D_MODEL] * jax.nn.sigmoid(u[..., D_MODEL:])
    u = causal_dwconv(u, w_dw, b_dw)
    u = jax.nn.silu(layer_norm(u, ln_g, ln_b))
    return u @ w_pw2 + b_pw2


def ssd_chunked(xs, dt, a, bm, cm):
    b, s, h, p = xs.shape
    g, n = bm.shape[2], bm.shape[3]
    r = h // g
    c = s // CHUNK
    xdt = (xs * dt[..., None]).reshape(b, c, CHUNK, g, r, p)
    a_dt = (dt * a).reshape(b, c, CHUNK, g, r).transpose(0, 1, 3, 4, 2)
    a_cs = jnp.cumsum(a_dt, axis=-1)
    bc = bm.reshape(b, c, CHUNK, g, n)
    cc = cm.reshape(b, c, CHUNK, g, n)
    tril = jnp.tril(jnp.ones((CHUNK, CHUNK), dtype=bool))
    seg = a_cs[..., :, None] - a_cs[..., None, :]
    decay_in = jnp.exp(jnp.where(tril, seg, -jnp.inf))
    cb = jnp.einsum('bclgn,bcsgn->bcgls', cc, bc)
    y_diag = jnp.einsum('bcgls,bcgrls,bcsgrp->bclgrp', cb, decay_in, xdt)
    decay_to_end = jnp.exp(a_cs[..., -1:] - a_cs)
    states = jnp.einsum('bclgn,bcgrl,bclgrp->bcgrpn', bc, decay_to_end, xdt)
    chunk_decay = jnp.exp(a_cs[..., -1])

    def step(carry, inp):
        st, dec = inp
        return carry * dec[..., None, None] + st, carry

    init = jnp.zeros((b, g, r, p, n), jnp.float32)
    _, prev = lax.scan(step, init, (jnp.moveaxis(states, 1, 0), jnp.moveaxis(chunk_decay, 1, 0)))
    prev = jnp.moveaxis(prev, 0, 1)
    y_off = jnp.einsum('bclgn,bcgrpn,bcgrl->bclgrp', cc, prev, jnp.exp(a_cs))
    return (y_diag + y_off).reshape(b, s, h, p)


def mamba2_ssd(h, w_in, conv_w, conv_b, dt_bias, a_log, d_skip, norm_w, w_out):
    b, s, _ = h.shape
    proj = h @ w_in
    z = proj[..., :SSM_D_INNER]
    xbc = proj[..., SSM_D_INNER:SSM_D_INNER + SSM_CONV_DIM]
    dt = proj[..., SSM_D_INNER + SSM_CONV_DIM:]
    xbc = jax.nn.silu(causal_dwconv(xbc, conv_w, conv_b))
    gn = SSM_GROUPS * SSM_STATE
    xs = xbc[..., :SSM_D_INNER].reshape(b, s, SSM_HEADS, SSM_HEAD_DIM).astype(jnp.float32)
    bm = xbc[..., SSM_D_INNER:SSM_D_INNER + gn].reshape(b, s, SSM_GROUPS, SSM_STATE).astype(jnp.float32)
    cm = xbc[..., SSM_D_INNER + gn:].reshape(b, s, SSM_GROUPS, SSM_STATE).astype(jnp.float32)
    dt = jax.nn.softplus(dt.astype(jnp.float32) + dt_bias.astype(jnp.float32))
    a = -jnp.exp(a_log.astype(jnp.float32))
    y = ssd_chunked(xs, dt, a, bm, cm) + d_skip.astype(jnp.float32)[:, None] * xs
    y = y.reshape(b, s, SSM_D_INNER) * jax.nn.silu(z.astype(jnp.float32))
    y = y.reshape(b, s, SSM_GROUPS, SSM_D_INNER // SSM_GROUPS)
    y = y * lax.rsqrt(jnp.mean(y * y, axis=-1, keepdims=True) + EPS)
    y = y.reshape(b, s, SSM_D_INNER) * norm_w.astype(jnp.float32)
    return y.astype(h.dtype) @ w_out


def mla_attention(h, positions, w_in, q_norm, w_uq, kv_norm, w_ukv, w_o):
    b, s, _ = h.shape
    lat = h @ w_in
    cq = rms_norm(lat[..., :MLA_Q_LORA], q_norm)
    ckv = rms_norm(lat[..., MLA_Q_LORA:MLA_Q_LORA + MLA_KV_LORA], kv_norm)
    k_pe = apply_rope(lat[..., MLA_Q_LORA + MLA_KV_LORA:][:, :, None, :], positions)[:, :, 0, :]
    q = (cq @ w_uq).reshape(b, s, MLA_HEADS, MLA_NOPE + MLA_ROPE)
    q_nope = q[..., :MLA_NOPE]
    q_pe = apply_rope(q[..., MLA_NOPE:], positions)
    kv = (ckv @ w_ukv).reshape(b, s, MLA_HEADS, MLA_NOPE + MLA_V)
    k_nope = kv[..., :MLA_NOPE]
    v = kv[..., MLA_NOPE:]
    nb = s // Q_BLOCK

    def to_blocks(t):
        return jnp.moveaxis(t.reshape(b, nb, Q_BLOCK, MLA_HEADS, t.shape[-1]), 1, 0)

    key_chunk = jnp.arange(s) // CHUNK
    scale = (MLA_NOPE + MLA_ROPE) ** -0.5

    def attend(args):
        qn, qp, blk = args
        q_chunk = (blk * Q_BLOCK + jnp.arange(Q_BLOCK)) // CHUNK
        sc = jnp.einsum('bqhd,bkhd->bhqk', qn, k_nope) + jnp.einsum('bqhd,bkd->bhqk', qp, k_pe)
        sc = sc.astype(jnp.float32) * scale
        sc = jnp.where(key_chunk[None, :] <= q_chunk[:, None], sc, -jnp.inf)
        pr = jax.nn.softmax(sc, axis=-1).astype(v.dtype)
        return jnp.einsum('bhqk,bkhd->bqhd', pr, v)

    out = lax.map(attend, (to_blocks(q_nope), to_blocks(q_pe), jnp.arange(nb)))
    out = jnp.moveaxis(out, 0, 1).reshape(b, s, MLA_HEADS * MLA_V)
    return out @ w_o


def sqrelu_mlp(h, w1, w2):
    return jnp.square(jax.nn.relu(h @ w1)) @ w2


def setup_inputs(seed: int = 0) -> dict:
    key = jax.random.key(seed)
    ks = iter(jax.random.split(key, 40))
    f32 = jnp.float32

    def nrm(shape, scale):
        return jax.random.normal(next(ks), shape, f32) * scale

    def gain(shape):
        return 1.0 + nrm(shape, 0.1)

    nc, ns, nm = N_CONV_LAYERS, N_SSM_LAYERS, N_MLA_LAYERS
    x = nrm((BATCH, SEQ, D_MODEL), 1.0)
    offs = jax.random.randint(next(ks), (BATCH, 1), 0, 16) * CHUNK
    positions = (offs + jnp.arange(SEQ, dtype=jnp.int32)[None, :]).astype(jnp.int32)
    dt0 = jnp.exp(jax.random.uniform(next(ks), (ns, SSM_HEADS), f32, math.log(1e-3), math.log(1e-1)))
    return {
        "x": x,
        "positions": positions,
        "norm_mix_pre": gain((DEPTH, D_MODEL)),
        "norm_mix_post": gain((DEPTH, D_MODEL)),
        "norm_ffn_pre": gain((DEPTH, D_MODEL)),
        "norm_ffn_post": gain((DEPTH, D_MODEL)),
        "ffn_w_in": nrm((DEPTH, D_MODEL, FFN_DIM), D_MODEL ** -0.5),
        "ffn_w_out": nrm((DEPTH, FFN_DIM, D_MODEL), FFN_DIM ** -0.5),
        "conv_w_pw1": nrm((nc, D_MODEL, 2 * D_MODEL), D_MODEL ** -0.5),
        "conv_b_pw1": nrm((nc, 2 * D_MODEL), 0.01),
        "conv_w_dw": nrm((nc, CONV_KERNEL, D_MODEL), CONV_KERNEL ** -0.5),
        "conv_b_dw": nrm((nc, D_MODEL), 0.01),
        "conv_ln_g": gain((nc, D_MODEL)),
        "conv_ln_b": nrm((nc, D_MODEL), 0.01),
        "conv_w_pw2": nrm((nc, D_MODEL, D_MODEL), D_MODEL ** -0.5),
        "conv_b_pw2": nrm((nc, D_MODEL), 0.01),
        "ssm_w_in": nrm((ns, D_MODEL, SSM_IN_DIM), D_MODEL ** -0.5),
        "ssm_conv_w": nrm((ns, SSM_CONV, SSM_CONV_DIM), SSM_CONV ** -0.5),
        "ssm_conv_b": nrm((ns, SSM_CONV_DIM), 0.01),
        "ssm_dt_bias": dt0 + jnp.log(-jnp.expm1(-dt0)),
        "ssm_a_log": jnp.log(jax.random.uniform(next(ks), (ns, SSM_HEADS), f32, 1.0, 16.0)),
        "ssm_d": gain((ns, SSM_HEADS)),
        "ssm_norm_w": gain((ns, SSM_D_INNER)),
        "ssm_w_out": nrm((ns, SSM_D_INNER, D_MODEL), SSM_D_INNER ** -0.5),
        "mla_w_in": nrm((nm, D_MODEL, MLA_IN_DIM), D_MODEL ** -0.5),
        "mla_q_norm": gain((nm, MLA_Q_LORA)),
        "mla_w_uq": nrm((nm, MLA_Q_LORA, MLA_HEADS * (MLA_NOPE + MLA_ROPE)), MLA_Q_LORA ** -0.5),
        "mla_kv_norm": gain((nm, MLA_KV_LORA)),
        "mla_w_ukv": nrm((nm, MLA_KV_LORA, MLA_HEADS * (MLA_NOPE + MLA_V)), MLA_KV_LORA ** -0.5),
        "mla_w_o": nrm((nm, MLA_HEADS * MLA_V, D_MODEL), (MLA_HEADS * MLA_V) ** -0.5),
    }


def reference(x, positions, norm_mix_pre, norm_mix_post, norm_ffn_pre, norm_ffn_post,
              ffn_w_in, ffn_w_out,
              conv_w_pw1, conv_b_pw1, conv_w_dw, conv_b_dw, conv_ln_g, conv_ln_b, conv_w_pw2, conv_b_pw2,
              ssm_w_in, ssm_conv_w, ssm_conv_b, ssm_dt_bias, ssm_a_log, ssm_d, ssm_norm_w, ssm_w_out,
              mla_w_in, mla_q_norm, mla_w_uq, mla_kv_norm, mla_w_ukv, mla_w_o):
    i_conv, i_ssm, i_mla = 0, 0, 0
    for i in range(DEPTH):
        kind = i % N_MIXERS
        h = rms_norm(x, norm_mix_pre[i])
        if kind == 0:
            j = i_conv
            y = conformer_conv(h, conv_w_pw1[j], conv_b_pw1[j], conv_w_dw[j], conv_b_dw[j],
                               conv_ln_g[j], conv_ln_b[j], conv_w_pw2[j], conv_b_pw2[j])
            i_conv += 1
        elif kind == 1:
            j = i_ssm
            y = mamba2_ssd(h, ssm_w_in[j], ssm_conv_w[j], ssm_conv_b[j], ssm_dt_bias[j],
                           ssm_a_log[j], ssm_d[j], ssm_norm_w[j], ssm_w_out[j])
            i_ssm += 1
        else:
            j = i_mla
            y = mla_attention(h, positions, mla_w_in[j], mla_q_norm[j], mla_w_uq[j],
                              mla_kv_norm[j], mla_w_ukv[j], mla_w_o[j])
            i_mla += 1
        x = x + rms_norm(y, norm_mix_post[i])
        h = rms_norm(x, norm_ffn_pre[i])
        x = x + rms_norm(sqrelu_mlp(h, ffn_w_in[i], ffn_w_out[i]), norm_ffn_post[i])
    return x
```

```python
import math
from contextlib import ExitStack
import numpy as np
import concourse.bass as bass
import concourse.mybir as mybir
from concourse.bass_utils import run_bass_kernel_spmd
from concourse.alu_op_type import AluOpType as ALU

AF = mybir.ActivationFunctionType
F32 = mybir.dt.float32
BF16 = mybir.dt.bfloat16
I32 = mybir.dt.int32

D = 1024
S = 2048
DEPTH = 4
EPS = 1e-6
TB = 512
NTB = S // TB


class H:
    __slots__ = ("name", "w", "r")

    def __init__(self, name=""):
        self.name = name
        self.w = None
        self.r = []


class Op:
    __slots__ = ("id", "eng", "kind", "payload", "inc", "deps", "dur", "succ", "nleft", "ready", "fin", "seq", "pos")


def _free_elems(ap):
    try:
        n = 1
        for d in ap.shape[1:]:
            n *= int(d)
        return n
    except Exception:
        return 512


class Prog:
    COMPUTE = ("pe", "act", "dve", "pool")
    ALL = ("pe", "act", "dve", "pool", "sp")
    LAT = 250.0

    def __init__(self, nc, ring=8):
        self.nc = nc
        self.ring_n = ring
        self.segs = [[]]
        self.cur_reorder = True
        self.seg_reorder = [True]

    def _new(self, eng, kind, payload, inc, reads, writes, dur):
        seg = self.segs[-1]
        o = Op()
        o.id = len(seg)
        o.eng, o.kind, o.payload, o.inc, o.dur = eng, kind, payload, inc, dur
        deps = {}
        sid = len(self.segs) - 1
        for h in reads:
            if h.w is not None and h.w[0] == sid:
                deps[h.w[1]] = True
        for h in writes:
            if h.w is not None and h.w[0] == sid:
                deps.setdefault(h.w[1], False)
            for (sg, r) in h.r:
                if sg == sid:
                    deps.setdefault(r, False)
        o.deps = deps
        seg.append(o)
        for h in reads:
            h.r.append((sid, o.id))
        for h in writes:
            h.w = (sid, o.id)
            h.r = []
        return o

    def op(self, eng, name, *args, reads=(), writes=(), inc=True, **kwargs):
        if eng == "pe":
            rhs = kwargs.get("rhs", kwargs.get("identity"))
            n = _free_elems(rhs) if rhs is not None else 128
            mult = 4.0 if (rhs is not None and rhs.dtype == F32) else 1.0
            dur = 35.0 + mult * max(64, n) / 2.4
        elif eng == "act":
            o_ = kwargs.get("out")
            dur = 200.0 + _free_elems(o_) / 1.2
        else:
            o_ = kwargs.get("out", args[0] if args else None)
            dur = 120.0 + _free_elems(o_) / 0.96
        return self._new(eng, "op", (name, args, kwargs), inc, reads, writes, dur)

    def dma(self, eng, out_ap, in_ap, reads=(), writes=(), **kw):
        try:
            nbytes = int(out_ap.shape[0]) * _free_elems(out_ap) * 4
        except Exception:
            nbytes = 1 << 20
        dur = 2000.0 + nbytes / 300.0
        return self._new(eng, "dma", (out_ap, in_ap, kw), True, reads, writes, dur)

    def barrier(self):
        if self.segs[-1]:
            self.segs.append([])
            self.seg_reorder.append(self.cur_reorder)
        else:
            self.seg_reorder[-1] = self.cur_reorder

    def wait_tok(self, eng, tok):
        pass

    def _schedule(self, seg, reorder=True):
        import heapq
        unit_of = [0] * len(seg)
        units = []
        open_pe = None
        for o in seg:
            if o.eng == "pe" and o.kind == "op":
                if open_pe is None:
                    open_pe = len(units)
                    units.append({"m": [], "eng": "pe", "dur": 0.0, "kind": "op"})
                u = units[open_pe]
                u["m"].append(o.id)
                u["dur"] += o.dur
                unit_of[o.id] = open_pe
                if o.inc:
                    open_pe = None
            else:
                unit_of[o.id] = len(units)
                units.append({"m": [o.id], "eng": o.eng, "dur": o.dur, "kind": o.kind})
        if open_pe is not None:
            seg[units[open_pe]["m"][-1]].inc = True
        nu = len(units)
        deps = [set() for _ in range(nu)]
        for o in seg:
            u = unit_of[o.id]
            for d in o.deps:
                du = unit_of[d]
                if du != u:
                    deps[u].add(du)
        succ = [[] for _ in range(nu)]
        nleft = [0] * nu
        ready = [0.0] * nu
        fin = [0.0] * nu
        for u in range(nu):
            nleft[u] = len(deps[u])
            for d in deps[u]:
                succ[d].append(u)
        if not (getattr(self, "reorder", True) and reorder):
            order = {e: [] for e in self.ALL}
            for u in range(nu):
                for oid in units[u]["m"]:
                    order[units[u]["eng"]].append(seg[oid])
            return order
        tail = [0.0] * nu
        for u in range(nu - 1, -1, -1):
            m_ = 0.0
            for su in succ[u]:
                v_ = tail[su] + (0.0 if units[su]["eng"] == units[u]["eng"] else self.LAT)
                if v_ > m_:
                    m_ = v_
            tail[u] = units[u]["dur"] + m_
        heaps = {e: [] for e in self.ALL}
        tnow = {e: 0.0 for e in self.ALL}
        for u in range(nu):
            if nleft[u] == 0:
                heapq.heappush(heaps[units[u]["eng"]], (0.0, u))
        order = {e: [] for e in self.ALL}
        done = 0
        while done < nu:
            best = None
            for e in self.ALL:
                hp = heaps[e]
                if not hp:
                    continue
                t = tnow[e]
                if hp[0][0] <= t:
                    tmp = []
                    while hp and hp[0][0] <= t:
                        tmp.append(heapq.heappop(hp))
                    cand = min(tmp, key=lambda it: (-tail[it[1]], it[1]))
                    for it in tmp:
                        heapq.heappush(hp, it)
                    key = (t, cand[1], e, cand)
                else:
                    key = (hp[0][0], hp[0][1], e, hp[0])
                if best is None or key[:2] < best[:2]:
                    best = key
            start, u, e, item = best
            hp = heaps[e]
            hp.remove(item)
            heapq.heapify(hp)
            un = units[u]
            fin[u] = start + un["dur"]
            tnow[e] = start + (60.0 if un["kind"] == "dma" else un["dur"])
            for oid in un["m"]:
                order[e].append(seg[oid])
            done += 1
            for su in succ[u]:
                lat = 0.0 if (units[su]["eng"] == e and un["kind"] != "dma") else self.LAT
                r = fin[u] + lat
                if r > ready[su]:
                    ready[su] = r
                nleft[su] -= 1
                if nleft[su] == 0:
                    heapq.heappush(heaps[units[su]["eng"]], (ready[su], su))
        return order

    def replay(self, block, sems):
        names = {"pe": "tensor", "act": "scalar", "dve": "vector", "pool": "gpsimd", "sp": "sync"}
        streams = {e: [] for e in self.ALL}
        cnt = {e: 0 for e in self.COMPUTE}
        waited = {e: {} for e in self.ALL}
        ring_idx = {e: 0 for e in ("sp", "act", "pool")}
        dma_uses = {}

        def emit_wait(e, k, v):
            if waited[e].get(k, 0) < v:
                streams[e].append(("wait", k, v))
                waited[e][k] = v

        for si_, seg in enumerate(self.segs):
            if not seg:
                continue
            order = self._schedule(seg, self.seg_reorder[si_])
            for e in self.COMPUTE:
                for o in reversed(order[e]):
                    if o.kind == "op":
                        o.inc = True
                        break
            tok = {}
            for e in self.ALL:
                c = cnt.get(e, 0)
                pend = []
                ri = ring_idx.get(e, 0)
                for o in order[e]:
                    if o.kind == "dma":
                        j = ri % self.ring_n
                        ri += 1
                        k = ("dma", e, j)
                        prev = dma_uses.get(k, 0)
                        dma_uses[k] = prev + 1
                        tok[o.id] = (k, 16 * (prev + 1))
                        o.seq = (k, prev)
                    else:
                        if o.inc:
                            c += 1
                            tok[o.id] = (("tl", e), c)
                            for q in pend:
                                tok[q] = (("tl", e), c)
                            pend = []
                        else:
                            pend.append(o.id)
                assert not pend
                if e in cnt:
                    cnt[e] = c
                if e in ring_idx:
                    ring_idx[e] = ri
            for e in self.ALL:
                for o in order[e]:
                    need = {}
                    for d, raw in o.deps.items():
                        po = seg[d]
                        k, v = tok[d]
                        if po.eng == e and po.kind != "dma":
                            if e == "pe":
                                continue
                        if need.get(k, 0) < v:
                            need[k] = v
                    if o.kind == "dma":
                        k, prev = o.seq
                        if prev > 0 and need.get(k, 0) < 16 * prev:
                            need[k] = 16 * prev
                    for k, v in need.items():
                        emit_wait(e, k, v)
                    if o.kind == "dma":
                        streams[e].append(("dma", o.payload, o.seq[0]))
                    else:
                        streams[e].append(("op", o.payload, o.inc))
            toks = [(("tl", e), cnt[e]) for e in self.COMPUTE if cnt[e] > 0]
            toks += [(k, 16 * n_) for k, n_ in dma_uses.items()]
            for e in self.ALL:
                for k, v in toks:
                    emit_wait(e, k, v)

        def make(ename):
            stream = streams[ename]

            def body(eng):
                for it in stream:
                    if it[0] == "wait":
                        eng.wait_ge(sems[it[1]], it[2])
                    elif it[0] == "op":
                        nm, a, kw = it[1]
                        ins = getattr(eng, nm)(*a, **kw)
                        if it[2]:
                            ins.then_inc(sems[("tl", ename)], 1)
                    else:
                        o, i, kw = it[1]
                        eng.dma_start(out=o, in_=i, **kw).then_inc(sems[it[2]], 16)
            return body

        self.n_instr = {e: len(streams[e]) for e in self.ALL}
        self.dbg_streams = streams
        semv = {}
        pc = {e: 0 for e in self.ALL}
        progress = True
        while progress:
            progress = False
            for e in self.ALL:
                st = streams[e]
                while pc[e] < len(st):
                    it = st[pc[e]]
                    if it[0] == "wait":
                        if semv.get(it[1], 0) < it[2]:
                            break
                    elif it[0] == "op":
                        if it[2]:
                            semv[("tl", e)] = semv.get(("tl", e), 0) + 1
                    else:
                        semv[it[2]] = semv.get(it[2], 0) + 16
                    pc[e] += 1
                    progress = True
        stuck = {e: (pc[e], len(streams[e]), streams[e][pc[e]][:3] if pc[e] < len(streams[e]) else None) for e in self.ALL}
        if any(pc[e] < len(streams[e]) for e in self.ALL):
            raise RuntimeError(f"semaphore program deadlocks: {stuck} sems={ {k: v for k, v in semv.items() if k[0] == 'tl'} }")
        for ename in self.ALL:
            if streams[ename]:
                getattr(block, names[ename])(make(ename))

    def sem_keys(self):
        keys = [("tl", e) for e in self.COMPUTE]
        for e in ("sp", "act", "pool"):
            for j in range(self.ring_n):
                keys.append(("dma", e, j))
        return keys


def tileA(W, k0, n0):
    blk = np.zeros((1024, 1024), np.float32)
    sub = W[k0:k0 + 1024, n0:n0 + 1024]
    blk[:sub.shape[0], :sub.shape[1]] = sub
    return blk.reshape(8, 128, 1024).transpose(1, 0, 2).reshape(128, 8192)


def tileB(W, n0, ncol, nk):
    sub = W[:nk * 128, n0:n0 + ncol]
    return np.ascontiguousarray(sub.reshape(nk, 128, ncol).transpose(1, 0, 2)).reshape(128, nk * ncol)


def colvec(v):
    n = v.shape[0] // 128
    return np.ascontiguousarray(v.reshape(n, 128).T)


class Layout:
    def __init__(self):
        self.tiles = []
        self.tile_ids = {}
        self.cols = []
        self.col_off = {}
        self.ncol = 0

    def add_tile(self, name, arr):
        assert arr.shape == (128, 8192), (name, arr.shape)
        self.tile_ids[name] = len(self.tiles)
        self.tiles.append(arr)

    def add_cols(self, name, arr):
        arr = np.asarray(arr, np.float32)
        assert arr.shape[0] == 128
        self.col_off[name] = (self.ncol, arr.shape[1])
        self.cols.append(arr)
        self.ncol += arr.shape[1]


def make_layout(inp):
    L = Layout()
    for i in range(DEPTH):
        for nm in ("norm_mix_pre", "norm_mix_post", "norm_ffn_pre", "norm_ffn_post"):
            L.add_cols(f"{nm}{i}", colvec(inp[nm][i]))
        w1 = inp["ffn_w_in"][i]
        w2 = inp["ffn_w_out"][i]
        for fg in range(4):
            L.add_tile(f"ffn{i}_w1_{fg}", tileA(w1, 0, fg * 1024))
        for og in range(4):
            L.add_tile(f"ffn{i}_w2_{og}", tileB(w2, og * 256, 256, 32))
    for j in range(2):
        w1 = inp["conv_w_pw1"][j]
        L.add_tile(f"conv{j}_pw1_a", tileA(w1, 0, 0))
        L.add_tile(f"conv{j}_pw1_b", tileA(w1, 0, 1024))
        L.add_tile(f"conv{j}_pw2", tileA(inp["conv_w_pw2"][j], 0, 0))
        L.add_cols(f"conv{j}_b_pw1", colvec(inp["conv_b_pw1"][j]))
        wdw = inp["conv_w_dw"][j]
        L.add_cols(f"conv{j}_w_dw", np.ascontiguousarray(wdw.reshape(31, 8, 128).transpose(2, 1, 0)).reshape(128, 8 * 31))
        L.add_cols(f"conv{j}_b_dw", colvec(inp["conv_b_dw"][j]))
        L.add_cols(f"conv{j}_ln_g", colvec(inp["conv_ln_g"][j]))
        L.add_cols(f"conv{j}_ln_b", colvec(inp["conv_ln_b"][j]))
        L.add_cols(f"conv{j}_b_pw2", colvec(inp["conv_b_pw2"][j]))
    wi_ = inp["ssm_w_in"][0]
    for j in range(2):
        L.add_tile(f"ssm_z_{j}", tileA(wi_, 0, j * 1024))
    for j in range(4):
        L.add_tile(f"ssm_xbc_{j}", tileA(wi_, 0, 2048 + j * 1024))
    L.add_tile("ssm_dt", tileA(wi_, 0, 6144))
    wo_ = inp["ssm_w_out"][0]
    for j in range(2):
        L.add_tile(f"ssm_out_{j}", tileA(wo_, j * 1024, 0))
    cw = inp["ssm_conv_w"][0]
    L.add_cols("ssm_conv_w", np.ascontiguousarray(cw.reshape(4, 32, 128).transpose(2, 1, 0)).reshape(128, 128))
    L.add_cols("ssm_conv_b", colvec(inp["ssm_conv_b"][0]))
    L.add_cols("ssm_dtb", np.broadcast_to(inp["ssm_dt_bias"][0][None, :], (128, 32)))
    L.add_cols("ssm_alog", np.broadcast_to(inp["ssm_a_log"][0][None, :], (128, 32)))
    L.add_cols("ssm_d", np.broadcast_to(inp["ssm_d"][0][None, :], (128, 32)))
    L.add_cols("ssm_norm_w", colvec(inp["ssm_norm_w"][0]))
    w_in = inp["mla_w_in"][0]
    w_in_ext = np.concatenate([w_in, w_in[:, 656:672], w_in[:, 640:656]], axis=1)
    L.add_tile("mla_in", tileA(w_in_ext, 0, 0))
    wuq = inp["mla_w_uq"][0]
    blocks = []
    for h in range(16):
        b = wuq[:, h * 96:(h + 1) * 96]
        blocks.append(np.concatenate([b, b[:, 80:96], b[:, 64:80]], axis=1))
    wuq_ext = np.concatenate(blocks, axis=1)
    t = np.zeros((128, 8192), np.float32)
    t[:, :3 * 2048] = tileB(wuq_ext, 0, 2048, 3)
    L.add_tile("mla_uq", t)
    t = np.zeros((128, 8192), np.float32)
    t[:, :2 * 2048] = tileB(inp["mla_w_ukv"][0], 0, 2048, 2)
    L.add_tile("mla_ukv", t)
    L.add_tile("mla_o", tileA(inp["mla_w_o"][0], 0, 0))
    L.add_cols("mla_q_norm", colvec(inp["mla_q_norm"][0]))
    L.add_cols("mla_kv_norm", colvec(inp["mla_kv_norm"][0]))
    return L


class K:
    pass


def build_program(L, stages):
    nc = bass.Bass("TRN2", target_bir_lowering=False)
    NT = len(L.tiles)
    NCV = L.ncol
    x_d = nc.dram_tensor("x", [D, S], F32, kind="ExternalInput").ap()
    w_d = nc.dram_tensor("wts", [NT, 128, 8192], F32, kind="ExternalInput").ap()
    cv_d = nc.dram_tensor("cvec", [128, NCV], F32, kind="ExternalInput").ap()
    cm_d = nc.dram_tensor("cmat", [128, 4, 128], F32, kind="ExternalInput").ap()
    cm32_d = nc.dram_tensor("cmat32", [128, 3, 128], F32, kind="ExternalInput").ap()
    pos_d = nc.dram_tensor("pos", [1, S], I32, kind="ExternalInput").ap()
    o_d = nc.dram_tensor("out", [D, S], F32, kind="ExternalOutput").ap()

    p = Prog(nc)
    import os
    p.reorder = os.environ.get("K_REORDER", "1") == "1"
    es = ExitStack()
    with es:
        sems = {k: es.enter_context(nc.semaphore("s_" + "_".join(map(str, k)))) for k in p.sem_keys()}
        x_sb = es.enter_context(nc.sbuf_tensor("x_sb", [128, 8, S], F32))
        wslot = [es.enter_context(nc.sbuf_tensor(f"wslot{i}", [128, 8192], BF16)) for i in range(3)]
        cvec = es.enter_context(nc.sbuf_tensor("cvec_sb", [128, NCV], F32))
        cmat = es.enter_context(nc.sbuf_tensor("cmat_sb", [128, 4, 128], BF16))
        cmat32 = es.enter_context(nc.sbuf_tensor("cmat32_sb", [128, 3, 128], F32))
        AW = 22016
        arena = es.enter_context(nc.sbuf_tensor("arena", [128, AW], F32))
        banks = [es.enter_context(nc.psum_tensor(f"bank{i}", [128, 512], F32)) for i in range(8)]
        block = es.enter_context(nc.Block())

        k = K()
        k.nc, k.p = nc, p
        hb = [H(f"bank{i}") for i in range(8)]
        hx = [H(f"x{tb}") for tb in range(NTB)]
        hslot = [H(f"slot{i}") for i in range(3)]
        hcv, hcm = H("cvec"), H("cmat")
        ident = cmat[:, 0, :]
        ones = cmat[:, 1, :]
        cmask = cmat[:, 2, :]
        tri_bf = cmat[:, 3, :]
        tri32 = cmat32[:, 0, :]
        mstrict32 = cmat32[:, 1, :]
        ones32 = cmat32[:, 2, :]

        apos = [0]

        def carve_f32(n):
            o = apos[0]
            apos[0] += n
            assert apos[0] <= AW, apos[0]
            return arena[:, o:o + n]

        def carve_bf(n):
            assert n % 2 == 0
            return carve_f32(n // 2).bitcast(BF16)

        def col(name, c=0, n=1):
            o, w = L.col_off[name]
            return cvec[:, o + c:o + c + n]

        bi = [0]

        def next_bank():
            i = bi[0] % 6
            bi[0] += 1
            return banks[i], hb[i]

        si = [0]

        def next_sbank():
            i = 6 + si[0] % 2
            si[0] += 1
            return banks[i], hb[i]

        wi = [0]

        def load_w(name):
            i = wi[0] % 3
            wi[0] += 1
            tid = L.tile_ids[name]
            dst = wslot[i][:].rearrange("p (a b) -> p a b", a=8)
            src = w_d[tid].rearrange("p (a b) -> p a b", a=8)
            p.dma("pool", dst, src, writes=[hslot[i]])
            return wslot[i], hslot[i]

        p.dma("sp", cvec[:], cv_d, writes=[hcv])
        p.dma("pool", cmat[:], cm_d, writes=[hcm])
        p.dma("sp", cmat32[:], cm32_d, writes=[hcm])
        xv = x_d.rearrange("(c q) t -> q c t", q=128)
        for tb in range(NTB):
            p.dma("sp" if tb % 2 == 0 else "act", x_sb[:, :, tb * TB:(tb + 1) * TB], xv[:, :, tb * TB:(tb + 1) * TB], writes=[hx[tb]])

        def rstd_from_sq(sq, hsq, nchunk, dim, rstd, hrstd, lnt, hlnt, n=512):
            sb, hsb = next_sbank()
            for c in range(nchunk):
                p.op("pe", "matmul", sb[:, 0:n], lhsT=ones, rhs=sq[:, c, :], start=(c == 0), stop=(c == nchunk - 1), reads=[hsq, hcm], writes=[hsb], inc=(c == nchunk - 1))
            p.op("act", "activation", out=lnt, in_=sb[:, 0:n], func=AF.Ln, scale=1.0 / dim, bias=col("eps"), reads=[hsb, hcv], writes=[hlnt])
            p.op("act", "activation", out=rstd, in_=lnt, func=AF.Exp, scale=-0.5, reads=[hlnt], writes=[hrstd])

        def prenorm(tb, wname, hbuf, hh, sq, hsq, rstd, hrstd, lnt, hlnt):
            xs = x_sb[:, :, tb * TB:(tb + 1) * TB]
            for c in range(8):
                p.op("act", "activation", out=sq[:, c, :], in_=xs[:, c, :], func=AF.Square, reads=[hx[tb]], writes=[hsq])
            rstd_from_sq(sq, hsq, 8, D, rstd, hrstd, lnt, hlnt)
            for c in range(8):
                p.op("dve", "scalar_tensor_tensor", out=hbuf[:, c, :], in0=xs[:, c, :], scalar=col(wname, c), in1=rstd, op0=ALU.mult, op1=ALU.mult, reads=[hx[tb], hrstd, hcv], writes=[hh])

        def postnorm(tb, wname, ysb, hysb, sq, hsq, rstd, hrstd, lnt, hlnt):
            xs = x_sb[:, :, tb * TB:(tb + 1) * TB]
            rstd_from_sq(sq, hsq, 8, D, rstd, hrstd, lnt, hlnt)
            for c in range(8):
                p.op("dve", "scalar_tensor_tensor", out=ysb[:, c, :], in0=ysb[:, c, :], scalar=col(wname, c), in1=rstd, op0=ALU.mult, op1=ALU.mult, reads=[hysb, hrstd, hcv], writes=[hysb])
                p.op("dve", "tensor_tensor", out=xs[:, c, :], in0=xs[:, c, :], in1=ysb[:, c, :], op=ALU.add, reads=[hysb, hx[tb]], writes=[hx[tb]])

        def evac_y(ps, hps, c, ysb, hysb, sq, hsq, bias=None):
            if bias is None:
                p.op("act", "activation", out=ysb[:, c, :], in_=ps[:], func=AF.Copy, reads=[hps], writes=[hysb])
                p.op("act", "activation", out=sq[:, c, :], in_=ps[:], func=AF.Square, reads=[hps], writes=[hsq])
            else:
                p.op("act", "activation", out=ysb[:, c, :], in_=ps[:], func=AF.Identity, bias=bias, reads=[hps, hcv], writes=[hysb])
                p.op("act", "activation", out=sq[:, c, :], in_=ps[:], func=AF.Square, bias=bias, reads=[hps, hcv], writes=[hsq])

        def ffn(i):
            apos[0] = 0
            hbuf = [carve_bf(8 * 512).rearrange("p (a b) -> p a b", a=8) for _ in range(2)]
            hh = [H("h0"), H("h1")]
            abuf = carve_bf(32 * 512).rearrange("p (a b) -> p a b", a=32)
            ha = H("a")
            ysb = carve_f32(8 * 512).rearrange("p (a b) -> p a b", a=8)
            hysb = H("ysb")
            sq = carve_bf(8 * 512).rearrange("p (a b) -> p a b", a=8)
            hsq = H("sq")
            rstd = [carve_f32(512) for _ in range(2)]
            hrstd = [H("rstd0"), H("rstd1")]
            lnt = carve_f32(512)
            hlnt = H("lnt")
            rt = [carve_f32(512) for _ in range(2)]
            hrt = [H("rt0"), H("rt1")]
            rti = 0
            for tb in range(NTB):
                hb_, hh_ = hbuf[tb % 2], hh[tb % 2]
                prenorm(tb, f"norm_ffn_pre{i}", hb_, hh_, sq, hsq, rstd[0], hrstd[0], lnt, hlnt)
                for fg in range(4):
                    ws, hws = load_w(f"ffn{i}_w1_{fg}")
                    wv = ws[:].rearrange("p (a b) -> p a b", a=8)
                    for fc in range(8):
                        ps, hps = next_bank()
                        for kk in range(8):
                            p.op("pe", "matmul", ps[:], lhsT=wv[:, kk, fc * 128:(fc + 1) * 128], rhs=hb_[:, kk, :], start=(kk == 0), stop=(kk == 7), reads=[hws, hh_], writes=[hps], inc=(kk == 7))
                        r_, hr_ = rt[rti % 2], hrt[rti % 2]
                        rti += 1
                        p.op("act", "activation", out=r_, in_=ps[:], func=AF.Relu, reads=[hps], writes=[hr_])
                        ac = fg * 8 + fc
                        p.op("dve", "tensor_tensor", out=abuf[:, ac, :], in0=r_, in1=ps[:], op=ALU.mult, reads=[hps, hr_], writes=[ha])
                for og in range(4):
                    ws, hws = load_w(f"ffn{i}_w2_{og}")
                    wv = ws[:].rearrange("p (a b) -> p a b", a=32)
                    for oc in range(2):
                        ps, hps = next_bank()
                        for kk in range(32):
                            p.op("pe", "matmul", ps[:], lhsT=wv[:, kk, oc * 128:(oc + 1) * 128], rhs=abuf[:, kk, :], start=(kk == 0), stop=(kk == 31), reads=[hws, ha], writes=[hps], inc=(kk == 31))
                        evac_y(ps, hps, og * 2 + oc, ysb, hysb, sq, hsq)
                postnorm(tb, f"norm_ffn_post{i}", ysb, hysb, sq, hsq, rstd[1], hrstd[1], lnt, hlnt)

        def conv_mixer(i, j):
            apos[0] = 0
            PAD = 32
            G = carve_bf(8 * (PAD + TB)).rearrange("p (a b) -> p a b", a=8)
            hG = H("G")
            hbuf = carve_bf(8 * 512).rearrange("p (a b) -> p a b", a=8)
            hh = H("h")
            cvb, hcvb = hbuf, hh
            sq = carve_bf(8 * 512).rearrange("p (a b) -> p a b", a=8)
            hsq = H("sq")
            cvf = carve_f32(8 * 512).rearrange("p (a b) -> p a b", a=8)
            hcvf = H("cvf")
            ysb, hysb = cvf, hcvf
            vbuf = carve_bf(8 * 512).rearrange("p (a b) -> p a b", a=8)
            hv = H("v")
            diag = [carve_bf(31 * 128).rearrange("p (a b) -> p a b", a=31) for _ in range(2)]
            hdiag = [H("diag0"), H("diag1")]
            rstd = carve_f32(512)
            hrstd = H("rstd")
            lnt = carve_f32(512)
            hlnt = H("lnt")
            sg = [carve_f32(512) for _ in range(2)]
            hsg = [H("sg0"), H("sg1")]
            msq = carve_f32(512)
            hmsq = H("msq")
            var = carve_f32(512)
            hvar = H("var")
            tmpf = [carve_f32(512) for _ in range(2)]
            htmp = [H("tmp0"), H("tmp1")]
            wdw = col(f"conv{j}_w_dw", 0, 8 * 31).rearrange("p (c k) -> p c k", c=8)

            p.op("dve", "memset", G[:, :, 0:PAD], 0.0, writes=[hG])
            wa, hwa = load_w(f"conv{j}_pw1_a")
            wb, hwb = load_w(f"conv{j}_pw1_b")
            w2, hw2 = load_w(f"conv{j}_pw2")
            wav = wa[:].rearrange("p (a b) -> p a b", a=8)
            wbv = wb[:].rearrange("p (a b) -> p a b", a=8)
            w2v = w2[:].rearrange("p (a b) -> p a b", a=8)
            di = 0
            for tb in range(NTB):
                if tb > 0:
                    p.op("dve", "tensor_copy", out=G[:, :, 0:PAD], in_=G[:, :, TB:TB + PAD], reads=[hG], writes=[hG])
                prenorm(tb, f"norm_mix_pre{i}", hbuf, hh, sq, hsq, rstd, hrstd, lnt, hlnt)
                for c in range(8):
                    psa, hpsa = next_bank()
                    for kk in range(8):
                        p.op("pe", "matmul", psa[:], lhsT=wav[:, kk, c * 128:(c + 1) * 128], rhs=hbuf[:, kk, :], start=(kk == 0), stop=(kk == 7), reads=[hwa, hh], writes=[hpsa], inc=(kk == 7))
                    psb, hpsb = next_bank()
                    for kk in range(8):
                        p.op("pe", "matmul", psb[:], lhsT=wbv[:, kk, c * 128:(c + 1) * 128], rhs=hbuf[:, kk, :], start=(kk == 0), stop=(kk == 7), reads=[hwb, hh], writes=[hpsb], inc=(kk == 7))
                    s_, hs_ = sg[c % 2], hsg[c % 2]
                    p.op("act", "activation", out=s_, in_=psb[:], func=AF.Sigmoid, bias=col(f"conv{j}_b_pw1", 8 + c), reads=[hpsb, hcv], writes=[hs_])
                    p.op("dve", "scalar_tensor_tensor", out=G[:, c, PAD:PAD + TB], in0=psa[:], scalar=col(f"conv{j}_b_pw1", c), in1=s_, op0=ALU.add, op1=ALU.mult, reads=[hpsa, hs_, hcv], writes=[hG])
                for c in range(8):
                    dg, hdg = diag[di % 2], hdiag[di % 2]
                    di += 1
                    p.op("dve", "tensor_tensor", out=dg, in0=ident.unsqueeze(1).to_broadcast([128, 31, 128]), in1=wdw[:, c, :].unsqueeze(2).to_broadcast([128, 31, 128]), op=ALU.mult, reads=[hcm, hcv], writes=[hdg])
                    ps, hps = next_bank()
                    for kk in range(31):
                        o = PAD - 30 + kk
                        p.op("pe", "matmul", ps[:], lhsT=dg[:, kk, :], rhs=G[:, c, o:o + TB], start=(kk == 0), stop=(kk == 30), reads=[hdg, hG], writes=[hps], inc=(kk == 30))
                    bdw = col(f"conv{j}_b_dw", c)
                    p.op("act", "activation", out=cvf[:, c, :], in_=ps[:], func=AF.Identity, bias=bdw, reads=[hps, hcv], writes=[hcvf])
                    p.op("act", "activation", out=cvb[:, c, :], in_=ps[:], func=AF.Identity, bias=bdw, reads=[hps, hcv], writes=[hcvb])
                    p.op("act", "activation", out=sq[:, c, :], in_=ps[:], func=AF.Square, bias=bdw, reads=[hps, hcv], writes=[hsq])
                sm, hsm = next_sbank()
                for c in range(8):
                    p.op("pe", "matmul", sm[:], lhsT=ones, rhs=cvb[:, c, :], start=(c == 0), stop=(c == 7), reads=[hcvb, hcm], writes=[hsm], inc=(c == 7))
                s2, hs2 = next_sbank()
                for c in range(8):
                    p.op("pe", "matmul", s2[:], lhsT=ones, rhs=sq[:, c, :], start=(c == 0), stop=(c == 7), reads=[hsq, hcm], writes=[hs2], inc=(c == 7))
                p.op("act", "activation", out=msq, in_=sm[:], func=AF.Square, scale=1.0 / D, reads=[hsm], writes=[hmsq])
                p.op("dve", "scalar_tensor_tensor", out=var, in0=s2[:], scalar=1.0 / D, in1=msq, op0=ALU.mult, op1=ALU.subtract, reads=[hs2, hmsq], writes=[hvar])
                p.op("act", "activation", out=lnt, in_=var, func=AF.Ln, bias=col("eps"), reads=[hvar, hcv], writes=[hlnt])
                p.op("act", "activation", out=rstd, in_=lnt, func=AF.Exp, scale=-0.5, reads=[hlnt], writes=[hrstd])
                for c in range(8):
                    t_, ht_ = tmpf[c % 2], htmp[c % 2]
                    p.op("dve", "scalar_tensor_tensor", out=t_, in0=sm[:], scalar=-1.0 / D, in1=cvf[:, c, :], op0=ALU.mult, op1=ALU.add, reads=[hsm, hcvf], writes=[ht_])
                    p.op("dve", "tensor_tensor", out=t_, in0=t_, in1=rstd, op=ALU.mult, reads=[ht_, hrstd], writes=[ht_])
                    p.op("act", "activation", out=vbuf[:, c, :], in_=t_, func=AF.Silu, scale=col(f"conv{j}_ln_g", c), bias=col(f"conv{j}_ln_b", c), reads=[ht_, hcv], writes=[hv])
                for oc in range(8):
                    ps, hps = next_bank()
                    for kk in range(8):
                        p.op("pe", "matmul", ps[:], lhsT=w2v[:, kk, oc * 128:(oc + 1) * 128], rhs=vbuf[:, kk, :], start=(kk == 0), stop=(kk == 7), reads=[hw2, hv], writes=[hps], inc=(kk == 7))
                    evac_y(ps, hps, oc, ysb, hysb, sq, hsq, bias=col(f"conv{j}_b_pw2", oc))
                postnorm(tb, f"norm_mix_post{i}", ysb, hysb, sq, hsq, rstd, hrstd, lnt, hlnt)


        def mla_mixer(i):
            SC = 96 ** -0.5
            R0, R1, R2 = 0, 8192, 16384
            P = slice(64, 96)
            PI = 3.1415925
            TWO_PI = 2.0 * math.pi
            C1 = 6.28125
            C2 = TWO_PI - C1

            def v3(ap, a):
                return ap.rearrange("p (a b) -> p a b", a=a)

            apos[0] = R1
            cqn = v3(carve_bf(3 * S), 3)
            hcqn = H("cqn")
            ckvn = v3(carve_bf(2 * S), 2)
            hckvn = H("ckvn")
            kpe = carve_bf(S)
            hkpe = H("kpe")
            cosT = carve_bf(S)
            sinT = carve_bf(S)
            htab = H("tab")
            assert apos[0] == R2
            w_in, hw_in = load_w("mla_in")
            w_uq, hw_uq = load_w("mla_uq")
            w_kv, hw_kv = load_w("mla_ukv")
            winv = v3(w_in[:], 8)
            wuqv = w_uq[:, 0:3 * 2048].rearrange("p (a b) -> p a b", a=3)
            wkvv = w_kv[:, 0:2 * 2048].rearrange("p (a b) -> p a b", a=2)

            p.barrier()
            apos[0] = R0
            posi = carve_f32(S).bitcast(I32)
            ang = carve_f32(S)
            t1 = carve_f32(S)
            nf = carve_f32(S)
            apos[0] = R2
            ni = carve_f32(S).bitcast(I32)
            hA = H("ropeA")
            p.dma("sp", posi[P, :], pos_d.partition_broadcast(32), writes=[hA])
            p.op("dve", "tensor_copy", out=t1[P, :], in_=posi[P, :], reads=[hA], writes=[hA])
            p.op("dve", "tensor_scalar", out=ang[P, :], in0=t1[P, :], scalar1=col("rope_inv")[P, :], scalar2=None, op0=ALU.mult, reads=[hA, hcv], writes=[hA])
            for which in ("sin", "cos"):
                if which == "cos":
                    p.op("dve", "tensor_scalar", out=ang[P, :], in0=ang[P, :], scalar1=0.5 * math.pi, scalar2=None, op0=ALU.add, reads=[hA], writes=[hA])
                p.op("dve", "tensor_scalar", out=t1[P, :], in0=ang[P, :], scalar1=1.0 / TWO_PI, scalar2=None, op0=ALU.mult, reads=[hA], writes=[hA])
                p.op("dve", "tensor_copy", out=ni[P, :], in_=t1[P, :], reads=[hA], writes=[hA])
                p.op("dve", "tensor_copy", out=nf[P, :], in_=ni[P, :], reads=[hA], writes=[hA])
                p.op("dve", "scalar_tensor_tensor", out=t1[P, :], in0=nf[P, :], scalar=-C1, in1=ang[P, :], op0=ALU.mult, op1=ALU.add, reads=[hA], writes=[hA])
                p.op("dve", "scalar_tensor_tensor", out=t1[P, :], in0=nf[P, :], scalar=-C2, in1=t1[P, :], op0=ALU.mult, op1=ALU.add, reads=[hA], writes=[hA])
                p.op("dve", "tensor_scalar", out=nf[P, :], in0=t1[P, :], scalar1=PI, scalar2=-TWO_PI, op0=ALU.is_gt, op1=ALU.mult, reads=[hA], writes=[hA])
                p.op("dve", "tensor_tensor", out=t1[P, :], in0=t1[P, :], in1=nf[P, :], op=ALU.add, reads=[hA], writes=[hA])
                p.op("dve", "tensor_scalar", out=nf[P, :], in0=t1[P, :], scalar1=-PI, scalar2=TWO_PI, op0=ALU.is_lt, op1=ALU.mult, reads=[hA], writes=[hA])
                p.op("dve", "tensor_tensor", out=t1[P, :], in0=t1[P, :], in1=nf[P, :], op=ALU.add, reads=[hA], writes=[hA])
                p.op("dve", "tensor_scalar", out=t1[P, :], in0=t1[P, :], scalar1=PI, scalar2=-PI, op0=ALU.min, op1=ALU.max, reads=[hA], writes=[hA])
                if which == "sin":
                    p.op("act", "activation", out=nf[P, :], in_=t1[P, :], func=AF.Sin, reads=[hA], writes=[hA])
                    p.op("dve", "tensor_scalar", out=sinT[P, :], in0=nf[P, :], scalar1=col("rope_sign")[P, :], scalar2=None, op0=ALU.mult, reads=[hA, hcv], writes=[htab])
                else:
                    p.op("act", "activation", out=cosT[P, :], in_=t1[P, :], func=AF.Sin, reads=[hA], writes=[htab])

            p.barrier()
            apos[0] = R0
            hbuf = v3(carve_bf(8 * 512), 8)
            hh = H("h")
            sq = v3(carve_bf(8 * 512), 8)
            hsq = H("sq")
            raw = v3(carve_f32(5 * 512), 5)
            hraw = H("raw")
            rstdq = carve_f32(512)
            hrq = H("rstdq")
            rstdk = carve_f32(512)
            hrk = H("rstdk")
            lnt = carve_f32(512)
            hlnt = H("lnt")
            apos[0] = R2
            rtA = carve_f32(512)
            rtB = carve_f32(512)
            hrt = H("ropetmp")
            for tb in range(NTB):
                tbs = slice(tb * TB, (tb + 1) * TB)
                prenorm(tb, f"norm_mix_pre{i}", hbuf, hh, sq, hsq, rstdq, hrq, lnt, hlnt)
                for m in range(5):
                    ps, hps = next_bank()
                    for kk in range(8):
                        p.op("pe", "matmul", ps[:], lhsT=winv[:, kk, m * 128:(m + 1) * 128], rhs=hbuf[:, kk, :], start=(kk == 0), stop=(kk == 7), reads=[hw_in, hh], writes=[hps], inc=(kk == 7))
                    p.op("act", "activation", out=raw[:, m, :], in_=ps[:], func=AF.Copy, reads=[hps], writes=[hraw])
                    p.op("act", "activation", out=sq[:, m, :], in_=ps[:], func=AF.Square, reads=[hps], writes=[hsq])
                psr, hpsr = next_bank()
                for kk in range(8):
                    p.op("pe", "matmul", psr[P, :], lhsT=winv[:, kk, 640:672], rhs=hbuf[:, kk, :], start=(kk == 0), stop=(kk == 7), reads=[hw_in, hh], writes=[hpsr], inc=(kk == 7))
                pss, hpss = next_bank()
                for kk in range(8):
                    p.op("pe", "matmul", pss[P, :], lhsT=winv[:, kk, 672:704], rhs=hbuf[:, kk, :], start=(kk == 0), stop=(kk == 7), reads=[hw_in, hh], writes=[hpss], inc=(kk == 7))
                p.op("dve", "tensor_tensor", out=rtA[P, :], in0=psr[P, :], in1=cosT[P, tbs], op=ALU.mult, reads=[hpsr, htab], writes=[hrt])
                p.op("dve", "tensor_tensor", out=rtB[P, :], in0=pss[P, :], in1=sinT[P, tbs], op=ALU.mult, reads=[hpss, htab], writes=[hrt])
                p.op("dve", "tensor_tensor", out=kpe[P, tbs], in0=rtA[P, :], in1=rtB[P, :], op=ALU.add, reads=[hrt], writes=[hkpe])
                rstd_from_sq(sq[:, 0:3, :], hsq, 3, 384, rstdq, hrq, lnt, hlnt)
                for m in range(3):
                    p.op("dve", "scalar_tensor_tensor", out=cqn[:, m, tbs], in0=raw[:, m, :], scalar=col("mla_q_norm", m), in1=rstdq, op0=ALU.mult, op1=ALU.mult, reads=[hraw, hrq, hcv], writes=[hcqn])
                rstd_from_sq(sq[:, 3:5, :], hsq, 2, 256, rstdk, hrk, lnt, hlnt)
                for m in range(2):
                    p.op("dve", "scalar_tensor_tensor", out=ckvn[:, m, tbs], in0=raw[:, 3 + m, :], scalar=col("mla_kv_norm", m), in1=rstdk, op0=ALU.mult, op1=ALU.mult, reads=[hraw, hrk, hcv], writes=[hckvn])

            p.barrier()
            apos[0] = R0
            attnT = v3(carve_bf(8 * S), 8)
            hattnT = H("attnT")
            apos[0] = R2
            scr = w_in
            Qt = [carve_bf(S), scr[:, 0:S]]
            Kt = [carve_bf(S), scr[:, S:2 * S]]
            Vt = [v3(carve_bf(16 * 66), 16), v3(scr[:, 2 * S:2 * S + 16 * 66], 16)]
            hQ = [[H(f"Q{q}_{t}") for t in range(NTB)] for q in range(2)]
            hK = [H("K0"), H("K1")]
            hV = [H("V0"), H("V1")]
            pairb = [v3(carve_bf(4 * 64), 4) for _ in range(2)]
            hpair = [H("pair0"), H("pair1")]
            Eb = [carve_bf(512) for _ in range(4)]
            hE = [H(f"E{q}") for q in range(4)]
            rtA = [carve_f32(512), scr[:, 5632:6656].bitcast(F32)]
            rtB = [carve_f32(512), scr[:, 6656:7680].bitcast(F32)]
            hrt = [H("ropetmp0"), H("ropetmp1")]
            rc = carve_f32(8)
            hrc = H("rc")
            for q in range(2):
                p.op("dve", "memset", Vt[q][:, :, 64:66], 1.0, writes=[hV[q]])
                p.op("dve", "memset", Qt[q][96:128, :], 0.0, writes=hQ[q])
                p.op("dve", "memset", Kt[q][96:128, :], 0.0, writes=[hK[q]])
            pj = [0]

            def nb67():
                q_ = 6 + pj[0] % 2
                pj[0] += 1
                return banks[q_], hb[q_]

            ei = 0
            pi_ = 0
            ri_ = 0
            for h in range(16):
                hp, hq = h // 2, h % 2
                sb_ = h % 2
                Q_, K_, V_ = Qt[sb_], Kt[sb_], Vt[sb_]
                for tb in range(NTB):
                    tbs = slice(tb * TB, (tb + 1) * TB)
                    hQ_ = hQ[sb_][tb]
                    psx, hpsx = nb67()
                    for kk in range(3):
                        p.op("pe", "matmul", psx[0:96, :], lhsT=wuqv[:, kk, h * 128:h * 128 + 96], rhs=cqn[:, kk, tbs], start=(kk == 0), stop=(kk == 2), reads=[hw_uq, hcqn], writes=[hpsx], inc=(kk == 2))
                    psy, hpsy = nb67()
                    for kk in range(3):
                        p.op("pe", "matmul", psy[P, :], lhsT=wuqv[:, kk, h * 128 + 96:h * 128 + 128], rhs=cqn[:, kk, tbs], start=(kk == 0), stop=(kk == 2), reads=[hw_uq, hcqn], writes=[hpsy], inc=(kk == 2))
                    ra, rb_, hr_ = rtA[ri_ % 2], rtB[ri_ % 2], hrt[ri_ % 2]
                    ri_ += 1
                    p.op("act", "activation", out=Q_[0:64, tbs], in_=psx[0:64, :], func=AF.Copy, reads=[hpsx], writes=[hQ_])
                    p.op("dve", "tensor_tensor", out=ra[P, :], in0=psx[P, :], in1=cosT[P, tbs], op=ALU.mult, reads=[hpsx, htab], writes=[hr_])
                    p.op("dve", "tensor_tensor", out=rb_[P, :], in0=psy[P, :], in1=sinT[P, tbs], op=ALU.mult, reads=[hpsy, htab], writes=[hr_])
                    p.op("dve", "tensor_tensor", out=Q_[P, tbs], in0=ra[P, :], in1=rb_[P, :], op=ALU.add, reads=[hr_], writes=[hQ_])
                    psk, hpsk = nb67()
                    for kk in range(2):
                        p.op("pe", "matmul", psk[0:64, :], lhsT=wkvv[:, kk, h * 128:h * 128 + 64], rhs=ckvn[:, kk, tbs], start=(kk == 0), stop=(kk == 1), reads=[hw_kv, hckvn], writes=[hpsk], inc=(kk == 1))
                    p.op("act", "activation", out=K_[0:64, tbs], in_=psk[0:64, :], func=AF.Copy, reads=[hpsk], writes=[hK[sb_]])
                    p.op("dve", "tensor_copy", out=K_[P, tbs], in_=kpe[P, tbs], reads=[hkpe], writes=[hK[sb_]])
                    psv, hpsv = nb67()
                    for tt in range(4):
                        tok = slice(tb * TB + tt * 128, tb * TB + (tt + 1) * 128)
                        for kk in range(2):
                            p.op("pe", "matmul", psv[:, tt * 64:(tt + 1) * 64], lhsT=ckvn[:, kk, tok], rhs=wkvv[:, kk, h * 128 + 64:h * 128 + 128], start=(kk == 0), stop=(kk == 1), reads=[hw_kv, hckvn], writes=[hpsv], inc=(kk == 1 and tt == 3))
                    p.op("act", "activation", out=V_[:, tb * 4:(tb + 1) * 4, 0:64], in_=psv[:, 0:256].rearrange("p (a b) -> p a b", a=4), func=AF.Copy, reads=[hpsv], writes=[hV[sb_]])
                for qb in range(4):
                    qs = slice(qb * TB, (qb + 1) * TB)
                    nk = 4 * (qb + 1)
                    for kt in range(nk):
                        sbk = kt % 2
                        p.op("pe", "matmul", banks[sbk][:], lhsT=K_[:, kt * 128:(kt + 1) * 128], rhs=Q_[:, qs], start=True, stop=True, reads=[hK[sb_], hQ[sb_][qb]], writes=[hb[sbk]])
                        E_, hE_ = Eb[ei % 4], hE[ei % 4]
                        ei += 1
                        p.op("act", "activation", out=E_, in_=banks[sbk][:], func=AF.Exp, scale=SC, reads=[hb[sbk]], writes=[hE_])
                        kl = kt - 4 * qb
                        if kl >= 0:
                            p.op("dve", "tensor_tensor", out=E_[:, kl * 128:(kl + 1) * 128], in0=E_[:, kl * 128:(kl + 1) * 128], in1=cmask, op=ALU.mult, reads=[hE_, hcm], writes=[hE_])
                        for j in range(4):
                            qt = 4 * qb + j
                            if kt <= qt:
                                p.op("pe", "matmul", banks[2 + j][:, 0:65], lhsT=E_[:, j * 128:(j + 1) * 128], rhs=V_[:, kt, 0:65], start=(kt == 0), stop=(kt == qt), reads=[hE_, hV[sb_]], writes=[hb[2 + j]], inc=(kt == qt))
                    pr, hpr = pairb[pi_ % 2], hpair[pi_ % 2]
                    pi_ += 1
                    for j in range(4):
                        p.op("dve", "reciprocal", out=rc[:, j:j + 1], in_=banks[2 + j][:, 64:65], reads=[hb[2 + j]], writes=[hrc])
                        p.op("act", "activation", out=pr[:, j, :], in_=banks[2 + j][:, 0:64], func=AF.Copy, scale=rc[:, j:j + 1], reads=[hb[2 + j], hrc], writes=[hpr])
                    pst_b, hpst = nb67()
                    pst = pst_b[:].bitcast(BF16)
                    prow = slice(hq * 64, (hq + 1) * 64)
                    for j in range(4):
                        p.op("pe", "transpose", out=pst[prow, j * 128:(j + 1) * 128], in_=pr[:, j, :], identity=ident, reads=[hpr, hcm], writes=[hpst], inc=(j == 3))
                    p.op("dve", "tensor_copy", out=attnT[prow, hp, qs], in_=pst[prow, 0:512], reads=[hpst], writes=[hattnT])

            p.barrier()
            w_o, hw_o = load_w("mla_o")
            wov = v3(w_o[:], 8)
            apos[0] = R1
            ysb = v3(carve_f32(8 * 512), 8)
            hysb = H("ysb")
            sq = v3(carve_bf(8 * 512), 8)
            hsq = H("sq")
            rstd = carve_f32(512)
            hrstd = H("rstd")
            lnt = carve_f32(512)
            hlnt = H("lnt")
            for tb in range(NTB):
                tbs = slice(tb * TB, (tb + 1) * TB)
                for oc in range(8):
                    ps, hps = next_bank()
                    for kk in range(8):
                        p.op("pe", "matmul", ps[:], lhsT=wov[:, kk, oc * 128:(oc + 1) * 128], rhs=attnT[:, kk, tbs], start=(kk == 0), stop=(kk == 7), reads=[hw_o, hattnT], writes=[hps], inc=(kk == 7))
                    evac_y(ps, hps, oc, ysb, hysb, sq, hsq)
                postnorm(tb, f"norm_mix_post{i}", ysb, hysb, sq, hsq, rstd, hrstd, lnt, hlnt)
            p.barrier()


        def ssm_mixer(i):
            NB = 256
            r45 = [0]

            def nb45():
                q_ = 4 + r45[0] % 2
                r45[0] += 1
                return banks[q_], hb[q_]
            NTB2 = S // NB
            HALO = 4

            def v3(ap, a):
                return ap.rearrange("p (a b) -> p a b", a=a)

            apos[0] = 0
            hbuf = v3(carve_bf(8 * (HALO + NB)), 8)
            hh = H("h")
            sq = v3(carve_bf(8 * NB), 8)
            hsq = H("sq")
            rstd = carve_f32(NB)
            hrstd = H("rstd")
            lnt = carve_f32(NB)
            hlnt = H("lnt")
            XC = v3(carve_bf(16 * NB), 16)
            hXC = H("XC")
            xs_tok = v3(carve_bf(2 * 2048), 2)
            hxs = H("xs_tok")
            B_tok = v3(carve_bf(2 * 1024), 2)
            hBt = H("B_tok")
            gT = v3(carve_bf(16 * NB), 16)
            hgT = H("gT")
            Sst = carve_f32(2048)
            hS = H("S")
            Sbf = carve_bf(2048)
            hSbf = H("Sbf")
            abc = carve_f32(32)
            habc = H("a_bc")
            small = [carve_f32(32) for _ in range(8)]
            hsm = [H(f"small{q}") for q in range(8)]
            rawc = [carve_bf(HALO + NB) for _ in range(2)]
            hraw = [H("raw0"), H("raw1")]
            dg = [v3(carve_bf(4 * 128), 4) for _ in range(2)]
            hdg = [H("dg0"), H("dg1")]
            xsTc = [carve_bf(NB) for _ in range(2)]
            hxsT = [H("xsT0"), H("xsT1")]
            rhsb = [carve_f32(128) for _ in range(4)]
            hrhs = [H(f"rhs{q}") for q in range(4)]
            dec = [carve_bf(512) for _ in range(2)]
            hdec = [H("dec0"), H("dec1")]
            Mb = [v3(carve_bf(512), 4) for _ in range(2)]
            hM = [H("M0"), H("M1")]
            CBm = v3(carve_bf(8 * 128), 8)
            hCB = H("CBm")
            yb = carve_f32(1024)
            hyb = H("yb")
            gn = carve_bf(1024)
            hgn = H("gn")
            szb = [carve_bf(512) for _ in range(2)]
            hsz = [H("sz0"), H("sz1")]
            junk = carve_bf(256)
            hjunk = H("junk")
            ss = carve_f32(4)
            hss = H("ss")
            rs4 = carve_f32(4)
            hrs4 = H("rs4")
            alias0 = apos[0]
            xdt = carve_bf(2048)
            hxdt = H("xdt")
            xw = carve_bf(2048)
            hxw = H("xw")
            xsD = carve_bf(2048)
            hxsD = H("xsD")
            alias1 = apos[0]
            apos[0] = alias0
            ysb = v3(carve_f32(8 * NB), 8)
            hysb = H("ysb")
            sq2 = v3(carve_bf(8 * NB), 8)
            hsq2 = H("sq2")
            assert apos[0] <= alias1
            apos[0] = alias1

            cw = col("ssm_conv_w", 0, 128).rearrange("p (c k) -> p c k", c=32)
            dbc = col("ssm_d", 0, 32)
            p.op("act", "activation", out=abc, in_=col("ssm_alog", 0, 32), func=AF.Exp, reads=[hcv], writes=[habc])
            p.op("dve", "tensor_scalar", out=abc, in0=abc, scalar1=-1.0, scalar2=None, op0=ALU.mult, reads=[habc], writes=[habc])
            p.op("dve", "memset", hbuf[:, :, 0:HALO], 0.0, writes=[hh])
            p.op("dve", "memset", Sst, 0.0, writes=[hS])
            p.op("dve", "memset", Sbf, 0.0, writes=[hSbf])
            w_dt, hw_dt = load_w("ssm_dt")
            wdtv = v3(w_dt[:], 8)
            dt_slot = (wi[0] - 1) % 3

            def load_w2(name):
                if wi[0] % 3 == dt_slot:
                    wi[0] += 1
                return load_w(name)

            ri = 0
            for tb in range(NTB2):
                tsl = slice(tb * NB, (tb + 1) * NB)
                hxh = hx[tb // 2]
                xs = x_sb[:, :, tsl]
                hcur = hbuf[:, :, HALO:HALO + NB]
                if tb > 0:
                    p.op("dve", "tensor_copy", out=hbuf[:, :, 0:HALO], in_=hbuf[:, :, NB:NB + HALO], reads=[hh], writes=[hh])
                for c in range(8):
                    p.op("act", "activation", out=sq[:, c, :], in_=xs[:, c, :], func=AF.Square, reads=[hxh], writes=[hsq])
                rstd_from_sq(sq, hsq, 8, D, rstd, hrstd, lnt, hlnt, n=NB)
                for c in range(8):
                    p.op("dve", "scalar_tensor_tensor", out=hcur[:, c, :], in0=xs[:, c, :], scalar=col(f"norm_mix_pre{i}", c), in1=rstd, op0=ALU.mult, op1=ALU.mult, reads=[hxh, hrstd, hcv], writes=[hh])
                ws = None
                for c in range(32):
                    if c % 8 == 0:
                        ws, hws = load_w2(f"ssm_xbc_{c // 8}")
                        wv = v3(ws[:], 8)
                    ps, hps = next_bank()
                    for kk in range(8):
                        p.op("pe", "matmul", ps[:, 0:HALO + NB], lhsT=wv[:, kk, (c % 8) * 128:(c % 8 + 1) * 128], rhs=hbuf[:, kk, :], start=(kk == 0), stop=(kk == 7), reads=[hws, hh], writes=[hps], inc=(kk == 7))
                    r_, hr_ = rawc[ri % 2], hraw[ri % 2]
                    d_, hd_ = dg[ri % 2], hdg[ri % 2]
                    xt_, hxt_ = xsTc[ri % 2], hxsT[ri % 2]
                    ri += 1
                    p.op("act", "activation", out=r_, in_=ps[:, 0:HALO + NB], func=AF.Copy, reads=[hps], writes=[hr_])
                    p.op("dve", "tensor_tensor", out=d_, in0=ident.unsqueeze(1).to_broadcast([128, 4, 128]), in1=cw[:, c, :].unsqueeze(2).to_broadcast([128, 4, 128]), op=ALU.mult, reads=[hcm, hcv], writes=[hd_])
                    ps2, hps2 = next_bank()
                    for kk in range(4):
                        p.op("pe", "matmul", ps2[:, 0:NB], lhsT=d_[:, kk, :], rhs=r_[:, 1 + kk:1 + kk + NB], start=(kk == 0), stop=(kk == 3), reads=[hd_, hr_], writes=[hps2], inc=(kk == 3))
                    cb = col("ssm_conv_b", c)
                    if c < 24:
                        dst, hdst = xt_, hxt_
                    if c >= 16:
                        dst2, hdst2 = XC[:, c - 16, :], hXC
                    if c < 16:
                        p.op("act", "activation", out=xt_, in_=ps2[:, 0:NB], func=AF.Silu, bias=cb, reads=[hps2, hcv], writes=[hxt_])
                        src_t, hsrc_t = xt_, hxt_
                    else:
                        p.op("act", "activation", out=XC[:, c - 16, :], in_=ps2[:, 0:NB], func=AF.Silu, bias=cb, reads=[hps2, hcv], writes=[hXC])
                        src_t, hsrc_t = XC[:, c - 16, :], hXC
                    if c < 24:
                        bk = 6 + (c % 2)
                        pst = banks[bk][:].bitcast(BF16)
                        for tt in range(2):
                            p.op("pe", "transpose", out=pst[:, tt * 128:(tt + 1) * 128], in_=src_t[:, tt * 128:(tt + 1) * 128], identity=ident, reads=[hsrc_t, hcm], writes=[hb[bk]], inc=(tt == 1))
                        if c < 16:
                            p.op("dve", "tensor_copy", out=xs_tok[:, :, c * 128:(c + 1) * 128], in_=pst[:, 0:256].rearrange("p (a b) -> p a b", a=2), reads=[hb[bk]], writes=[hxs])
                        else:
                            p.op("dve", "tensor_copy", out=B_tok[:, :, (c - 16) * 128:(c - 15) * 128], in_=pst[:, 0:256].rearrange("p (a b) -> p a b", a=2), reads=[hb[bk]], writes=[hBt])
                wz = []
                for j in range(2):
                    wzj, hwzj = load_w2(f"ssm_z_{j}")
                    wz.append((v3(wzj[:], 8), hwzj))
                for tt in range(2):
                    tg = tb * 2 + tt
                    tok = slice(HALO + tt * 128, HALO + (tt + 1) * 128)
                    tk = slice(tt * 128, (tt + 1) * 128)
                    xu, dtv, adt, acs, tot_, ev, dte, cdv = small
                    hxu, hdt, hadt, hacs, htot, hev, hdte, hcd = hsm
                    ps, hps = nb45()
                    for kk in range(8):
                        p.op("pe", "matmul", ps[:, 0:32], lhsT=hbuf[:, kk, tok], rhs=wdtv[:, kk, 0:32], start=(kk == 0), stop=(kk == 7), reads=[hw_dt, hh], writes=[hps], inc=(kk == 7))
                    p.op("dve", "tensor_tensor", out=xu, in0=ps[:, 0:32], in1=col("ssm_dtb", 0, 32), op=ALU.add, reads=[hps, hcv], writes=[hxu])
                    p.op("act", "activation", out=adt, in_=xu, func=AF.Abs, reads=[hxu], writes=[hadt])
                    p.op("act", "activation", out=adt, in_=adt, func=AF.Exp, scale=-1.0, reads=[hadt], writes=[hadt])
                    p.op("act", "activation", out=adt, in_=adt, func=AF.Ln, bias=col("one"), reads=[hadt, hcv], writes=[hadt])
                    p.op("dve", "scalar_tensor_tensor", out=dtv, in0=xu, scalar=0.0, in1=adt, op0=ALU.max, op1=ALU.add, reads=[hxu, hadt], writes=[hdt])
                    p.op("dve", "tensor_tensor", out=adt, in0=dtv, in1=abc, op=ALU.mult, reads=[hdt, habc], writes=[hadt])
                    ps, hps = nb45()
                    p.op("pe", "matmul", ps[:, 0:32], lhsT=tri32, rhs=adt, start=True, stop=True, reads=[hadt, hcm], writes=[hps], inc=False)
                    p.op("pe", "matmul", ps[:, 32:64], lhsT=ones32, rhs=adt, start=True, stop=True, reads=[hadt, hcm], writes=[hps])
                    p.op("act", "activation", out=acs, in_=ps[:, 0:32], func=AF.Copy, reads=[hps], writes=[hacs])
                    p.op("act", "activation", out=ev, in_=ps[:, 0:32], func=AF.Exp, reads=[hps], writes=[hev])
                    p.op("act", "activation", out=cdv, in_=ps[:, 32:64], func=AF.Exp, reads=[hps], writes=[hcd])
                    p.op("dve", "tensor_tensor", out=tot_, in0=ps[:, 32:64], in1=acs, op=ALU.subtract, reads=[hps, hacs], writes=[htot])
                    p.op("act", "activation", out=dte, in_=tot_, func=AF.Exp, reads=[htot], writes=[hdte])
                    xs3 = xs_tok[:, tt, :].rearrange("p (h q) -> p h q", h=32)
                    p.op("dve", "tensor_tensor", out=xdt.rearrange("p (h q) -> p h q", h=32), in0=xs3, in1=dtv.unsqueeze(2).to_broadcast([128, 32, 64]), op=ALU.mult, reads=[hxs, hdt], writes=[hxdt])
                    p.op("dve", "tensor_tensor", out=xsD.rearrange("p (h q) -> p h q", h=32), in0=xs3, in1=dbc.unsqueeze(2).to_broadcast([128, 32, 64]), op=ALU.mult, reads=[hxs, hcv], writes=[hxsD])
                    p.op("dve", "tensor_tensor", out=xw.rearrange("p (h q) -> p h q", h=32), in0=xdt.rearrange("p (h q) -> p h q", h=32), in1=dte.unsqueeze(2).to_broadcast([128, 32, 64]), op=ALU.mult, reads=[hxdt, hdte], writes=[hxw])
                    for gh in range(2):
                        ps, hps = nb45()
                        for g4 in range(4):
                            g = gh * 4 + g4
                            p.op("pe", "matmul", ps[:, g4 * 128:(g4 + 1) * 128], lhsT=XC[:, g, tk], rhs=XC[:, 8 + g, tk], start=True, stop=True, reads=[hXC], writes=[hps], inc=(g4 == 3))
                        p.op("dve", "tensor_tensor", out=CBm[:, gh * 4:(gh + 1) * 4, :], in0=ps[:].rearrange("p (a b) -> p a b", a=4), in1=tri_bf.unsqueeze(1).to_broadcast([128, 4, 128]), op=ALU.mult, reads=[hps, hcm], writes=[hCB])
                    for hf in range(2):
                        yd = [(banks[0], hb[0]), (banks[1], hb[1])]
                        for q in range(2):
                            cols = slice(hf * 1024 + q * 512, hf * 1024 + (q + 1) * 512)
                            p.op("pe", "matmul", yd[q][0][:], lhsT=ident, rhs=xsD[:, cols], start=True, stop=False, reads=[hxsD, hcm], writes=[yd[q][1]], inc=False)
                        for g4 in range(4):
                            g = hf * 4 + g4
                            psd, hpsd = nb45()
                            for r in range(4):
                                hd = g * 4 + r
                                rb, hrb = rhsb[r], hrhs[r]
                                p.op("dve", "tensor_scalar", out=rb, in0=tri32, scalar1=adt[:, hd:hd + 1], scalar2=None, op0=ALU.mult, reads=[hcm, hadt], writes=[hrb])
                                p.op("pe", "matmul", psd[:, r * 128:(r + 1) * 128], lhsT=mstrict32, rhs=rb, start=True, stop=True, reads=[hrb, hcm], writes=[hpsd], inc=(r == 3))
                            dc, hdc = dec[g4 % 2], hdec[g4 % 2]
                            M_, hM_ = Mb[g4 % 2], hM[g4 % 2]
                            p.op("act", "activation", out=dc, in_=psd[:], func=AF.Exp, reads=[hpsd], writes=[hdc])
                            p.op("dve", "tensor_tensor", out=M_, in0=dc.rearrange("p (a b) -> p a b", a=4), in1=CBm[:, g, :].unsqueeze(1).to_broadcast([128, 4, 128]), op=ALU.mult, reads=[hdc, hCB], writes=[hM_])
                            for r in range(4):
                                hd = g * 4 + r
                                hl = hd - hf * 16
                                q, cq_ = hl // 8, hl % 8
                                last = (cq_ == 7)
                                p.op("pe", "matmul", yd[q][0][:, cq_ * 64:(cq_ + 1) * 64], lhsT=M_[:, r, :], rhs=xdt[:, hd * 64:(hd + 1) * 64], start=False, stop=last, reads=[hM_, hxdt], writes=[yd[q][1]], inc=last)
                        if tg > 0:
                            yo = [(banks[2], hb[2]), (banks[3], hb[3])]
                            for g4 in range(4):
                                g = hf * 4 + g4
                                q, cg = g4 // 2, g4 % 2
                                p.op("pe", "matmul", yo[q][0][:, cg * 256:(cg + 1) * 256], lhsT=XC[:, 8 + g, tk], rhs=Sbf[:, g * 256:(g + 1) * 256], start=True, stop=True, reads=[hXC, hSbf], writes=[yo[q][1]], inc=(cg == 1))
                            for q in range(2):
                                hsl = slice(hf * 16 + q * 8, hf * 16 + (q + 1) * 8)
                                ysl = yb[:, q * 512:(q + 1) * 512]
                                p.op("dve", "tensor_tensor", out=ysl.rearrange("p (h q) -> p h q", h=8), in0=yo[q][0][:].rearrange("p (h q) -> p h q", h=8), in1=ev[:, hsl].unsqueeze(2).to_broadcast([128, 8, 64]), op=ALU.mult, reads=[yo[q][1], hev], writes=[hyb])
                                p.op("dve", "tensor_tensor", out=ysl, in0=ysl, in1=yd[q][0][:], op=ALU.add, reads=[hyb, yd[q][1]], writes=[hyb])
                        else:
                            for q in range(2):
                                p.op("act", "activation", out=yb[:, q * 512:(q + 1) * 512], in_=yd[q][0][:], func=AF.Copy, reads=[yd[q][1]], writes=[hyb])
                        wzv, hwz = wz[hf]
                        for q in range(2):
                            psz, hpsz = nb45()
                            for kk in range(8):
                                p.op("pe", "matmul", psz[:], lhsT=hbuf[:, kk, tok], rhs=wzv[:, kk, q * 512:(q + 1) * 512], start=(kk == 0), stop=(kk == 7), reads=[hwz, hh], writes=[hpsz], inc=(kk == 7))
                            sz_, hsz_ = szb[q], hsz[q]
                            p.op("act", "activation", out=sz_, in_=psz[:], func=AF.Silu, reads=[hpsz], writes=[hsz_])
                            ysl = yb[:, q * 512:(q + 1) * 512]
                            p.op("dve", "tensor_tensor", out=ysl, in0=ysl, in1=sz_, op=ALU.mult, reads=[hyb, hsz_], writes=[hyb])
                            for g2 in range(2):
                                p.op("act", "activation", out=junk, in_=ysl[:, g2 * 256:(g2 + 1) * 256], func=AF.Square, accum_out=ss[:, q * 2 + g2:q * 2 + g2 + 1], reads=[hyb], writes=[hjunk, hss])
                        p.op("act", "activation", out=rs4, in_=ss, func=AF.Ln, scale=1.0 / 256, bias=col("eps"), reads=[hss, hcv], writes=[hrs4])
                        p.op("act", "activation", out=rs4, in_=rs4, func=AF.Exp, scale=-0.5, reads=[hrs4], writes=[hrs4])
                        p.op("dve", "tensor_tensor", out=gn.rearrange("p (a b) -> p a b", a=4), in0=yb.rearrange("p (a b) -> p a b", a=4), in1=rs4.unsqueeze(2).to_broadcast([128, 4, 256]), op=ALU.mult, reads=[hyb, hrs4], writes=[hgn])
                        for qg in range(2):
                            bk = 6 + (qg % 2)
                            pst = banks[bk][:].bitcast(BF16)
                            for j in range(4):
                                cc = qg * 4 + j
                                p.op("pe", "transpose", out=pst[:, j * 128:(j + 1) * 128], in_=gn[:, cc * 128:(cc + 1) * 128], identity=ident, reads=[hgn, hcm], writes=[hb[bk]], inc=(j == 3))
                            for j in range(4):
                                cc = hf * 8 + qg * 4 + j
                                p.op("act", "activation", out=gT[:, cc, tk], in_=pst[:, j * 128:(j + 1) * 128], func=AF.Copy, scale=col("ssm_norm_w", cc), reads=[hb[bk], hcv], writes=[hgT])
                    for gh in range(4):
                        ps, hps = nb45()
                        for g2 in range(2):
                            g = gh * 2 + g2
                            p.op("pe", "matmul", ps[:, g2 * 256:(g2 + 1) * 256], lhsT=B_tok[:, tt, g * 128:(g + 1) * 128], rhs=xw[:, g * 256:(g + 1) * 256], start=True, stop=True, reads=[hBt, hxw], writes=[hps], inc=(g2 == 1))
                        for r in range(8):
                            hd = gh * 8 + r
                            p.op("dve", "scalar_tensor_tensor", out=Sst[:, hd * 64:(hd + 1) * 64], in0=Sst[:, hd * 64:(hd + 1) * 64], scalar=cdv[:, hd:hd + 1], in1=ps[:, r * 64:(r + 1) * 64], op0=ALU.mult, op1=ALU.add, reads=[hS, hcd, hps], writes=[hS])
                    p.op("act", "activation", out=Sbf, in_=Sst, func=AF.Copy, reads=[hS], writes=[hSbf])
                p.barrier()
                wo = []
                for j in range(2):
                    woj, hwoj = load_w2(f"ssm_out_{j}")
                    wo.append((v3(woj[:], 8), hwoj))
                for oc in range(8):
                    ps, hps = next_bank()
                    for kk in range(16):
                        wv_, hwv_ = wo[kk // 8]
                        p.op("pe", "matmul", ps[:, 0:NB], lhsT=wv_[:, kk % 8, oc * 128:(oc + 1) * 128], rhs=gT[:, kk, :], start=(kk == 0), stop=(kk == 15), reads=[hwv_, hgT], writes=[hps], inc=(kk == 15))
                    p.op("act", "activation", out=ysb[:, oc, :], in_=ps[:, 0:NB], func=AF.Copy, reads=[hps], writes=[hysb])
                    p.op("act", "activation", out=sq2[:, oc, :], in_=ps[:, 0:NB], func=AF.Square, reads=[hps], writes=[hsq2])
                rstd_from_sq(sq2, hsq2, 8, D, rstd, hrstd, lnt, hlnt, n=NB)
                for c in range(8):
                    p.op("dve", "scalar_tensor_tensor", out=ysb[:, c, :], in0=ysb[:, c, :], scalar=col(f"norm_mix_post{i}", c), in1=rstd, op0=ALU.mult, op1=ALU.mult, reads=[hysb, hrstd, hcv], writes=[hysb])
                    p.op("dve", "tensor_tensor", out=xs[:, c, :], in0=xs[:, c, :], in1=ysb[:, c, :], op=ALU.add, reads=[hysb, hxh], writes=[hxh])
                p.barrier()

        for st in stages:
            kind, i = st
            p.cur_reorder = (kind != "conv") or os.environ.get("K_CONV_REORDER", "0") == "1"
            p.barrier()
            if kind == "ffn":
                ffn(i)
            elif kind == "conv":
                conv_mixer(i, i // 3)
            elif kind == "mla":
                mla_mixer(i)
            elif kind == "ssm":
                ssm_mixer(i)
            else:
                raise ValueError(st)

        ov = o_d.rearrange("(c q) t -> q c t", q=128)
        toks = []
        for tb in range(NTB):
            toks.append(p.dma("sp", ov[:, :, tb * TB:(tb + 1) * TB], x_sb[:, :, tb * TB:(tb + 1) * TB], reads=[hx[tb]]))
        for t in toks:
            p.wait_tok("sp", t)
        p.replay(block, sems)
        nc._dbg_prog = p
    return nc


ALL_STAGES = [("conv", 0), ("ffn", 0), ("ssm", 1), ("ffn", 1), ("mla", 2), ("ffn", 2), ("conv", 3), ("ffn", 3)]


def add_const_cols(L):
    L.add_cols("eps", np.full((128, 1), EPS, np.float32))
    L.add_cols("one", np.full((128, 1), 1.0, np.float32))
    jj = np.arange(128) % 32
    inv = (np.float32(10000.0) ** (-(np.arange(16, dtype=np.float32) / np.float32(16.0)))).astype(np.float32)
    L.add_cols("rope_inv", inv[jj % 16].reshape(128, 1))
    L.add_cols("rope_sign", np.where(jj < 16, -1.0, 1.0).astype(np.float32).reshape(128, 1))


def const_mats():
    kk_, qq_ = np.meshgrid(np.arange(128), np.arange(128), indexing="ij")
    cmask = ((kk_ // 64) <= (qq_ // 64)).astype(np.float32)
    tri = (kk_ <= qq_).astype(np.float32)
    mstrict = (kk_ > qq_).astype(np.float32)
    cmat = np.stack([np.eye(128, dtype=np.float32), np.ones((128, 128), np.float32), cmask, tri], 1)
    cmat32 = np.stack([tri, mstrict, np.ones((128, 128), np.float32)], 1)
    return cmat, cmat32


def run(inputs, stages, trace=False):
    inp = {k: np.asarray(v) for k, v in inputs.items()}
    L = make_layout(inp)
    add_const_cols(L)
    wts = np.stack(L.tiles, 0)
    cvec = np.concatenate(L.cols, 1)
    cmat, cmat32 = const_mats()
    nc = build_program(L, stages)
    x = inp["x"]
    in_maps = []
    for b in range(8):
        in_maps.append({"x": np.ascontiguousarray(x[b].T), "wts": wts, "cvec": cvec, "cmat": cmat, "cmat32": cmat32,
                        "pos": np.ascontiguousarray(inp["positions"][b].reshape(1, S).astype(np.int32))})
    res = run_bass_kernel_spmd(nc, in_maps, core_ids=list(range(8)), trace=trace)
    out = np.stack([res.results[b]["out"].T for b in range(8)], 0)
    return np.ascontiguousarray(out.astype(np.float32)), res


def kernel(**inputs):
    out, _ = run(inputs, ALL_STAGES)
    return out
```

```python
import math
from contextlib import ExitStack
import numpy as np
import concourse.bass as bass
import concourse.mybir as mybir
from concourse.bass_utils import run_bass_kernel_spmd
from concourse.alu_op_type import AluOpType as ALU

AF = mybir.ActivationFunctionType
F32 = mybir.dt.float32
BF16 = mybir.dt.bfloat16
I32 = mybir.dt.int32

D = 1024
S = 2048
DEPTH = 4
EPS = 1e-6
TB = 512
NTB = S // TB


class H:
    __slots__ = ("name", "w", "r")

    def __init__(self, name=""):
        self.name = name
        self.w = None
        self.r = []


class Op:
    __slots__ = ("id", "eng", "kind", "payload", "inc", "deps", "dur", "succ", "nleft", "ready", "fin", "seq", "pos")


def _free_elems(ap):
    try:
        n = 1
        for d in ap.shape[1:]:
            n *= int(d)
        return n
    except Exception:
        return 512


class Prog:
    COMPUTE = ("pe", "act", "dve", "pool")
    ALL = ("pe", "act", "dve", "pool", "sp")
    LAT = 250.0

    def __init__(self, nc, ring=8):
        self.nc = nc
        self.ring_n = ring
        self.segs = [[]]
        self.cur_reorder = True
        self.seg_reorder = [True]

    def _new(self, eng, kind, payload, inc, reads, writes, dur):
        seg = self.segs[-1]
        o = Op()
        o.id = len(seg)
        o.eng, o.kind, o.payload, o.inc, o.dur = eng, kind, payload, inc, dur
        deps = {}
        sid = len(self.segs) - 1
        for h in reads:
            if h.w is not None and h.w[0] == sid:
                deps[h.w[1]] = True
        for h in writes:
            if h.w is not None and h.w[0] == sid:
                deps.setdefault(h.w[1], False)
            for (sg, r) in h.r:
                if sg == sid:
                    deps.setdefault(r, False)
        o.deps = deps
        seg.append(o)
        for h in reads:
            h.r.append((sid, o.id))
        for h in writes:
            h.w = (sid, o.id)
            h.r = []
        return o

    def op(self, eng, name, *args, reads=(), writes=(), inc=True, **kwargs):
        if eng == "pe":
            rhs = kwargs.get("rhs", kwargs.get("identity"))
            n = _free_elems(rhs) if rhs is not None else 128
            mult = 4.0 if (rhs is not None and rhs.dtype == F32) else 1.0
            dur = 35.0 + mult * max(64, n) / 2.4
        elif eng == "act":
            o_ = kwargs.get("out")
            dur = 200.0 + _free_elems(o_) / 1.2
        else:
            o_ = kwargs.get("out", args[0] if args else None)
            dur = 120.0 + _free_elems(o_) / 0.96
        return self._new(eng, "op", (name, args, kwargs), inc, reads, writes, dur)

    def dma(self, eng, out_ap, in_ap, reads=(), writes=(), **kw):
        try:
            nbytes = int(out_ap.shape[0]) * _free_elems(out_ap) * 4
        except Exception:
            nbytes = 1 << 20
        dur = 2000.0 + nbytes / 300.0
        return self._new(eng, "dma", (out_ap, in_ap, kw), True, reads, writes, dur)

    def barrier(self):
        if self.segs[-1]:
            self.segs.append([])
            self.seg_reorder.append(self.cur_reorder)
        else:
            self.seg_reorder[-1] = self.cur_reorder

    def wait_tok(self, eng, tok):
        pass

    def _schedule(self, seg, reorder=True):
        import heapq
        unit_of = [0] * len(seg)
        units = []
        open_pe = None
        for o in seg:
            if o.eng == "pe" and o.kind == "op":
                if open_pe is None:
                    open_pe = len(units)
                    units.append({"m": [], "eng": "pe", "dur": 0.0, "kind": "op"})
                u = units[open_pe]
                u["m"].append(o.id)
                u["dur"] += o.dur
                unit_of[o.id] = open_pe
                if o.inc:
                    open_pe = None
            else:
                unit_of[o.id] = len(units)
                units.append({"m": [o.id], "eng": o.eng, "dur": o.dur, "kind": o.kind})
        if open_pe is not None:
            seg[units[open_pe]["m"][-1]].inc = True
        nu = len(units)
        deps = [set() for _ in range(nu)]
        for o in seg:
            u = unit_of[o.id]
            for d in o.deps:
                du = unit_of[d]
                if du != u:
                    deps[u].add(du)
        succ = [[] for _ in range(nu)]
        nleft = [0] * nu
        ready = [0.0] * nu
        fin = [0.0] * nu
        for u in range(nu):
            nleft[u] = len(deps[u])
            for d in deps[u]:
                succ[d].append(u)
        if not (getattr(self, "reorder", True) and reorder):
            order = {e: [] for e in self.ALL}
            for u in range(nu):
                for oid in units[u]["m"]:
                    order[units[u]["eng"]].append(seg[oid])
            return order
        tail = [0.0] * nu
        for u in range(nu - 1, -1, -1):
            m_ = 0.0
            for su in succ[u]:
                v_ = tail[su] + (0.0 if units[su]["eng"] == units[u]["eng"] else self.LAT)
                if v_ > m_:
                    m_ = v_
            tail[u] = units[u]["dur"] + m_
        heaps = {e: [] for e in self.ALL}
        tnow = {e: 0.0 for e in self.ALL}
        for u in range(nu):
            if nleft[u] == 0:
                heapq.heappush(heaps[units[u]["eng"]], (0.0, u))
        order = {e: [] for e in self.ALL}
        done = 0
        while done < nu:
            best = None
            for e in self.ALL:
                hp = heaps[e]
                if not hp:
                    continue
                t = tnow[e]
                if hp[0][0] <= t:
                    tmp = []
                    while hp and hp[0][0] <= t:
                        tmp.append(heapq.heappop(hp))
                    cand = min(tmp, key=lambda it: (-tail[it[1]], it[1]))
                    for it in tmp:
                        heapq.heappush(hp, it)
                    key = (t, cand[1], e, cand)
                else:
                    key = (hp[0][0], hp[0][1], e, hp[0])
                if best is None or key[:2] < best[:2]:
                    best = key
            start, u, e, item = best
            hp = heaps[e]
            hp.remove(item)
            heapq.heapify(hp)
            un = units[u]
            fin[u] = start + un["dur"]
            tnow[e] = start + (60.0 if un["kind"] == "dma" else un["dur"])
            for oid in un["m"]:
                order[e].append(seg[oid])
            done += 1
            for su in succ[u]:
                lat = 0.0 if (units[su]["eng"] == e and un["kind"] != "dma") else self.LAT
                r = fin[u] + lat
                if r > ready[su]:
                    ready[su] = r
                nleft[su] -= 1
                if nleft[su] == 0:
                    heapq.heappush(heaps[units[su]["eng"]], (ready[su], su))
        return order

    def replay(self, block, sems):
        names = {"pe": "tensor", "act": "scalar", "dve": "vector", "pool": "gpsimd", "sp": "sync"}
        streams = {e: [] for e in self.ALL}
        cnt = {e: 0 for e in self.COMPUTE}
        waited = {e: {} for e in self.ALL}
        ring_idx = {e: 0 for e in ("sp", "act", "pool")}
        dma_uses = {}

        def emit_wait(e, k, v):
            if waited[e].get(k, 0) < v:
                streams[e].append(("wait", k, v))
                waited[e][k] = v

        for si_, seg in enumerate(self.segs):
            if not seg:
                continue
            order = self._schedule(seg, self.seg_reorder[si_])
            for e in self.COMPUTE:
                for o in reversed(order[e]):
                    if o.kind == "op":
                        o.inc = True
                        break
            tok = {}
            for e in self.ALL:
                c = cnt.get(e, 0)
                pend = []
                ri = ring_idx.get(e, 0)
                for o in order[e]:
                    if o.kind == "dma":
                        j = ri % self.ring_n
                        ri += 1
                        k = ("dma", e, j)
                        prev = dma_uses.get(k, 0)
                        dma_uses[k] = prev + 1
                        tok[o.id] = (k, 16 * (prev + 1))
                        o.seq = (k, prev)
                    else:
                        if o.inc:
                            c += 1
                            tok[o.id] = (("tl", e), c)
                            for q in pend:
                                tok[q] = (("tl", e), c)
                            pend = []
                        else:
                            pend.append(o.id)
                assert not pend
                if e in cnt:
                    cnt[e] = c
                if e in ring_idx:
                    ring_idx[e] = ri
            for e in self.ALL:
                for o in order[e]:
                    need = {}
                    for d, raw in o.deps.items():
                        po = seg[d]
                        k, v = tok[d]
                        if po.eng == e and po.kind != "dma":
                            if e == "pe":
                                continue
                        if need.get(k, 0) < v:
                            need[k] = v
                    if o.kind == "dma":
                        k, prev = o.seq
                        if prev > 0 and need.get(k, 0) < 16 * prev:
                            need[k] = 16 * prev
                    for k, v in need.items():
                        emit_wait(e, k, v)
                    if o.kind == "dma":
                        streams[e].append(("dma", o.payload, o.seq[0]))
                    else:
                        streams[e].append(("op", o.payload, o.inc))
            toks = [(("tl", e), cnt[e]) for e in self.COMPUTE if cnt[e] > 0]
            toks += [(k, 16 * n_) for k, n_ in dma_uses.items()]
            for e in self.ALL:
                for k, v in toks:
                    emit_wait(e, k, v)

        def make(ename):
            stream = streams[ename]

            def body(eng):
                for it in stream:
                    if it[0] == "wait":
                        eng.wait_ge(sems[it[1]], it[2])
                    elif it[0] == "op":
                        nm, a, kw = it[1]
                        ins = getattr(eng, nm)(*a, **kw)
                        if it[2]:
                            ins.then_inc(sems[("tl", ename)], 1)
                    else:
                        o, i, kw = it[1]
                        eng.dma_start(out=o, in_=i, **kw).then_inc(sems[it[2]], 16)
            return body

        self.n_instr = {e: len(streams[e]) for e in self.ALL}
        self.dbg_streams = streams
        semv = {}
        pc = {e: 0 for e in self.ALL}
        progress = True
        while progress:
            progress = False
            for e in self.ALL:
                st = streams[e]
                while pc[e] < len(st):
                    it = st[pc[e]]
                    if it[0] == "wait":
                        if semv.get(it[1], 0) < it[2]:
                            break
                    elif it[0] == "op":
                        if it[2]:
                            semv[("tl", e)] = semv.get(("tl", e), 0) + 1
                    else:
                        semv[it[2]] = semv.get(it[2], 0) + 16
                    pc[e] += 1
                    progress = True
        stuck = {e: (pc[e], len(streams[e]), streams[e][pc[e]][:3] if pc[e] < len(streams[e]) else None) for e in self.ALL}
        if any(pc[e] < len(streams[e]) for e in self.ALL):
            raise RuntimeError(f"semaphore program deadlocks: {stuck} sems={ {k: v for k, v in semv.items() if k[0] == 'tl'} }")
        for ename in self.ALL:
            if streams[ename]:
                getattr(block, names[ename])(make(ename))

    def sem_keys(self):
        keys = [("tl", e) for e in self.COMPUTE]
        for e in ("sp", "act", "pool"):
            for j in range(self.ring_n):
                keys.append(("dma", e, j))
        return keys


def tileA(W, k0, n0):
    blk = np.zeros((1024, 1024), np.float32)
    sub = W[k0:k0 + 1024, n0:n0 + 1024]
    blk[:sub.shape[0], :sub.shape[1]] = sub
    return blk.reshape(8, 128, 1024).transpose(1, 0, 2).reshape(128, 8192)


def tileB(W, n0, ncol, nk):
    sub = W[:nk * 128, n0:n0 + ncol]
    return np.ascontiguousarray(sub.reshape(nk, 128, ncol).transpose(1, 0, 2)).reshape(128, nk * ncol)


def colvec(v):
    n = v.shape[0] // 128
    return np.ascontiguousarray(v.reshape(n, 128).T)


class Layout:
    def __init__(self):
        self.tiles = []
        self.tile_ids = {}
        self.cols = []
        self.col_off = {}
        self.ncol = 0

    def add_tile(self, name, arr):
        assert arr.shape == (128, 8192), (name, arr.shape)
        self.tile_ids[name] = len(self.tiles)
        self.tiles.append(arr)

    def add_cols(self, name, arr):
        arr = np.asarray(arr, np.float32)
        assert arr.shape[0] == 128
        self.col_off[name] = (self.ncol, arr.shape[1])
        self.cols.append(arr)
        self.ncol += arr.shape[1]


def make_layout(inp):
    L = Layout()
    for i in range(DEPTH):
        for nm in ("norm_mix_pre", "norm_mix_post", "norm_ffn_pre", "norm_ffn_post"):
            L.add_cols(f"{nm}{i}", colvec(inp[nm][i]))
        w1 = inp["ffn_w_in"][i]
        w2 = inp["ffn_w_out"][i]
        for fg in range(4):
            L.add_tile(f"ffn{i}_w1_{fg}", tileA(w1, 0, fg * 1024))
        for og in range(4):
            L.add_tile(f"ffn{i}_w2_{og}", tileB(w2, og * 256, 256, 32))
    for j in range(2):
        w1 = inp["conv_w_pw1"][j]
        L.add_tile(f"conv{j}_pw1_a", tileA(w1, 0, 0))
        L.add_tile(f"conv{j}_pw1_b", tileA(w1, 0, 1024))
        L.add_tile(f"conv{j}_pw2", tileA(inp["conv_w_pw2"][j], 0, 0))
        L.add_cols(f"conv{j}_b_pw1", colvec(inp["conv_b_pw1"][j]))
        wdw = inp["conv_w_dw"][j]
        L.add_cols(f"conv{j}_w_dw", np.ascontiguousarray(wdw.reshape(31, 8, 128).transpose(2, 1, 0)).reshape(128, 8 * 31))
        L.add_cols(f"conv{j}_b_dw", colvec(inp["conv_b_dw"][j]))
        L.add_cols(f"conv{j}_ln_g", colvec(inp["conv_ln_g"][j]))
        L.add_cols(f"conv{j}_ln_b", colvec(inp["conv_ln_b"][j]))
        L.add_cols(f"conv{j}_b_pw2", colvec(inp["conv_b_pw2"][j]))
    wi_ = inp["ssm_w_in"][0]
    for j in range(2):
        L.add_tile(f"ssm_z_{j}", tileA(wi_, 0, j * 1024))
    for j in range(4):
        L.add_tile(f"ssm_xbc_{j}", tileA(wi_, 0, 2048 + j * 1024))
    L.add_tile("ssm_dt", tileA(wi_, 0, 6144))
    wo_ = inp["ssm_w_out"][0]
    for j in range(2):
        L.add_tile(f"ssm_out_{j}", tileA(wo_, j * 1024, 0))
    cw = inp["ssm_conv_w"][0]
    L.add_cols("ssm_conv_w", np.ascontiguousarray(cw.reshape(4, 32, 128).transpose(2, 1, 0)).reshape(128, 128))
    L.add_cols("ssm_conv_b", colvec(inp["ssm_conv_b"][0]))
    L.add_cols("ssm_dtb", np.broadcast_to(inp["ssm_dt_bias"][0][None, :], (128, 32)))
    L.add_cols("ssm_alog", np.broadcast_to(inp["ssm_a_log"][0][None, :], (128, 32)))
    L.add_cols("ssm_d", np.broadcast_to(inp["ssm_d"][0][None, :], (128, 32)))
    L.add_cols("ssm_norm_w", colvec(inp["ssm_norm_w"][0]))
    w_in = inp["mla_w_in"][0]
    w_in_ext = np.concatenate([w_in, w_in[:, 656:672], w_in[:, 640:656]], axis=1)
    L.add_tile("mla_in", tileA(w_in_ext, 0, 0))
    wuq = inp["mla_w_uq"][0]
    blocks = []
    for h in range(16):
        b = wuq[:, h * 96:(h + 1) * 96]
        blocks.append(np.concatenate([b, b[:, 80:96], b[:, 64:80]], axis=1))
    wuq_ext = np.concatenate(blocks, axis=1)
    t = np.zeros((128, 8192), np.float32)
    t[:, :3 * 2048] = tileB(wuq_ext, 0, 2048, 3)
    L.add_tile("mla_uq", t)
    t = np.zeros((128, 8192), np.float32)
    t[:, :2 * 2048] = tileB(inp["mla_w_ukv"][0], 0, 2048, 2)
    L.add_tile("mla_ukv", t)
    L.add_tile("mla_o", tileA(inp["mla_w_o"][0], 0, 0))
    L.add_cols("mla_q_norm", colvec(inp["mla_q_norm"][0]))
    L.add_cols("mla_kv_norm", colvec(inp["mla_kv_norm"][0]))
    return L


class K:
    pass


def build_program(L, stages):
    nc = bass.Bass("TRN2", target_bir_lowering=False)
    NT = len(L.tiles)
    NCV = L.ncol
    x_d = nc.dram_tensor("x", [D, S], F32, kind="ExternalInput").ap()
    w_d = nc.dram_tensor("wts", [NT, 128, 8192], F32, kind="ExternalInput").ap()
    cv_d = nc.dram_tensor("cvec", [128, NCV], F32, kind="ExternalInput").ap()
    cm_d = nc.dram_tensor("cmat", [128, 4, 128], F32, kind="ExternalInput").ap()
    cm32_d = nc.dram_tensor("cmat32", [128, 3, 128], F32, kind="ExternalInput").ap()
    pos_d = nc.dram_tensor("pos", [1, S], I32, kind="ExternalInput").ap()
    o_d = nc.dram_tensor("out", [D, S], F32, kind="ExternalOutput").ap()

    p = Prog(nc)
    import os
    p.reorder = os.environ.get("K_REORDER", "1") == "1"
    es = ExitStack()
    with es:
        sems = {k: es.enter_context(nc.semaphore("s_" + "_".join(map(str, k)))) for k in p.sem_keys()}
        x_sb = es.enter_context(nc.sbuf_tensor("x_sb", [128, 8, S], F32))
        wslot = [es.enter_context(nc.sbuf_tensor(f"wslot{i}", [128, 8192], BF16)) for i in range(3)]
        cvec = es.enter_context(nc.sbuf_tensor("cvec_sb", [128, NCV], F32))
        cmat = es.enter_context(nc.sbuf_tensor("cmat_sb", [128, 4, 128], BF16))
        cmat32 = es.enter_context(nc.sbuf_tensor("cmat32_sb", [128, 3, 128], F32))
        AW = 22016
        arena = es.enter_context(nc.sbuf_tensor("arena", [128, AW], F32))
        banks = [es.enter_context(nc.psum_tensor(f"bank{i}", [128, 512], F32)) for i in range(8)]
        block = es.enter_context(nc.Block())

        k = K()
        k.nc, k.p = nc, p
        hb = [H(f"bank{i}") for i in range(8)]
        hx = [H(f"x{tb}") for tb in range(NTB)]
        hslot = [H(f"slot{i}") for i in range(3)]
        hcv, hcm = H("cvec"), H("cmat")
        ident = cmat[:, 0, :]
        ones = cmat[:, 1, :]
        cmask = cmat[:, 2, :]
        tri_bf = cmat[:, 3, :]
        tri32 = cmat32[:, 0, :]
        mstrict32 = cmat32[:, 1, :]
        ones32 = cmat32[:, 2, :]

        apos = [0]

        def carve_f32(n):
            o = apos[0]
            apos[0] += n
            assert apos[0] <= AW, apos[0]
            return arena[:, o:o + n]

        def carve_bf(n):
            assert n % 2 == 0
            return carve_f32(n // 2).bitcast(BF16)

        def col(name, c=0, n=1):
            o, w = L.col_off[name]
            return cvec[:, o + c:o + c + n]

        bi = [0]

        def next_bank():
            i = bi[0] % 6
            bi[0] += 1
            return banks[i], hb[i]

        si = [0]

        def next_sbank():
            i = 6 + si[0] % 2
            si[0] += 1
            return banks[i], hb[i]

        wi = [0]

        def load_w(name):
            i = wi[0] % 3
            wi[0] += 1
            tid = L.tile_ids[name]
            dst = wslot[i][:].rearrange("p (a b) -> p a b", a=8)
            src = w_d[tid].rearrange("p (a b) -> p a b", a=8)
            p.dma("pool", dst, src, writes=[hslot[i]])
            return wslot[i], hslot[i]

        p.dma("sp", cvec[:], cv_d, writes=[hcv])
        p.dma("pool", cmat[:], cm_d, writes=[hcm])
        p.dma("sp", cmat32[:], cm32_d, writes=[hcm])
        xv = x_d.rearrange("(c q) t -> q c t", q=128)
        for tb in range(NTB):
            p.dma("sp" if tb % 2 == 0 else "act", x_sb[:, :, tb * TB:(tb + 1) * TB], xv[:, :, tb * TB:(tb + 1) * TB], writes=[hx[tb]])

        def rstd_from_sq(sq, hsq, nchunk, dim, rstd, hrstd, lnt, hlnt, n=512):
            sb, hsb = next_sbank()
            for c in range(nchunk):
                p.op("pe", "matmul", sb[:, 0:n], lhsT=ones, rhs=sq[:, c, :], start=(c == 0), stop=(c == nchunk - 1), reads=[hsq, hcm], writes=[hsb], inc=(c == nchunk - 1))
            p.op("act", "activation", out=lnt, in_=sb[:, 0:n], func=AF.Ln, scale=1.0 / dim, bias=col("eps"), reads=[hsb, hcv], writes=[hlnt])
            p.op("act", "activation", out=rstd, in_=lnt, func=AF.Exp, scale=-0.5, reads=[hlnt], writes=[hrstd])

        def prenorm(tb, wname, hbuf, hh, sq, hsq, rstd, hrstd, lnt, hlnt):
            xs = x_sb[:, :, tb * TB:(tb + 1) * TB]
            for c in range(8):
                p.op("act", "activation", out=sq[:, c, :], in_=xs[:, c, :], func=AF.Square, reads=[hx[tb]], writes=[hsq])
            rstd_from_sq(sq, hsq, 8, D, rstd, hrstd, lnt, hlnt)
            for c in range(8):
                p.op("dve", "scalar_tensor_tensor", out=hbuf[:, c, :], in0=xs[:, c, :], scalar=col(wname, c), in1=rstd, op0=ALU.mult, op1=ALU.mult, reads=[hx[tb], hrstd, hcv], writes=[hh])

        def postnorm(tb, wname, ysb, hysb, sq, hsq, rstd, hrstd, lnt, hlnt):
            xs = x_sb[:, :, tb * TB:(tb + 1) * TB]
            rstd_from_sq(sq, hsq, 8, D, rstd, hrstd, lnt, hlnt)
            for c in range(8):
                p.op("dve", "scalar_tensor_tensor", out=ysb[:, c, :], in0=ysb[:, c, :], scalar=col(wname, c), in1=rstd, op0=ALU.mult, op1=ALU.mult, reads=[hysb, hrstd, hcv], writes=[hysb])
                p.op("dve", "tensor_tensor", out=xs[:, c, :], in0=xs[:, c, :], in1=ysb[:, c, :], op=ALU.add, reads=[hysb, hx[tb]], writes=[hx[tb]])

        def evac_y(ps, hps, c, ysb, hysb, sq, hsq, bias=None):
            if bias is None:
                p.op("act", "activation", out=ysb[:, c, :], in_=ps[:], func=AF.Copy, reads=[hps], writes=[hysb])
                p.op("act", "activation", out=sq[:, c, :], in_=ps[:], func=AF.Square, reads=[hps], writes=[hsq])
            else:
                p.op("act", "activation", out=ysb[:, c, :], in_=ps[:], func=AF.Identity, bias=bias, reads=[hps, hcv], writes=[hysb])
                p.op("act", "activation", out=sq[:, c, :], in_=ps[:], func=AF.Square, bias=bias, reads=[hps, hcv], writes=[hsq])

        def ffn(i):
            apos[0] = 0
            hbuf = [carve_bf(8 * 512).rearrange("p (a b) -> p a b", a=8) for _ in range(2)]
            hh = [H("h0"), H("h1")]
            abuf = carve_bf(32 * 512).rearrange("p (a b) -> p a b", a=32)
            ha = H("a")
            ysb = carve_f32(8 * 512).rearrange("p (a b) -> p a b", a=8)
            hysb = H("ysb")
            sq = carve_bf(8 * 512).rearrange("p (a b) -> p a b", a=8)
            hsq = H("sq")
            rstd = [carve_f32(512) for _ in range(2)]
            hrstd = [H("rstd0"), H("rstd1")]
            lnt = carve_f32(512)
            hlnt = H("lnt")
            rt = [carve_f32(512) for _ in range(2)]
            hrt = [H("rt0"), H("rt1")]
            rti = 0
            for tb in range(NTB):
                hb_, hh_ = hbuf[tb % 2], hh[tb % 2]
                prenorm(tb, f"norm_ffn_pre{i}", hb_, hh_, sq, hsq, rstd[0], hrstd[0], lnt, hlnt)
                for fg in range(4):
                    ws, hws = load_w(f"ffn{i}_w1_{fg}")
                    wv = ws[:].rearrange("p (a b) -> p a b", a=8)
                    for fc in range(8):
                        ps, hps = next_bank()
                        for kk in range(8):
                            p.op("pe", "matmul", ps[:], lhsT=wv[:, kk, fc * 128:(fc + 1) * 128], rhs=hb_[:, kk, :], start=(kk == 0), stop=(kk == 7), reads=[hws, hh_], writes=[hps], inc=(kk == 7))
                        r_, hr_ = rt[rti % 2], hrt[rti % 2]
                        rti += 1
                        p.op("act", "activation", out=r_, in_=ps[:], func=AF.Relu, reads=[hps], writes=[hr_])
                        ac = fg * 8 + fc
                        p.op("dve", "tensor_tensor", out=abuf[:, ac, :], in0=r_, in1=ps[:], op=ALU.mult, reads=[hps, hr_], writes=[ha])
                for og in range(4):
                    ws, hws = load_w(f"ffn{i}_w2_{og}")
                    wv = ws[:].rearrange("p (a b) -> p a b", a=32)
                    for oc in range(2):
                        ps, hps = next_bank()
                        for kk in range(32):
                            p.op("pe", "matmul", ps[:], lhsT=wv[:, kk, oc * 128:(oc + 1) * 128], rhs=abuf[:, kk, :], start=(kk == 0), stop=(kk == 31), reads=[hws, ha], writes=[hps], inc=(kk == 31))
                        evac_y(ps, hps, og * 2 + oc, ysb, hysb, sq, hsq)
                postnorm(tb, f"norm_ffn_post{i}", ysb, hysb, sq, hsq, rstd[1], hrstd[1], lnt, hlnt)

        def conv_mixer(i, j):
            apos[0] = 0
            PAD = 32
            G = carve_bf(8 * (PAD + TB)).rearrange("p (a b) -> p a b", a=8)
            hG = H("G")
            hbuf = carve_bf(8 * 512).rearrange("p (a b) -> p a b", a=8)
            hh = H("h")
            cvb, hcvb = hbuf, hh
            sq = carve_bf(8 * 512).rearrange("p (a b) -> p a b", a=8)
            hsq = H("sq")
            cvf = carve_f32(8 * 512).rearrange("p (a b) -> p a b", a=8)
            hcvf = H("cvf")
            ysb, hysb = cvf, hcvf
            vbuf = carve_bf(8 * 512).rearrange("p (a b) -> p a b", a=8)
            hv = H("v")
            diag = [carve_bf(31 * 128).rearrange("p (a b) -> p a b", a=31) for _ in range(2)]
            hdiag = [H("diag0"), H("diag1")]
            rstd = carve_f32(512)
            hrstd = H("rstd")
            lnt = carve_f32(512)
            hlnt = H("lnt")
            sg = [carve_f32(512) for _ in range(2)]
            hsg = [H("sg0"), H("sg1")]
            msq = carve_f32(512)
            hmsq = H("msq")
            var = carve_f32(512)
            hvar = H("var")
            tmpf = [carve_f32(512) for _ in range(2)]
            htmp = [H("tmp0"), H("tmp1")]
            wdw = col(f"conv{j}_w_dw", 0, 8 * 31).rearrange("p (c k) -> p c k", c=8)

            p.op("dve", "memset", G[:, :, 0:PAD], 0.0, writes=[hG])
            wa, hwa = load_w(f"conv{j}_pw1_a")
            wb, hwb = load_w(f"conv{j}_pw1_b")
            w2, hw2 = load_w(f"conv{j}_pw2")
            wav = wa[:].rearrange("p (a b) -> p a b", a=8)
            wbv = wb[:].rearrange("p (a b) -> p a b", a=8)
            w2v = w2[:].rearrange("p (a b) -> p a b", a=8)
            di = 0
            for tb in range(NTB):
                if tb > 0:
                    p.op("dve", "tensor_copy", out=G[:, :, 0:PAD], in_=G[:, :, TB:TB + PAD], reads=[hG], writes=[hG])
                prenorm(tb, f"norm_mix_pre{i}", hbuf, hh, sq, hsq, rstd, hrstd, lnt, hlnt)
                for c in range(8):
                    psa, hpsa = next_bank()
                    for kk in range(8):
                        p.op("pe", "matmul", psa[:], lhsT=wav[:, kk, c * 128:(c + 1) * 128], rhs=hbuf[:, kk, :], start=(kk == 0), stop=(kk == 7), reads=[hwa, hh], writes=[hpsa], inc=(kk == 7))
                    psb, hpsb = next_bank()
                    for kk in range(8):
                        p.op("pe", "matmul", psb[:], lhsT=wbv[:, kk, c * 128:(c + 1) * 128], rhs=hbuf[:, kk, :], start=(kk == 0), stop=(kk == 7), reads=[hwb, hh], writes=[hpsb], inc=(kk == 7))
                    s_, hs_ = sg[c % 2], hsg[c % 2]
                    p.op("act", "activation", out=s_, in_=psb[:], func=AF.Sigmoid, bias=col(f"conv{j}_b_pw1", 8 + c), reads=[hpsb, hcv], writes=[hs_])
                    p.op("dve", "scalar_tensor_tensor", out=G[:, c, PAD:PAD + TB], in0=psa[:], scalar=col(f"conv{j}_b_pw1", c), in1=s_, op0=ALU.add, op1=ALU.mult, reads=[hpsa, hs_, hcv], writes=[hG])
                for c in range(8):
                    dg, hdg = diag[di % 2], hdiag[di % 2]
                    di += 1
                    p.op("dve", "tensor_tensor", out=dg, in0=ident.unsqueeze(1).to_broadcast([128, 31, 128]), in1=wdw[:, c, :].unsqueeze(2).to_broadcast([128, 31, 128]), op=ALU.mult, reads=[hcm, hcv], writes=[hdg])
                    ps, hps = next_bank()
                    for kk in range(31):
                        o = PAD - 30 + kk
                        p.op("pe", "matmul", ps[:], lhsT=dg[:, kk, :], rhs=G[:, c, o:o + TB], start=(kk == 0), stop=(kk == 30), reads=[hdg, hG], writes=[hps], inc=(kk == 30))
                    bdw = col(f"conv{j}_b_dw", c)
                    p.op("act", "activation", out=cvf[:, c, :], in_=ps[:], func=AF.Identity, bias=bdw, reads=[hps, hcv], writes=[hcvf])
                    p.op("act", "activation", out=cvb[:, c, :], in_=ps[:], func=AF.Identity, bias=bdw, reads=[hps, hcv], writes=[hcvb])
                    p.op("act", "activation", out=sq[:, c, :], in_=ps[:], func=AF.Square, bias=bdw, reads=[hps, hcv], writes=[hsq])
                sm, hsm = next_sbank()
                for c in range(8):
                    p.op("pe", "matmul", sm[:], lhsT=ones, rhs=cvb[:, c, :], start=(c == 0), stop=(c == 7), reads=[hcvb, hcm], writes=[hsm], inc=(c == 7))
                s2, hs2 = next_sbank()
                for c in range(8):
                    p.op("pe", "matmul", s2[:], lhsT=ones, rhs=sq[:, c, :], start=(c == 0), stop=(c == 7), reads=[hsq, hcm], writes=[hs2], inc=(c == 7))
                p.op("act", "activation", out=msq, in_=sm[:], func=AF.Square, scale=1.0 / D, reads=[hsm], writes=[hmsq])
                p.op("dve", "scalar_tensor_tensor", out=var, in0=s2[:], scalar=1.0 / D, in1=msq, op0=ALU.mult, op1=ALU.subtract, reads=[hs2, hmsq], writes=[hvar])
                p.op("act", "activation", out=lnt, in_=var, func=AF.Ln, bias=col("eps"), reads=[hvar, hcv], writes=[hlnt])
                p.op("act", "activation", out=rstd, in_=lnt, func=AF.Exp, scale=-0.5, reads=[hlnt], writes=[hrstd])
                for c in range(8):
                    t_, ht_ = tmpf[c % 2], htmp[c % 2]
                    p.op("dve", "scalar_tensor_tensor", out=t_, in0=sm[:], scalar=-1.0 / D, in1=cvf[:, c, :], op0=ALU.mult, op1=ALU.add, reads=[hsm, hcvf], writes=[ht_])
                    p.op("dve", "tensor_tensor", out=t_, in0=t_, in1=rstd, op=ALU.mult, reads=[ht_, hrstd], writes=[ht_])
                    p.op("act", "activation", out=vbuf[:, c, :], in_=t_, func=AF.Silu, scale=col(f"conv{j}_ln_g", c), bias=col(f"conv{j}_ln_b", c), reads=[ht_, hcv], writes=[hv])
                for oc in range(8):
                    ps, hps = next_bank()
                    for kk in range(8):
                        p.op("pe", "matmul", ps[:], lhsT=w2v[:, kk, oc * 128:(oc + 1) * 128], rhs=vbuf[:, kk, :], start=(kk == 0), stop=(kk == 7), reads=[hw2, hv], writes=[hps], inc=(kk == 7))
                    evac_y(ps, hps, oc, ysb, hysb, sq, hsq, bias=col(f"conv{j}_b_pw2", oc))
                postnorm(tb, f"norm_mix_post{i}", ysb, hysb, sq, hsq, rstd, hrstd, lnt, hlnt)


        def mla_mixer(i):
            SC = 96 ** -0.5
            R0, R1, R2 = 0, 8192, 16384
            P = slice(64, 96)
            PI = 3.1415925
            TWO_PI = 2.0 * math.pi
            C1 = 6.28125
            C2 = TWO_PI - C1

            def v3(ap, a):
                return ap.rearrange("p (a b) -> p a b", a=a)

            apos[0] = R1
            cqn = v3(carve_bf(3 * S), 3)
            hcqn = H("cqn")
            ckvn = v3(carve_bf(2 * S), 2)
            hckvn = H("ckvn")
            kpe = carve_bf(S)
            hkpe = H("kpe")
            cosT = carve_bf(S)
            sinT = carve_bf(S)
            htab = H("tab")
            assert apos[0] == R2
            w_in, hw_in = load_w("mla_in")
            w_uq, hw_uq = load_w("mla_uq")
            w_kv, hw_kv = load_w("mla_ukv")
            winv = v3(w_in[:], 8)
            wuqv = w_uq[:, 0:3 * 2048].rearrange("p (a b) -> p a b", a=3)
            wkvv = w_kv[:, 0:2 * 2048].rearrange("p (a b) -> p a b", a=2)

            p.barrier()
            apos[0] = R0
            posi = carve_f32(S).bitcast(I32)
            ang = carve_f32(S)
            t1 = carve_f32(S)
            nf = carve_f32(S)
            apos[0] = R2
            ni = carve_f32(S).bitcast(I32)
            hA = H("ropeA")
            p.dma("sp", posi[P, :], pos_d.partition_broadcast(32), writes=[hA])
            p.op("dve", "tensor_copy", out=t1[P, :], in_=posi[P, :], reads=[hA], writes=[hA])
            p.op("dve", "tensor_scalar", out=ang[P, :], in0=t1[P, :], scalar1=col("rope_inv")[P, :], scalar2=None, op0=ALU.mult, reads=[hA, hcv], writes=[hA])
            for which in ("sin", "cos"):
                if which == "cos":
                    p.op("dve", "tensor_scalar", out=ang[P, :], in0=ang[P, :], scalar1=0.5 * math.pi, scalar2=None, op0=ALU.add, reads=[hA], writes=[hA])
                p.op("dve", "tensor_scalar", out=t1[P, :], in0=ang[P, :], scalar1=1.0 / TWO_PI, scalar2=None, op0=ALU.mult, reads=[hA], writes=[hA])
                p.op("dve", "tensor_copy", out=ni[P, :], in_=t1[P, :], reads=[hA], writes=[hA])
                p.op("dve", "tensor_copy", out=nf[P, :], in_=ni[P, :], reads=[hA], writes=[hA])
                p.op("dve", "scalar_tensor_tensor", out=t1[P, :], in0=nf[P, :], scalar=-C1, in1=ang[P, :], op0=ALU.mult, op1=ALU.add, reads=[hA], writes=[hA])
                p.op("dve", "scalar_tensor_tensor", out=t1[P, :], in0=nf[P, :], scalar=-C2, in1=t1[P, :], op0=ALU.mult, op1=ALU.add, reads=[hA], writes=[hA])
                p.op("dve", "tensor_scalar", out=nf[P, :], in0=t1[P, :], scalar1=PI, scalar2=-TWO_PI, op0=ALU.is_gt, op1=ALU.mult, reads=[hA], writes=[hA])
                p.op("dve", "tensor_tensor", out=t1[P, :], in0=t1[P, :], in1=nf[P, :], op=ALU.add, reads=[hA], writes=[hA])
                p.op("dve", "tensor_scalar", out=nf[P, :], in0=t1[P, :], scalar1=-PI, scalar2=TWO_PI, op0=ALU.is_lt, op1=ALU.mult, reads=[hA], writes=[hA])
                p.op("dve", "tensor_tensor", out=t1[P, :], in0=t1[P, :], in1=nf[P, :], op=ALU.add, reads=[hA], writes=[hA])
                p.op("dve", "tensor_scalar", out=t1[P, :], in0=t1[P, :], scalar1=PI, scalar2=-PI, op0=ALU.min, op1=ALU.max, reads=[hA], writes=[hA])
                if which == "sin":
                    p.op("act", "activation", out=nf[P, :], in_=t1[P, :], func=AF.Sin, reads=[hA], writes=[hA])
                    p.op("dve", "tensor_scalar", out=sinT[P, :], in0=nf[P, :], scalar1=col("rope_sign")[P, :], scalar2=None, op0=ALU.mult, reads=[hA, hcv], writes=[htab])
                else:
                    p.op("act", "activation", out=cosT[P, :], in_=t1[P, :], func=AF.Sin, reads=[hA], writes=[htab])

            p.barrier()
            apos[0] = R0
            hbuf = v3(carve_bf(8 * 512), 8)
            hh = H("h")
            sq = v3(carve_bf(8 * 512), 8)
            hsq = H("sq")
            raw = v3(carve_f32(5 * 512), 5)
            hraw = H("raw")
            rstdq = carve_f32(512)
            hrq = H("rstdq")
            rstdk = carve_f32(512)
            hrk = H("rstdk")
            lnt = carve_f32(512)
            hlnt = H("lnt")
            apos[0] = R2
            rtA = carve_f32(512)
            rtB = carve_f32(512)
            hrt = H("ropetmp")
            for tb in range(NTB):
                tbs = slice(tb * TB, (tb + 1) * TB)
                prenorm(tb, f"norm_mix_pre{i}", hbuf, hh, sq, hsq, rstdq, hrq, lnt, hlnt)
                for m in range(5):
                    ps, hps = next_bank()
                    for kk in range(8):
                        p.op("pe", "matmul", ps[:], lhsT=winv[:, kk, m * 128:(m + 1) * 128], rhs=hbuf[:, kk, :], start=(kk == 0), stop=(kk == 7), reads=[hw_in, hh], writes=[hps], inc=(kk == 7))
                    p.op("act", "activation", out=raw[:, m, :], in_=ps[:], func=AF.Copy, reads=[hps], writes=[hraw])
                    p.op("act", "activation", out=sq[:, m, :], in_=ps[:], func=AF.Square, reads=[hps], writes=[hsq])
                psr, hpsr = next_bank()
                for kk in range(8):
                    p.op("pe", "matmul", psr[P, :], lhsT=winv[:, kk, 640:672], rhs=hbuf[:, kk, :], start=(kk == 0), stop=(kk == 7), reads=[hw_in, hh], writes=[hpsr], inc=(kk == 7))
                pss, hpss = next_bank()
                for kk in range(8):
                    p.op("pe", "matmul", pss[P, :], lhsT=winv[:, kk, 672:704], rhs=hbuf[:, kk, :], start=(kk == 0), stop=(kk == 7), reads=[hw_in, hh], writes=[hpss], inc=(kk == 7))
                p.op("dve", "tensor_tensor", out=rtA[P, :], in0=psr[P, :], in1=cosT[P, tbs], op=ALU.mult, reads=[hpsr, htab], writes=[hrt])
                p.op("dve", "tensor_tensor", out=rtB[P, :], in0=pss[P, :], in1=sinT[P, tbs], op=ALU.mult, reads=[hpss, htab], writes=[hrt])
                p.op("dve", "tensor_tensor", out=kpe[P, tbs], in0=rtA[P, :], in1=rtB[P, :], op=ALU.add, reads=[hrt], writes=[hkpe])
                rstd_from_sq(sq[:, 0:3, :], hsq, 3, 384, rstdq, hrq, lnt, hlnt)
                for m in range(3):
                    p.op("dve", "scalar_tensor_tensor", out=cqn[:, m, tbs], in0=raw[:, m, :], scalar=col("mla_q_norm", m), in1=rstdq, op0=ALU.mult, op1=ALU.mult, reads=[hraw, hrq, hcv], writes=[hcqn])
                rstd_from_sq(sq[:, 3:5, :], hsq, 2, 256, rstdk, hrk, lnt, hlnt)
                for m in range(2):
                    p.op("dve", "scalar_tensor_tensor", out=ckvn[:, m, tbs], in0=raw[:, 3 + m, :], scalar=col("mla_kv_norm", m), in1=rstdk, op0=ALU.mult, op1=ALU.mult, reads=[hraw, hrk, hcv], writes=[hckvn])

            p.barrier()
            apos[0] = R0
            attnT = v3(carve_bf(8 * S), 8)
            hattnT = H("attnT")
            apos[0] = R2
            scr = w_in
            Qt = [carve_bf(S), scr[:, 0:S]]
            Kt = [carve_bf(S), scr[:, S:2 * S]]
            Vt = [v3(carve_bf(16 * 66), 16), v3(scr[:, 2 * S:2 * S + 16 * 66], 16)]
            hQ = [[H(f"Q{q}_{t}") for t in range(NTB)] for q in range(2)]
            hK = [H("K0"), H("K1")]
            hV = [H("V0"), H("V1")]
            pairb = [v3(carve_bf(4 * 64), 4) for _ in range(2)]
            hpair = [H("pair0"), H("pair1")]
            Eb = [carve_bf(512) for _ in range(4)]
            hE = [H(f"E{q}") for q in range(4)]
            rtA = [carve_f32(512), scr[:, 5632:6656].bitcast(F32)]
            rtB = [carve_f32(512), scr[:, 6656:7680].bitcast(F32)]
            hrt = [H("ropetmp0"), H("ropetmp1")]
            rc = carve_f32(8)
            hrc = H("rc")
            for q in range(2):
                p.op("dve", "memset", Vt[q][:, :, 64:66], 1.0, writes=[hV[q]])
                p.op("dve", "memset", Qt[q][96:128, :], 0.0, writes=hQ[q])
                p.op("dve", "memset", Kt[q][96:128, :], 0.0, writes=[hK[q]])
            pj = [0]

            def nb67():
                q_ = 6 + pj[0] % 2
                pj[0] += 1
                return banks[q_], hb[q_]

            ei = 0
            pi_ = 0
            ri_ = 0
            stb = [0, 1, 4, 5]
            sti = [0]
            qbc = [0]
            hacc = [[H(f"acc{q}_{j}") for j in range(4)] for q in range(2)]
            for h in range(16):
                hp, hq = h // 2, h % 2
                sb_ = h % 2
                Q_, K_, V_ = Qt[sb_], Kt[sb_], Vt[sb_]
                for tb in range(NTB):
                    tbs = slice(tb * TB, (tb + 1) * TB)
                    hQ_ = hQ[sb_][tb]
                    psx, hpsx = nb67()
                    for kk in range(3):
                        p.op("pe", "matmul", psx[0:96, :], lhsT=wuqv[:, kk, h * 128:h * 128 + 96], rhs=cqn[:, kk, tbs], start=(kk == 0), stop=(kk == 2), reads=[hw_uq, hcqn], writes=[hpsx], inc=(kk == 2))
                    psy, hpsy = nb67()
                    for kk in range(3):
                        p.op("pe", "matmul", psy[P, :], lhsT=wuqv[:, kk, h * 128 + 96:h * 128 + 128], rhs=cqn[:, kk, tbs], start=(kk == 0), stop=(kk == 2), reads=[hw_uq, hcqn], writes=[hpsy], inc=(kk == 2))
                    ra, rb_, hr_ = rtA[ri_ % 2], rtB[ri_ % 2], hrt[ri_ % 2]
                    ri_ += 1
                    p.op("act", "activation", out=Q_[0:64, tbs], in_=psx[0:64, :], func=AF.Copy, reads=[hpsx], writes=[hQ_])
                    p.op("dve", "tensor_tensor", out=ra[P, :], in0=psx[P, :], in1=cosT[P, tbs], op=ALU.mult, reads=[hpsx, htab], writes=[hr_])
                    p.op("dve", "tensor_tensor", out=rb_[P, :], in0=psy[P, :], in1=sinT[P, tbs], op=ALU.mult, reads=[hpsy, htab], writes=[hr_])
                    p.op("dve", "tensor_tensor", out=Q_[P, tbs], in0=ra[P, :], in1=rb_[P, :], op=ALU.add, reads=[hr_], writes=[hQ_])
                    psk, hpsk = nb67()
                    for kk in range(2):
                        p.op("pe", "matmul", psk[0:64, :], lhsT=wkvv[:, kk, h * 128:h * 128 + 64], rhs=ckvn[:, kk, tbs], start=(kk == 0), stop=(kk == 1), reads=[hw_kv, hckvn], writes=[hpsk], inc=(kk == 1))
                    p.op("act", "activation", out=K_[0:64, tbs], in_=psk[0:64, :], func=AF.Copy, reads=[hpsk], writes=[hK[sb_]])
                    p.op("dve", "tensor_copy", out=K_[P, tbs], in_=kpe[P, tbs], reads=[hkpe], writes=[hK[sb_]])
                    psv, hpsv = nb67()
                    for tt in range(4):
                        tok = slice(tb * TB + tt * 128, tb * TB + (tt + 1) * 128)
                        for kk in range(2):
                            p.op("pe", "matmul", psv[:, tt * 64:(tt + 1) * 64], lhsT=ckvn[:, kk, tok], rhs=wkvv[:, kk, h * 128 + 64:h * 128 + 128], start=(kk == 0), stop=(kk == 1), reads=[hw_kv, hckvn], writes=[hpsv], inc=(kk == 1 and tt == 3))
                    p.op("act", "activation", out=V_[:, tb * 4:(tb + 1) * 4, 0:64], in_=psv[:, 0:256].rearrange("p (a b) -> p a b", a=4), func=AF.Copy, reads=[hpsv], writes=[hV[sb_]])
                for qb in range(4):
                    qs = slice(qb * TB, (qb + 1) * TB)
                    nk = 4 * (qb + 1)
                    abk = 2 + (qbc[0] % 2)
                    hacc_ = hacc[qbc[0] % 2]
                    qbc[0] += 1
                    accb = banks[abk]
                    for kt in range(nk):
                        sbk = stb[sti[0] % 4]
                        sti[0] += 1
                        p.op("pe", "matmul", banks[sbk][:], lhsT=K_[:, kt * 128:(kt + 1) * 128], rhs=Q_[:, qs], start=True, stop=True, reads=[hK[sb_], hQ[sb_][qb]], writes=[hb[sbk]])
                        E_, hE_ = Eb[ei % 4], hE[ei % 4]
                        ei += 1
                        p.op("act", "activation", out=E_, in_=banks[sbk][:], func=AF.Exp, scale=SC, reads=[hb[sbk]], writes=[hE_])
                        kl = kt - 4 * qb
                        if kl >= 0:
                            p.op("dve", "tensor_tensor", out=E_[:, kl * 128:(kl + 1) * 128], in0=E_[:, kl * 128:(kl + 1) * 128], in1=cmask, op=ALU.mult, reads=[hE_, hcm], writes=[hE_])
                        for j in range(4):
                            qt = 4 * qb + j
                            if kt <= qt:
                                first = (kt == 0 and j == 0)
                                last = (kt == nk - 1 and j == 3)
                                p.op("pe", "matmul", accb[:, j * 66:j * 66 + 65], lhsT=E_[:, j * 128:(j + 1) * 128], rhs=V_[:, kt, 0:65], start=first, stop=last, skip_group_check=True,
                                     reads=[hE_, hV[sb_]], writes=[hb[abk]], inc=(kt == qt))
                    pr, hpr = pairb[pi_ % 2], hpair[pi_ % 2]
                    pi_ += 1
                    for j in range(4):
                        p.op("dve", "reciprocal", out=rc[:, j:j + 1], in_=accb[:, j * 66 + 64:j * 66 + 65], reads=[hb[abk]], writes=[hrc])
                        p.op("act", "activation", out=pr[:, j, :], in_=accb[:, j * 66:j * 66 + 64], func=AF.Copy, scale=rc[:, j:j + 1], reads=[hb[abk], hrc], writes=[hpr])
                    pst_b, hpst = nb67()
                    pst = pst_b[:].bitcast(BF16)
                    prow = slice(hq * 64, (hq + 1) * 64)
                    for j in range(4):
                        p.op("pe", "transpose", out=pst[prow, j * 128:(j + 1) * 128], in_=pr[:, j, :], identity=ident, reads=[hpr, hcm], writes=[hpst], inc=(j == 3))
                    p.op("dve", "tensor_copy", out=attnT[prow, hp, qs], in_=pst[prow, 0:512], reads=[hpst], writes=[hattnT])

            p.barrier()
            w_o, hw_o = load_w("mla_o")
            wov = v3(w_o[:], 8)
            apos[0] = R1
            ysb = v3(carve_f32(8 * 512), 8)
            hysb = H("ysb")
            sq = v3(carve_bf(8 * 512), 8)
            hsq = H("sq")
            rstd = carve_f32(512)
            hrstd = H("rstd")
            lnt = carve_f32(512)
            hlnt = H("lnt")
            for tb in range(NTB):
                tbs = slice(tb * TB, (tb + 1) * TB)
                for oc in range(8):
                    ps, hps = next_bank()
                    for kk in range(8):
                        p.op("pe", "matmul", ps[:], lhsT=wov[:, kk, oc * 128:(oc + 1) * 128], rhs=attnT[:, kk, tbs], start=(kk == 0), stop=(kk == 7), reads=[hw_o, hattnT], writes=[hps], inc=(kk == 7))
                    evac_y(ps, hps, oc, ysb, hysb, sq, hsq)
                postnorm(tb, f"norm_mix_post{i}", ysb, hysb, sq, hsq, rstd, hrstd, lnt, hlnt)
            p.barrier()


        def ssm_mixer(i):
            NB = 256
            r45 = [0]

            def nb45():
                q_ = 4 + r45[0] % 2
                r45[0] += 1
                return banks[q_], hb[q_]
            NTB2 = S // NB
            HALO = 4

            def v3(ap, a):
                return ap.rearrange("p (a b) -> p a b", a=a)

            apos[0] = 0
            hbuf = v3(carve_bf(8 * (HALO + NB)), 8)
            hh = H("h")
            sq = v3(carve_bf(8 * NB), 8)
            hsq = H("sq")
            rstd = carve_f32(NB)
            hrstd = H("rstd")
            lnt = carve_f32(NB)
            hlnt = H("lnt")
            XC = v3(carve_bf(16 * NB), 16)
            hXC = H("XC")
            xs_tok = v3(carve_bf(2 * 2048), 2)
            hxs = H("xs_tok")
            B_tok = v3(carve_bf(2 * 1024), 2)
            hBt = H("B_tok")
            gT = v3(carve_bf(16 * NB), 16)
            hgT = H("gT")
            Sst = carve_f32(2048)
            hS = H("S")
            Sbf = carve_bf(2048)
            hSbf = H("Sbf")
            abc = carve_f32(32)
            habc = H("a_bc")
            small = [carve_f32(32) for _ in range(8)]
            hsm = [H(f"small{q}") for q in range(8)]
            rawc = [carve_bf(HALO + NB) for _ in range(2)]
            hraw = [H("raw0"), H("raw1")]
            dg = [v3(carve_bf(4 * 128), 4) for _ in range(2)]
            hdg = [H("dg0"), H("dg1")]
            xsTc = [carve_bf(NB) for _ in range(2)]
            hxsT = [H("xsT0"), H("xsT1")]
            rhsb = [carve_f32(128) for _ in range(4)]
            hrhs = [H(f"rhs{q}") for q in range(4)]
            dec = [carve_bf(512) for _ in range(2)]
            hdec = [H("dec0"), H("dec1")]
            Mb = [v3(carve_bf(512), 4) for _ in range(2)]
            hM = [H("M0"), H("M1")]
            CBm = v3(carve_bf(8 * 128), 8)
            hCB = H("CBm")
            yb = carve_f32(1024)
            hyb = H("yb")
            gn = carve_bf(1024)
            hgn = H("gn")
            szb = [carve_bf(512) for _ in range(2)]
            hsz = [H("sz0"), H("sz1")]
            junk = carve_bf(256)
            hjunk = H("junk")
            ss = carve_f32(4)
            hss = H("ss")
            rs4 = carve_f32(4)
            hrs4 = H("rs4")
            alias0 = apos[0]
            hAl = H("aliasA")
            hBl = H("aliasB")
            xdt = carve_bf(2048)
            hxdt = hAl
            xw = carve_bf(2048)
            hxw = hAl
            xsD = carve_bf(2048)
            hxsD = hBl
            alias1 = apos[0]
            apos[0] = alias0
            ysb = v3(carve_f32(8 * NB), 8)
            hysb = hAl
            sq2 = v3(carve_bf(8 * NB), 8)
            hsq2 = hBl
            assert apos[0] <= alias1
            apos[0] = alias1

            cw = col("ssm_conv_w", 0, 128).rearrange("p (c k) -> p c k", c=32)
            dbc = col("ssm_d", 0, 32)
            p.op("act", "activation", out=abc, in_=col("ssm_alog", 0, 32), func=AF.Exp, reads=[hcv], writes=[habc])
            p.op("dve", "tensor_scalar", out=abc, in0=abc, scalar1=-1.0, scalar2=None, op0=ALU.mult, reads=[habc], writes=[habc])
            p.op("dve", "memset", hbuf[:, :, 0:HALO], 0.0, writes=[hh])
            p.op("dve", "memset", Sst, 0.0, writes=[hS])
            p.op("dve", "memset", Sbf, 0.0, writes=[hSbf])
            wdtv = v3(carve_bf(8 * 32), 8)
            hw_dt = H("w_dt")
            p.dma("pool", wdtv, w_d[L.tile_ids["ssm_dt"]].rearrange("p (a b) -> p a b", a=8)[:, :, 0:32], writes=[hw_dt])
            load_w2 = load_w

            ri = 0
            for tb in range(NTB2):
                tsl = slice(tb * NB, (tb + 1) * NB)
                hxh = hx[tb // 2]
                xs = x_sb[:, :, tsl]
                hcur = hbuf[:, :, HALO:HALO + NB]
                if tb > 0:
                    p.op("dve", "tensor_copy", out=hbuf[:, :, 0:HALO], in_=hbuf[:, :, NB:NB + HALO], reads=[hh], writes=[hh])
                for c in range(8):
                    p.op("act", "activation", out=sq[:, c, :], in_=xs[:, c, :], func=AF.Square, reads=[hxh], writes=[hsq])
                rstd_from_sq(sq, hsq, 8, D, rstd, hrstd, lnt, hlnt, n=NB)
                for c in range(8):
                    p.op("dve", "scalar_tensor_tensor", out=hcur[:, c, :], in0=xs[:, c, :], scalar=col(f"norm_mix_pre{i}", c), in1=rstd, op0=ALU.mult, op1=ALU.mult, reads=[hxh, hrstd, hcv], writes=[hh])
                ws = None
                for c in range(32):
                    if c % 8 == 0:
                        ws, hws = load_w2(f"ssm_xbc_{c // 8}")
                        wv = v3(ws[:], 8)
                    ps, hps = next_bank()
                    for kk in range(8):
                        p.op("pe", "matmul", ps[:, 0:HALO + NB], lhsT=wv[:, kk, (c % 8) * 128:(c % 8 + 1) * 128], rhs=hbuf[:, kk, :], start=(kk == 0), stop=(kk == 7), reads=[hws, hh], writes=[hps], inc=(kk == 7))
                    r_, hr_ = rawc[ri % 2], hraw[ri % 2]
                    d_, hd_ = dg[ri % 2], hdg[ri % 2]
                    xt_, hxt_ = xsTc[ri % 2], hxsT[ri % 2]
                    ri += 1
                    p.op("act", "activation", out=r_, in_=ps[:, 0:HALO + NB], func=AF.Copy, reads=[hps], writes=[hr_])
                    p.op("dve", "tensor_tensor", out=d_, in0=ident.unsqueeze(1).to_broadcast([128, 4, 128]), in1=cw[:, c, :].unsqueeze(2).to_broadcast([128, 4, 128]), op=ALU.mult, reads=[hcm, hcv], writes=[hd_])
                    ps2, hps2 = next_bank()
                    for kk in range(4):
                        p.op("pe", "matmul", ps2[:, 0:NB], lhsT=d_[:, kk, :], rhs=r_[:, 1 + kk:1 + kk + NB], start=(kk == 0), stop=(kk == 3), reads=[hd_, hr_], writes=[hps2], inc=(kk == 3))
                    cb = col("ssm_conv_b", c)
                    if c < 24:
                        dst, hdst = xt_, hxt_
                    if c >= 16:
                        dst2, hdst2 = XC[:, c - 16, :], hXC
                    if c < 16:
                        p.op("act", "activation", out=xt_, in_=ps2[:, 0:NB], func=AF.Silu, bias=cb, reads=[hps2, hcv], writes=[hxt_])
                        src_t, hsrc_t = xt_, hxt_
                    else:
                        p.op("act", "activation", out=XC[:, c - 16, :], in_=ps2[:, 0:NB], func=AF.Silu, bias=cb, reads=[hps2, hcv], writes=[hXC])
                        src_t, hsrc_t = XC[:, c - 16, :], hXC
                    if c < 24:
                        bk = 6 + (c % 2)
                        pst = banks[bk][:].bitcast(BF16)
                        for tt in range(2):
                            p.op("pe", "transpose", out=pst[:, tt * 128:(tt + 1) * 128], in_=src_t[:, tt * 128:(tt + 1) * 128], identity=ident, reads=[hsrc_t, hcm], writes=[hb[bk]], inc=(tt == 1))
                        if c < 16:
                            p.op("dve", "tensor_copy", out=xs_tok[:, :, c * 128:(c + 1) * 128], in_=pst[:, 0:256].rearrange("p (a b) -> p a b", a=2), reads=[hb[bk]], writes=[hxs])
                        else:
                            p.op("dve", "tensor_copy", out=B_tok[:, :, (c - 16) * 128:(c - 15) * 128], in_=pst[:, 0:256].rearrange("p (a b) -> p a b", a=2), reads=[hb[bk]], writes=[hBt])
                wz = []
                for j in range(2):
                    wzj, hwzj = load_w2(f"ssm_z_{j}")
                    wz.append((v3(wzj[:], 8), hwzj))
                for tt in range(2):
                    tg = tb * 2 + tt
                    tok = slice(HALO + tt * 128, HALO + (tt + 1) * 128)
                    tk = slice(tt * 128, (tt + 1) * 128)
                    xu, dtv, adt, acs, tot_, ev, dte, cdv = small
                    hxu, hdt, hadt, hacs, htot, hev, hdte, hcd = hsm
                    ps, hps = nb45()
                    for kk in range(8):
                        p.op("pe", "matmul", ps[:, 0:32], lhsT=hbuf[:, kk, tok], rhs=wdtv[:, kk, 0:32], start=(kk == 0), stop=(kk == 7), reads=[hw_dt, hh], writes=[hps], inc=(kk == 7))
                    p.op("dve", "tensor_tensor", out=xu, in0=ps[:, 0:32], in1=col("ssm_dtb", 0, 32), op=ALU.add, reads=[hps, hcv], writes=[hxu])
                    p.op("act", "activation", out=adt, in_=xu, func=AF.Abs, reads=[hxu], writes=[hadt])
                    p.op("act", "activation", out=adt, in_=adt, func=AF.Exp, scale=-1.0, reads=[hadt], writes=[hadt])
                    p.op("act", "activation", out=adt, in_=adt, func=AF.Ln, bias=col("one"), reads=[hadt, hcv], writes=[hadt])
                    p.op("dve", "scalar_tensor_tensor", out=dtv, in0=xu, scalar=0.0, in1=adt, op0=ALU.max, op1=ALU.add, reads=[hxu, hadt], writes=[hdt])
                    p.op("dve", "tensor_tensor", out=adt, in0=dtv, in1=abc, op=ALU.mult, reads=[hdt, habc], writes=[hadt])
                    ps, hps = nb45()
                    p.op("pe", "matmul", ps[:, 0:32], lhsT=tri32, rhs=adt, start=True, stop=True, reads=[hadt, hcm], writes=[hps], inc=False)
                    p.op("pe", "matmul", ps[:, 32:64], lhsT=ones32, rhs=adt, start=True, stop=True, reads=[hadt, hcm], writes=[hps])
                    p.op("act", "activation", out=acs, in_=ps[:, 0:32], func=AF.Copy, reads=[hps], writes=[hacs])
                    p.op("act", "activation", out=ev, in_=ps[:, 0:32], func=AF.Exp, reads=[hps], writes=[hev])
                    p.op("act", "activation", out=cdv, in_=ps[:, 32:64], func=AF.Exp, reads=[hps], writes=[hcd])
                    p.op("dve", "tensor_tensor", out=tot_, in0=ps[:, 32:64], in1=acs, op=ALU.subtract, reads=[hps, hacs], writes=[htot])
                    p.op("act", "activation", out=dte, in_=tot_, func=AF.Exp, reads=[htot], writes=[hdte])
                    xs3 = xs_tok[:, tt, :].rearrange("p (h q) -> p h q", h=32)
                    p.op("dve", "tensor_tensor", out=xdt.rearrange("p (h q) -> p h q", h=32), in0=xs3, in1=dtv.unsqueeze(2).to_broadcast([128, 32, 64]), op=ALU.mult, reads=[hxs, hdt], writes=[hxdt])
                    p.op("dve", "tensor_tensor", out=xsD.rearrange("p (h q) -> p h q", h=32), in0=xs3, in1=dbc.unsqueeze(2).to_broadcast([128, 32, 64]), op=ALU.mult, reads=[hxs, hcv], writes=[hxsD])
                    p.op("dve", "tensor_tensor", out=xw.rearrange("p (h q) -> p h q", h=32), in0=xdt.rearrange("p (h q) -> p h q", h=32), in1=dte.unsqueeze(2).to_broadcast([128, 32, 64]), op=ALU.mult, reads=[hxdt, hdte], writes=[hxw])
                    for gh in range(2):
                        ps, hps = nb45()
                        for g4 in range(4):
                            g = gh * 4 + g4
                            p.op("pe", "matmul", ps[:, g4 * 128:(g4 + 1) * 128], lhsT=XC[:, g, tk], rhs=XC[:, 8 + g, tk], start=True, stop=True, reads=[hXC], writes=[hps], inc=(g4 == 3))
                        p.op("dve", "tensor_tensor", out=CBm[:, gh * 4:(gh + 1) * 4, :], in0=ps[:].rearrange("p (a b) -> p a b", a=4), in1=tri_bf.unsqueeze(1).to_broadcast([128, 4, 128]), op=ALU.mult, reads=[hps, hcm], writes=[hCB])
                    for hf in range(2):
                        yd = [(banks[0], hb[0]), (banks[1], hb[1])]
                        for q in range(2):
                            cols = slice(hf * 1024 + q * 512, hf * 1024 + (q + 1) * 512)
                            p.op("pe", "matmul", yd[q][0][:], lhsT=ident, rhs=xsD[:, cols], start=True, stop=False, reads=[hxsD, hcm], writes=[yd[q][1]], inc=False)
                        for g4 in range(4):
                            g = hf * 4 + g4
                            psd, hpsd = nb45()
                            for r in range(4):
                                hd = g * 4 + r
                                rb, hrb = rhsb[r], hrhs[r]
                                p.op("dve", "tensor_scalar", out=rb, in0=tri32, scalar1=adt[:, hd:hd + 1], scalar2=None, op0=ALU.mult, reads=[hcm, hadt], writes=[hrb])
                                p.op("pe", "matmul", psd[:, r * 128:(r + 1) * 128], lhsT=mstrict32, rhs=rb, start=True, stop=True, reads=[hrb, hcm], writes=[hpsd], inc=(r == 3))
                            dc, hdc = dec[g4 % 2], hdec[g4 % 2]
                            M_, hM_ = Mb[g4 % 2], hM[g4 % 2]
                            p.op("act", "activation", out=dc, in_=psd[:], func=AF.Exp, reads=[hpsd], writes=[hdc])
                            p.op("dve", "tensor_tensor", out=M_, in0=dc.rearrange("p (a b) -> p a b", a=4), in1=CBm[:, g, :].unsqueeze(1).to_broadcast([128, 4, 128]), op=ALU.mult, reads=[hdc, hCB], writes=[hM_])
                            for r in range(4):
                                hd = g * 4 + r
                                hl = hd - hf * 16
                                q, cq_ = hl // 8, hl % 8
                                last = (cq_ == 7)
                                p.op("pe", "matmul", yd[q][0][:, cq_ * 64:(cq_ + 1) * 64], lhsT=M_[:, r, :], rhs=xdt[:, hd * 64:(hd + 1) * 64], start=False, stop=last, reads=[hM_, hxdt], writes=[yd[q][1]], inc=last)
                        if tg > 0:
                            yo = [(banks[2], hb[2]), (banks[3], hb[3])]
                            for g4 in range(4):
                                g = hf * 4 + g4
                                q, cg = g4 // 2, g4 % 2
                                p.op("pe", "matmul", yo[q][0][:, cg * 256:(cg + 1) * 256], lhsT=XC[:, 8 + g, tk], rhs=Sbf[:, g * 256:(g + 1) * 256], start=True, stop=True, reads=[hXC, hSbf], writes=[yo[q][1]], inc=(cg == 1))
                            for q in range(2):
                                hsl = slice(hf * 16 + q * 8, hf * 16 + (q + 1) * 8)
                                ysl = yb[:, q * 512:(q + 1) * 512]
                                p.op("dve", "tensor_tensor", out=ysl.rearrange("p (h q) -> p h q", h=8), in0=yo[q][0][:].rearrange("p (h q) -> p h q", h=8), in1=ev[:, hsl].unsqueeze(2).to_broadcast([128, 8, 64]), op=ALU.mult, reads=[yo[q][1], hev], writes=[hyb])
                                p.op("dve", "tensor_tensor", out=ysl, in0=ysl, in1=yd[q][0][:], op=ALU.add, reads=[hyb, yd[q][1]], writes=[hyb])
                        else:
                            for q in range(2):
                                p.op("act", "activation", out=yb[:, q * 512:(q + 1) * 512], in_=yd[q][0][:], func=AF.Copy, reads=[yd[q][1]], writes=[hyb])
                        wzv, hwz = wz[hf]
                        for q in range(2):
                            psz, hpsz = nb45()
                            for kk in range(8):
                                p.op("pe", "matmul", psz[:], lhsT=hbuf[:, kk, tok], rhs=wzv[:, kk, q * 512:(q + 1) * 512], start=(kk == 0), stop=(kk == 7), reads=[hwz, hh], writes=[hpsz], inc=(kk == 7))
                            sz_, hsz_ = szb[q], hsz[q]
                            p.op("act", "activation", out=sz_, in_=psz[:], func=AF.Silu, reads=[hpsz], writes=[hsz_])
                            ysl = yb[:, q * 512:(q + 1) * 512]
                            p.op("dve", "tensor_tensor", out=ysl, in0=ysl, in1=sz_, op=ALU.mult, reads=[hyb, hsz_], writes=[hyb])
                            for g2 in range(2):
                                p.op("act", "activation", out=junk, in_=ysl[:, g2 * 256:(g2 + 1) * 256], func=AF.Square, accum_out=ss[:, q * 2 + g2:q * 2 + g2 + 1], reads=[hyb], writes=[hjunk, hss])
                        p.op("act", "activation", out=rs4, in_=ss, func=AF.Ln, scale=1.0 / 256, bias=col("eps"), reads=[hss, hcv], writes=[hrs4])
                        p.op("act", "activation", out=rs4, in_=rs4, func=AF.Exp, scale=-0.5, reads=[hrs4], writes=[hrs4])
                        p.op("dve", "tensor_tensor", out=gn.rearrange("p (a b) -> p a b", a=4), in0=yb.rearrange("p (a b) -> p a b", a=4), in1=rs4.unsqueeze(2).to_broadcast([128, 4, 256]), op=ALU.mult, reads=[hyb, hrs4], writes=[hgn])
                        for qg in range(2):
                            bk = 6 + (qg % 2)
                            pst = banks[bk][:].bitcast(BF16)
                            for j in range(4):
                                cc = qg * 4 + j
                                p.op("pe", "transpose", out=pst[:, j * 128:(j + 1) * 128], in_=gn[:, cc * 128:(cc + 1) * 128], identity=ident, reads=[hgn, hcm], writes=[hb[bk]], inc=(j == 3))
                            for j in range(4):
                                cc = hf * 8 + qg * 4 + j
                                p.op("act", "activation", out=gT[:, cc, tk], in_=pst[:, j * 128:(j + 1) * 128], func=AF.Copy, scale=col("ssm_norm_w", cc), reads=[hb[bk], hcv], writes=[hgT])
                    for gh in range(4):
                        ps, hps = nb45()
                        for g2 in range(2):
                            g = gh * 2 + g2
                            p.op("pe", "matmul", ps[:, g2 * 256:(g2 + 1) * 256], lhsT=B_tok[:, tt, g * 128:(g + 1) * 128], rhs=xw[:, g * 256:(g + 1) * 256], start=True, stop=True, reads=[hBt, hxw], writes=[hps], inc=(g2 == 1))
                        for r in range(8):
                            hd = gh * 8 + r
                            p.op("dve", "scalar_tensor_tensor", out=Sst[:, hd * 64:(hd + 1) * 64], in0=Sst[:, hd * 64:(hd + 1) * 64], scalar=cdv[:, hd:hd + 1], in1=ps[:, r * 64:(r + 1) * 64], op0=ALU.mult, op1=ALU.add, reads=[hS, hcd, hps], writes=[hS])
                    p.op("act", "activation", out=Sbf, in_=Sst, func=AF.Copy, reads=[hS], writes=[hSbf])
                wo = []
                for j in range(2):
                    woj, hwoj = load_w2(f"ssm_out_{j}")
                    wo.append((v3(woj[:], 8), hwoj))
                for oc in range(8):
                    ps, hps = next_bank()
                    for kk in range(16):
                        wv_, hwv_ = wo[kk // 8]
                        p.op("pe", "matmul", ps[:, 0:NB], lhsT=wv_[:, kk % 8, oc * 128:(oc + 1) * 128], rhs=gT[:, kk, :], start=(kk == 0), stop=(kk == 15), reads=[hwv_, hgT], writes=[hps], inc=(kk == 15))
                    p.op("act", "activation", out=ysb[:, oc, :], in_=ps[:, 0:NB], func=AF.Copy, reads=[hps], writes=[hysb])
                    p.op("act", "activation", out=sq2[:, oc, :], in_=ps[:, 0:NB], func=AF.Square, reads=[hps], writes=[hsq2])
                rstd_from_sq(sq2, hsq2, 8, D, rstd, hrstd, lnt, hlnt, n=NB)
                for c in range(8):
                    p.op("dve", "scalar_tensor_tensor", out=ysb[:, c, :], in0=ysb[:, c, :], scalar=col(f"norm_mix_post{i}", c), in1=rstd, op0=ALU.mult, op1=ALU.mult, reads=[hysb, hrstd, hcv], writes=[hysb])
                    p.op("dve", "tensor_tensor", out=xs[:, c, :], in0=xs[:, c, :], in1=ysb[:, c, :], op=ALU.add, reads=[hysb, hxh], writes=[hxh])

        for st in stages:
            kind, i = st
            p.cur_reorder = (kind != "conv") or os.environ.get("K_CONV_REORDER", "0") == "1"
            p.barrier()
            if kind == "ffn":
                ffn(i)
            elif kind == "conv":
                conv_mixer(i, i // 3)
            elif kind == "mla":
                mla_mixer(i)
            elif kind == "ssm":
                ssm_mixer(i)
            else:
                raise ValueError(st)

        ov = o_d.rearrange("(c q) t -> q c t", q=128)
        toks = []
        for tb in range(NTB):
            toks.append(p.dma("sp", ov[:, :, tb * TB:(tb + 1) * TB], x_sb[:, :, tb * TB:(tb + 1) * TB], reads=[hx[tb]]))
        for t in toks:
            p.wait_tok("sp", t)
        p.replay(block, sems)
        nc._dbg_prog = p
    return nc


ALL_STAGES = [("conv", 0), ("ffn", 0), ("ssm", 1), ("ffn", 1), ("mla", 2), ("ffn", 2), ("conv", 3), ("ffn", 3)]


def add_const_cols(L):
    L.add_cols("eps", np.full((128, 1), EPS, np.float32))
    L.add_cols("one", np.full((128, 1), 1.0, np.float32))
    jj = np.arange(128) % 32
    inv = (np.float32(10000.0) ** (-(np.arange(16, dtype=np.float32) / np.float32(16.0)))).astype(np.float32)
    L.add_cols("rope_inv", inv[jj % 16].reshape(128, 1))
    L.add_cols("rope_sign", np.where(jj < 16, -1.0, 1.0).astype(np.float32).reshape(128, 1))


def const_mats():
    kk_, qq_ = np.meshgrid(np.arange(128), np.arange(128), indexing="ij")
    cmask = ((kk_ // 64) <= (qq_ // 64)).astype(np.float32)
    tri = (kk_ <= qq_).astype(np.float32)
    mstrict = (kk_ > qq_).astype(np.float32)
    cmat = np.stack([np.eye(128, dtype=np.float32), np.ones((128, 128), np.float32), cmask, tri], 1)
    cmat32 = np.stack([tri, mstrict, np.ones((128, 128), np.float32)], 1)
    return cmat, cmat32


def run(inputs, stages, trace=False):
    inp = {k: np.asarray(v) for k, v in inputs.items()}
    L = make_layout(inp)
    add_const_cols(L)
    wts = np.stack(L.tiles, 0)
    cvec = np.concatenate(L.cols, 1)
    cmat, cmat32 = const_mats()
    nc = build_program(L, stages)
    x = inp["x"]
    in_maps = []
    for b in range(8):
        in_maps.append({"x": np.ascontiguousarray(x[b].T), "wts": wts, "cvec": cvec, "cmat": cmat, "cmat32": cmat32,
                        "pos": np.ascontiguousarray(inp["positions"][b].reshape(1, S).astype(np.int32))})
    res = run_bass_kernel_spmd(nc, in_maps, core_ids=list(range(8)), trace=trace)
    out = np.stack([res.results[b]["out"].T for b in range(8)], 0)
    return np.ascontiguousarray(out.astype(np.float32)), res


def kernel(**inputs):
    out, _ = run(inputs, ALL_STAGES)
    return out
```

```python
import math
from contextlib import ExitStack
import numpy as np
import concourse.bass as bass
import concourse.mybir as mybir
from concourse.bass_utils import run_bass_kernel_spmd
from concourse.alu_op_type import AluOpType as ALU

AF = mybir.ActivationFunctionType
F32 = mybir.dt.float32
BF16 = mybir.dt.bfloat16
I32 = mybir.dt.int32

D = 1024
S = 2048
DEPTH = 4
EPS = 1e-6
TB = 512
NTB = S // TB


class H:
    __slots__ = ("name", "w", "r")

    def __init__(self, name=""):
        self.name = name
        self.w = None
        self.r = []


class Op:
    __slots__ = ("id", "eng", "kind", "payload", "inc", "deps", "dur", "succ", "nleft", "ready", "fin", "seq", "pos")


def _free_elems(ap):
    try:
        n = 1
        for d in ap.shape[1:]:
            n *= int(d)
        return n
    except Exception:
        return 512


class Prog:
    COMPUTE = ("pe", "act", "dve", "pool")
    ALL = ("pe", "act", "dve", "pool", "sp")
    LAT = 250.0

    def __init__(self, nc, ring=8):
        self.nc = nc
        self.ring_n = ring
        self.segs = [[]]
        self.cur_reorder = True
        self.seg_reorder = [True]

    def _new(self, eng, kind, payload, inc, reads, writes, dur):
        seg = self.segs[-1]
        o = Op()
        o.id = len(seg)
        o.eng, o.kind, o.payload, o.inc, o.dur = eng, kind, payload, inc, dur
        deps = {}
        sid = len(self.segs) - 1
        for h in reads:
            if h.w is not None and h.w[0] == sid:
                deps[h.w[1]] = True
        for h in writes:
            if h.w is not None and h.w[0] == sid:
                deps.setdefault(h.w[1], False)
            for (sg, r) in h.r:
                if sg == sid:
                    deps.setdefault(r, False)
        o.deps = deps
        seg.append(o)
        for h in reads:
            h.r.append((sid, o.id))
        for h in writes:
            h.w = (sid, o.id)
            h.r = []
        return o

    def op(self, eng, name, *args, reads=(), writes=(), inc=True, **kwargs):
        if eng == "pe":
            rhs = kwargs.get("rhs", kwargs.get("identity"))
            n = _free_elems(rhs) if rhs is not None else 128
            mult = 4.0 if (rhs is not None and rhs.dtype == F32) else 1.0
            dur = 35.0 + mult * max(64, n) / 2.4
        elif eng == "act":
            o_ = kwargs.get("out")
            dur = 200.0 + _free_elems(o_) / 1.2
        else:
            o_ = kwargs.get("out", args[0] if args else None)
            dur = 120.0 + _free_elems(o_) / 0.96
        return self._new(eng, "op", (name, args, kwargs), inc, reads, writes, dur)

    def dma(self, eng, out_ap, in_ap, reads=(), writes=(), **kw):
        try:
            nbytes = int(out_ap.shape[0]) * _free_elems(out_ap) * 4
        except Exception:
            nbytes = 1 << 20
        dur = 2000.0 + nbytes / 300.0
        return self._new(eng, "dma", (out_ap, in_ap, kw), True, reads, writes, dur)

    def barrier(self):
        if self.segs[-1]:
            self.segs.append([])
            self.seg_reorder.append(self.cur_reorder)
        else:
            self.seg_reorder[-1] = self.cur_reorder

    def wait_tok(self, eng, tok):
        pass

    def _schedule(self, seg, reorder=True):
        import heapq
        unit_of = [0] * len(seg)
        units = []
        open_pe = None
        for o in seg:
            if o.eng == "pe" and o.kind == "op":
                if open_pe is None:
                    open_pe = len(units)
                    units.append({"m": [], "eng": "pe", "dur": 0.0, "kind": "op"})
                u = units[open_pe]
                u["m"].append(o.id)
                u["dur"] += o.dur
                unit_of[o.id] = open_pe
                if o.inc:
                    open_pe = None
            else:
                unit_of[o.id] = len(units)
                units.append({"m": [o.id], "eng": o.eng, "dur": o.dur, "kind": o.kind})
        if open_pe is not None:
            seg[units[open_pe]["m"][-1]].inc = True
        nu = len(units)
        deps = [set() for _ in range(nu)]
        for o in seg:
            u = unit_of[o.id]
            for d in o.deps:
                du = unit_of[d]
                if du != u:
                    deps[u].add(du)
        succ = [[] for _ in range(nu)]
        nleft = [0] * nu
        ready = [0.0] * nu
        fin = [0.0] * nu
        for u in range(nu):
            nleft[u] = len(deps[u])
            for d in deps[u]:
                succ[d].append(u)
        if not (getattr(self, "reorder", True) and reorder):
            order = {e: [] for e in self.ALL}
            for u in range(nu):
                for oid in units[u]["m"]:
                    order[units[u]["eng"]].append(seg[oid])
            return order
        tail = [0.0] * nu
        for u in range(nu - 1, -1, -1):
            m_ = 0.0
            for su in succ[u]:
                v_ = tail[su] + (0.0 if units[su]["eng"] == units[u]["eng"] else self.LAT)
                if v_ > m_:
                    m_ = v_
            tail[u] = units[u]["dur"] + m_
        heaps = {e: [] for e in self.ALL}
        tnow = {e: 0.0 for e in self.ALL}
        for u in range(nu):
            if nleft[u] == 0:
                heapq.heappush(heaps[units[u]["eng"]], (0.0, u))
        order = {e: [] for e in self.ALL}
        done = 0
        while done < nu:
            best = None
            for e in self.ALL:
                hp = heaps[e]
                if not hp:
                    continue
                t = tnow[e]
                if hp[0][0] <= t:
                    tmp = []
                    while hp and hp[0][0] <= t:
                        tmp.append(heapq.heappop(hp))
                    cand = min(tmp, key=lambda it: (-tail[it[1]], it[1]))
                    for it in tmp:
                        heapq.heappush(hp, it)
                    key = (t, cand[1], e, cand)
                else:
                    key = (hp[0][0], hp[0][1], e, hp[0])
                if best is None or key[:2] < best[:2]:
                    best = key
            start, u, e, item = best
            hp = heaps[e]
            hp.remove(item)
            heapq.heapify(hp)
            un = units[u]
            fin[u] = start + un["dur"]
            tnow[e] = start + (60.0 if un["kind"] == "dma" else un["dur"])
            for oid in un["m"]:
                order[e].append(seg[oid])
            done += 1
            for su in succ[u]:
                lat = 0.0 if (units[su]["eng"] == e and un["kind"] != "dma") else self.LAT
                r = fin[u] + lat
                if r > ready[su]:
                    ready[su] = r
                nleft[su] -= 1
                if nleft[su] == 0:
                    heapq.heappush(heaps[units[su]["eng"]], (ready[su], su))
        return order

    def replay(self, block, sems):
        names = {"pe": "tensor", "act": "scalar", "dve": "vector", "pool": "gpsimd", "sp": "sync"}
        streams = {e: [] for e in self.ALL}
        cnt = {e: 0 for e in self.COMPUTE}
        waited = {e: {} for e in self.ALL}
        ring_idx = {e: 0 for e in ("sp", "act", "pool")}
        dma_uses = {}

        def emit_wait(e, k, v):
            if waited[e].get(k, 0) < v:
                streams[e].append(("wait", k, v))
                waited[e][k] = v

        for si_, seg in enumerate(self.segs):
            if not seg:
                continue
            order = self._schedule(seg, self.seg_reorder[si_])
            for e in self.COMPUTE:
                for o in reversed(order[e]):
                    if o.kind == "op":
                        o.inc = True
                        break
            tok = {}
            for e in self.ALL:
                c = cnt.get(e, 0)
                pend = []
                ri = ring_idx.get(e, 0)
                for o in order[e]:
                    if o.kind == "dma":
                        j = ri % self.ring_n
                        ri += 1
                        k = ("dma", e, j)
                        prev = dma_uses.get(k, 0)
                        dma_uses[k] = prev + 1
                        tok[o.id] = (k, 16 * (prev + 1))
                        o.seq = (k, prev)
                    else:
                        if o.inc:
                            c += 1
                            tok[o.id] = (("tl", e), c)
                            for q in pend:
                                tok[q] = (("tl", e), c)
                            pend = []
                        else:
                            pend.append(o.id)
                assert not pend
                if e in cnt:
                    cnt[e] = c
                if e in ring_idx:
                    ring_idx[e] = ri
            for e in self.ALL:
                for o in order[e]:
                    need = {}
                    for d, raw in o.deps.items():
                        po = seg[d]
                        k, v = tok[d]
                        if po.eng == e and po.kind != "dma":
                            if e == "pe":
                                continue
                        if need.get(k, 0) < v:
                            need[k] = v
                    if o.kind == "dma":
                        k, prev = o.seq
                        if prev > 0 and need.get(k, 0) < 16 * prev:
                            need[k] = 16 * prev
                    for k, v in need.items():
                        emit_wait(e, k, v)
                    if o.kind == "dma":
                        streams[e].append(("dma", o.payload, o.seq[0]))
                    else:
                        streams[e].append(("op", o.payload, o.inc))
            toks = [(("tl", e), cnt[e]) for e in self.COMPUTE if cnt[e] > 0]
            toks += [(k, 16 * n_) for k, n_ in dma_uses.items()]
            for e in self.ALL:
                for k, v in toks:
                    emit_wait(e, k, v)

        def make(ename):
            stream = streams[ename]

            def body(eng):
                for it in stream:
                    if it[0] == "wait":
                        eng.wait_ge(sems[it[1]], it[2])
                    elif it[0] == "op":
                        nm, a, kw = it[1]
                        ins = getattr(eng, nm)(*a, **kw)
                        if it[2]:
                            ins.then_inc(sems[("tl", ename)], 1)
                    else:
                        o, i, kw = it[1]
                        eng.dma_start(out=o, in_=i, **kw).then_inc(sems[it[2]], 16)
            return body

        self.n_instr = {e: len(streams[e]) for e in self.ALL}
        self.dbg_streams = streams
        semv = {}
        pc = {e: 0 for e in self.ALL}
        progress = True
        while progress:
            progress = False
            for e in self.ALL:
                st = streams[e]
                while pc[e] < len(st):
                    it = st[pc[e]]
                    if it[0] == "wait":
                        if semv.get(it[1], 0) < it[2]:
                            break
                    elif it[0] == "op":
                        if it[2]:
                            semv[("tl", e)] = semv.get(("tl", e), 0) + 1
                    else:
                        semv[it[2]] = semv.get(it[2], 0) + 16
                    pc[e] += 1
                    progress = True
        stuck = {e: (pc[e], len(streams[e]), streams[e][pc[e]][:3] if pc[e] < len(streams[e]) else None) for e in self.ALL}
        if any(pc[e] < len(streams[e]) for e in self.ALL):
            raise RuntimeError(f"semaphore program deadlocks: {stuck} sems={ {k: v for k, v in semv.items() if k[0] == 'tl'} }")
        for ename in self.ALL:
            if streams[ename]:
                getattr(block, names[ename])(make(ename))

    def sem_keys(self):
        keys = [("tl", e) for e in self.COMPUTE]
        for e in ("sp", "act", "pool"):
            for j in range(self.ring_n):
                keys.append(("dma", e, j))
        return keys


def tileA(W, k0, n0):
    blk = np.zeros((1024, 1024), np.float32)
    sub = W[k0:k0 + 1024, n0:n0 + 1024]
    blk[:sub.shape[0], :sub.shape[1]] = sub
    return blk.reshape(8, 128, 1024).transpose(1, 0, 2).reshape(128, 8192)


def tileB(W, n0, ncol, nk):
    sub = W[:nk * 128, n0:n0 + ncol]
    return np.ascontiguousarray(sub.reshape(nk, 128, ncol).transpose(1, 0, 2)).reshape(128, nk * ncol)


def colvec(v):
    n = v.shape[0] // 128
    return np.ascontiguousarray(v.reshape(n, 128).T)


class Layout:
    def __init__(self):
        self.tiles = []
        self.tile_ids = {}
        self.cols = []
        self.col_off = {}
        self.ncol = 0

    def add_tile(self, name, arr):
        assert arr.shape == (128, 8192), (name, arr.shape)
        self.tile_ids[name] = len(self.tiles)
        self.tiles.append(arr)

    def add_cols(self, name, arr):
        arr = np.asarray(arr, np.float32)
        assert arr.shape[0] == 128
        self.col_off[name] = (self.ncol, arr.shape[1])
        self.cols.append(arr)
        self.ncol += arr.shape[1]


def make_layout(inp):
    L = Layout()
    for i in range(DEPTH):
        for nm in ("norm_mix_pre", "norm_mix_post", "norm_ffn_pre", "norm_ffn_post"):
            L.add_cols(f"{nm}{i}", colvec(inp[nm][i]))
        w1 = inp["ffn_w_in"][i]
        w2 = inp["ffn_w_out"][i]
        for fg in range(4):
            L.add_tile(f"ffn{i}_w1_{fg}", tileA(w1, 0, fg * 1024))
        for og in range(4):
            L.add_tile(f"ffn{i}_w2_{og}", tileB(w2, og * 256, 256, 32))
    for j in range(2):
        w1 = inp["conv_w_pw1"][j]
        L.add_tile(f"conv{j}_pw1_a", tileA(w1, 0, 0))
        L.add_tile(f"conv{j}_pw1_b", tileA(w1, 0, 1024))
        L.add_tile(f"conv{j}_pw2", tileA(inp["conv_w_pw2"][j], 0, 0))
        L.add_cols(f"conv{j}_b_pw1", colvec(inp["conv_b_pw1"][j]))
        wdw = inp["conv_w_dw"][j]
        L.add_cols(f"conv{j}_w_dw", np.ascontiguousarray(wdw.reshape(31, 8, 128).transpose(2, 1, 0)).reshape(128, 8 * 31))
        L.add_cols(f"conv{j}_b_dw", colvec(inp["conv_b_dw"][j]))
        L.add_cols(f"conv{j}_ln_g", colvec(inp["conv_ln_g"][j]))
        L.add_cols(f"conv{j}_ln_b", colvec(inp["conv_ln_b"][j]))
        L.add_cols(f"conv{j}_b_pw2", colvec(inp["conv_b_pw2"][j]))
    wi_ = inp["ssm_w_in"][0]
    for j in range(2):
        L.add_tile(f"ssm_z_{j}", tileA(wi_, 0, j * 1024))
    for j in range(4):
        L.add_tile(f"ssm_xbc_{j}", tileA(wi_, 0, 2048 + j * 1024))
    L.add_tile("ssm_dt", tileA(wi_, 0, 6144))
    wo_ = inp["ssm_w_out"][0]
    for j in range(2):
        L.add_tile(f"ssm_out_{j}", tileA(wo_, j * 1024, 0))
    cw = inp["ssm_conv_w"][0]
    L.add_cols("ssm_conv_w", np.ascontiguousarray(cw.reshape(4, 32, 128).transpose(2, 1, 0)).reshape(128, 128))
    L.add_cols("ssm_conv_b", colvec(inp["ssm_conv_b"][0]))
    L.add_cols("ssm_dtb", np.broadcast_to(inp["ssm_dt_bias"][0][None, :], (128, 32)))
    L.add_cols("ssm_alog", np.broadcast_to(inp["ssm_a_log"][0][None, :], (128, 32)))
    L.add_cols("ssm_d", np.broadcast_to(inp["ssm_d"][0][None, :], (128, 32)))
    L.add_cols("ssm_norm_w", colvec(inp["ssm_norm_w"][0]))
    w_in = inp["mla_w_in"][0]
    w_in_ext = np.concatenate([w_in, w_in[:, 656:672], w_in[:, 640:656]], axis=1)
    L.add_tile("mla_in", tileA(w_in_ext, 0, 0))
    wuq = inp["mla_w_uq"][0]
    blocks = []
    for h in range(16):
        b = wuq[:, h * 96:(h + 1) * 96]
        blocks.append(np.concatenate([b, b[:, 80:96], b[:, 64:80]], axis=1))
    wuq_ext = np.concatenate(blocks, axis=1)
    t = np.zeros((128, 8192), np.float32)
    t[:, :3 * 2048] = tileB(wuq_ext, 0, 2048, 3)
    L.add_tile("mla_uq", t)
    t = np.zeros((128, 8192), np.float32)
    t[:, :2 * 2048] = tileB(inp["mla_w_ukv"][0], 0, 2048, 2)
    L.add_tile("mla_ukv", t)
    L.add_tile("mla_o", tileA(inp["mla_w_o"][0], 0, 0))
    L.add_cols("mla_q_norm", colvec(inp["mla_q_norm"][0]))
    L.add_cols("mla_kv_norm", colvec(inp["mla_kv_norm"][0]))
    return L


class K:
    pass


def build_program(L, stages):
    nc = bass.Bass("TRN2", target_bir_lowering=False)
    NT = len(L.tiles)
    NCV = L.ncol
    x_d = nc.dram_tensor("x", [D, S], F32, kind="ExternalInput").ap()
    w_d = nc.dram_tensor("wts", [NT, 128, 8192], F32, kind="ExternalInput").ap()
    cv_d = nc.dram_tensor("cvec", [128, NCV], F32, kind="ExternalInput").ap()
    cm_d = nc.dram_tensor("cmat", [128, 4, 128], F32, kind="ExternalInput").ap()
    cm32_d = nc.dram_tensor("cmat32", [128, 3, 128], F32, kind="ExternalInput").ap()
    pos_d = nc.dram_tensor("pos", [1, S], I32, kind="ExternalInput").ap()
    o_d = nc.dram_tensor("out", [D, S], F32, kind="ExternalOutput").ap()

    p = Prog(nc)
    import os
    p.reorder = os.environ.get("K_REORDER", "1") == "1"
    es = ExitStack()
    with es:
        sems = {k: es.enter_context(nc.semaphore("s_" + "_".join(map(str, k)))) for k in p.sem_keys()}
        x_sb = es.enter_context(nc.sbuf_tensor("x_sb", [128, 8, S], F32))
        wslot = [es.enter_context(nc.sbuf_tensor(f"wslot{i}", [128, 8192], BF16)) for i in range(3)]
        cvec = es.enter_context(nc.sbuf_tensor("cvec_sb", [128, NCV], F32))
        cmat = es.enter_context(nc.sbuf_tensor("cmat_sb", [128, 4, 128], BF16))
        cmat32 = es.enter_context(nc.sbuf_tensor("cmat32_sb", [128, 3, 128], F32))
        AW = 22016
        arena = es.enter_context(nc.sbuf_tensor("arena", [128, AW], F32))
        banks = [es.enter_context(nc.psum_tensor(f"bank{i}", [128, 512], F32)) for i in range(8)]
        block = es.enter_context(nc.Block())

        k = K()
        k.nc, k.p = nc, p
        hb = [H(f"bank{i}") for i in range(8)]
        hx = [H(f"x{tb}") for tb in range(NTB)]
        hslot = [H(f"slot{i}") for i in range(3)]
        hcv, hcm = H("cvec"), H("cmat")
        ident = cmat[:, 0, :]
        ones = cmat[:, 1, :]
        cmask = cmat[:, 2, :]
        tri_bf = cmat[:, 3, :]
        tri32 = cmat32[:, 0, :]
        mstrict32 = cmat32[:, 1, :]
        ones32 = cmat32[:, 2, :]

        apos = [0]

        def carve_f32(n):
            o = apos[0]
            apos[0] += n
            assert apos[0] <= AW, apos[0]
            return arena[:, o:o + n]

        def carve_bf(n):
            assert n % 2 == 0
            return carve_f32(n // 2).bitcast(BF16)

        def col(name, c=0, n=1):
            o, w = L.col_off[name]
            return cvec[:, o + c:o + c + n]

        bi = [0]

        def next_bank():
            i = bi[0] % 6
            bi[0] += 1
            return banks[i], hb[i]

        si = [0]

        def next_sbank():
            i = 6 + si[0] % 2
            si[0] += 1
            return banks[i], hb[i]

        wi = [0]

        def load_w(name):
            i = wi[0] % 3
            wi[0] += 1
            tid = L.tile_ids[name]
            dst = wslot[i][:].rearrange("p (a b) -> p a b", a=8)
            src = w_d[tid].rearrange("p (a b) -> p a b", a=8)
            p.dma("pool", dst, src, writes=[hslot[i]])
            return wslot[i], hslot[i]

        p.dma("sp", cvec[:], cv_d, writes=[hcv])
        p.dma("pool", cmat[:], cm_d, writes=[hcm])
        p.dma("sp", cmat32[:], cm32_d, writes=[hcm])
        xv = x_d.rearrange("(c q) t -> q c t", q=128)
        for tb in range(NTB):
            p.dma("sp" if tb % 2 == 0 else "act", x_sb[:, :, tb * TB:(tb + 1) * TB], xv[:, :, tb * TB:(tb + 1) * TB], writes=[hx[tb]])

        def rstd_from_sq(sq, hsq, nchunk, dim, rstd, hrstd, lnt, hlnt, n=512):
            sb, hsb = next_sbank()
            for c in range(nchunk):
                p.op("pe", "matmul", sb[:, 0:n], lhsT=ones, rhs=sq[:, c, :], start=(c == 0), stop=(c == nchunk - 1), reads=[hsq, hcm], writes=[hsb], inc=(c == nchunk - 1))
            p.op("act", "activation", out=lnt, in_=sb[:, 0:n], func=AF.Ln, scale=1.0 / dim, bias=col("eps"), reads=[hsb, hcv], writes=[hlnt])
            p.op("act", "activation", out=rstd, in_=lnt, func=AF.Exp, scale=-0.5, reads=[hlnt], writes=[hrstd])

        def prenorm(tb, wname, hbuf, hh, sq, hsq, rstd, hrstd, lnt, hlnt):
            xs = x_sb[:, :, tb * TB:(tb + 1) * TB]
            for c in range(8):
                p.op("act", "activation", out=sq[:, c, :], in_=xs[:, c, :], func=AF.Square, reads=[hx[tb]], writes=[hsq])
            rstd_from_sq(sq, hsq, 8, D, rstd, hrstd, lnt, hlnt)
            for c in range(8):
                p.op("dve", "scalar_tensor_tensor", out=hbuf[:, c, :], in0=xs[:, c, :], scalar=col(wname, c), in1=rstd, op0=ALU.mult, op1=ALU.mult, reads=[hx[tb], hrstd, hcv], writes=[hh])

        def postnorm(tb, wname, ysb, hysb, sq, hsq, rstd, hrstd, lnt, hlnt):
            xs = x_sb[:, :, tb * TB:(tb + 1) * TB]
            rstd_from_sq(sq, hsq, 8, D, rstd, hrstd, lnt, hlnt)
            for c in range(8):
                p.op("dve", "scalar_tensor_tensor", out=ysb[:, c, :], in0=ysb[:, c, :], scalar=col(wname, c), in1=rstd, op0=ALU.mult, op1=ALU.mult, reads=[hysb, hrstd, hcv], writes=[hysb])
                p.op("dve", "tensor_tensor", out=xs[:, c, :], in0=xs[:, c, :], in1=ysb[:, c, :], op=ALU.add, reads=[hysb, hx[tb]], writes=[hx[tb]])

        def evac_y(ps, hps, c, ysb, hysb, sq, hsq, bias=None):
            if bias is None:
                p.op("act", "activation", out=ysb[:, c, :], in_=ps[:], func=AF.Copy, reads=[hps], writes=[hysb])
                p.op("act", "activation", out=sq[:, c, :], in_=ps[:], func=AF.Square, reads=[hps], writes=[hsq])
            else:
                p.op("act", "activation", out=ysb[:, c, :], in_=ps[:], func=AF.Identity, bias=bias, reads=[hps, hcv], writes=[hysb])
                p.op("act", "activation", out=sq[:, c, :], in_=ps[:], func=AF.Square, bias=bias, reads=[hps, hcv], writes=[hsq])

        def ffn(i):
            apos[0] = 0
            hbuf = [carve_bf(8 * 512).rearrange("p (a b) -> p a b", a=8) for _ in range(2)]
            hh = [H("h0"), H("h1")]
            abuf = carve_bf(32 * 512).rearrange("p (a b) -> p a b", a=32)
            ha = H("a")
            ysb = carve_f32(8 * 512).rearrange("p (a b) -> p a b", a=8)
            hysb = H("ysb")
            sq = carve_bf(8 * 512).rearrange("p (a b) -> p a b", a=8)
            hsq = H("sq")
            rstd = [carve_f32(512) for _ in range(2)]
            hrstd = [H("rstd0"), H("rstd1")]
            lnt = carve_f32(512)
            hlnt = H("lnt")
            rt = [carve_f32(512) for _ in range(2)]
            hrt = [H("rt0"), H("rt1")]
            rti = 0
            for tb in range(NTB):
                hb_, hh_ = hbuf[tb % 2], hh[tb % 2]
                prenorm(tb, f"norm_ffn_pre{i}", hb_, hh_, sq, hsq, rstd[0], hrstd[0], lnt, hlnt)
                for fg in range(4):
                    ws, hws = load_w(f"ffn{i}_w1_{fg}")
                    wv = ws[:].rearrange("p (a b) -> p a b", a=8)
                    for fc in range(8):
                        ps, hps = next_bank()
                        for kk in range(8):
                            p.op("pe", "matmul", ps[:], lhsT=wv[:, kk, fc * 128:(fc + 1) * 128], rhs=hb_[:, kk, :], start=(kk == 0), stop=(kk == 7), reads=[hws, hh_], writes=[hps], inc=(kk == 7))
                        r_, hr_ = rt[rti % 2], hrt[rti % 2]
                        rti += 1
                        p.op("act", "activation", out=r_, in_=ps[:], func=AF.Relu, reads=[hps], writes=[hr_])
                        ac = fg * 8 + fc
                        p.op("dve", "tensor_tensor", out=abuf[:, ac, :], in0=r_, in1=ps[:], op=ALU.mult, reads=[hps, hr_], writes=[ha])
                for og in range(4):
                    ws, hws = load_w(f"ffn{i}_w2_{og}")
                    wv = ws[:].rearrange("p (a b) -> p a b", a=32)
                    for oc in range(2):
                        ps, hps = next_bank()
                        for kk in range(32):
                            p.op("pe", "matmul", ps[:], lhsT=wv[:, kk, oc * 128:(oc + 1) * 128], rhs=abuf[:, kk, :], start=(kk == 0), stop=(kk == 31), reads=[hws, ha], writes=[hps], inc=(kk == 31))
                        evac_y(ps, hps, og * 2 + oc, ysb, hysb, sq, hsq)
                postnorm(tb, f"norm_ffn_post{i}", ysb, hysb, sq, hsq, rstd[1], hrstd[1], lnt, hlnt)

        def conv_mixer(i, j):
            apos[0] = 0
            PAD = 32
            G = carve_bf(8 * (PAD + TB)).rearrange("p (a b) -> p a b", a=8)
            hG = H("G")
            hbuf = carve_bf(8 * 512).rearrange("p (a b) -> p a b", a=8)
            hh = H("h")
            cvb, hcvb = hbuf, hh
            sq = carve_bf(8 * 512).rearrange("p (a b) -> p a b", a=8)
            hsq = H("sq")
            cvf = carve_f32(8 * 512).rearrange("p (a b) -> p a b", a=8)
            hcvf = H("cvf")
            ysb, hysb = cvf, hcvf
            vbuf = carve_bf(8 * 512).rearrange("p (a b) -> p a b", a=8)
            hv = H("v")
            diag = [carve_bf(31 * 128).rearrange("p (a b) -> p a b", a=31) for _ in range(2)]
            hdiag = [H("diag0"), H("diag1")]
            rstd = carve_f32(512)
            hrstd = H("rstd")
            lnt = carve_f32(512)
            hlnt = H("lnt")
            sg = [carve_f32(512) for _ in range(2)]
            hsg = [H("sg0"), H("sg1")]
            msq = carve_f32(512)
            hmsq = H("msq")
            var, hvar = msq, hmsq
            tmpf, htmp = sg, hsg
            sqp = carve_bf(8 * 512).rearrange("p (a b) -> p a b", a=8)
            hsqp = H("sqp")
            rstdp = carve_f32(512)
            hrstdp = H("rstdp")
            lntp = carve_f32(512)
            hlntp = H("lntp")
            wdw = col(f"conv{j}_w_dw", 0, 8 * 31).rearrange("p (c k) -> p c k", c=8)

            p.op("dve", "memset", G[:, :, 0:PAD], 0.0, writes=[hG])
            wa, hwa = load_w(f"conv{j}_pw1_a")
            wb, hwb = load_w(f"conv{j}_pw1_b")
            w2, hw2 = load_w(f"conv{j}_pw2")
            wav = wa[:].rearrange("p (a b) -> p a b", a=8)
            wbv = wb[:].rearrange("p (a b) -> p a b", a=8)
            w2v = w2[:].rearrange("p (a b) -> p a b", a=8)
            def prenorm_stats(tb):
                xs_ = x_sb[:, :, tb * TB:(tb + 1) * TB]
                for c in range(8):
                    p.op("act", "activation", out=sqp[:, c, :], in_=xs_[:, c, :], func=AF.Square, reads=[hx[tb]], writes=[hsqp])
                rstd_from_sq(sqp, hsqp, 8, D, rstdp, hrstdp, lntp, hlntp)

            def prenorm_apply(tb):
                xs_ = x_sb[:, :, tb * TB:(tb + 1) * TB]
                for c in range(8):
                    p.op("dve", "scalar_tensor_tensor", out=hbuf[:, c, :], in0=xs_[:, c, :], scalar=col(f"norm_mix_pre{i}", c), in1=rstdp, op0=ALU.mult, op1=ALU.mult,
                         reads=[hx[tb], hrstdp, hcv], writes=[hh])

            def build_diag(n):
                if n >= NTB * 8:
                    return
                c_ = n % 8
                p.op("dve", "tensor_tensor", out=diag[n % 2], in0=ident.unsqueeze(1).to_broadcast([128, 31, 128]), in1=wdw[:, c_, :].unsqueeze(2).to_broadcast([128, 31, 128]), op=ALU.mult,
                     reads=[hcm, hcv], writes=[hdiag[n % 2]])

            di = 0
            prenorm_stats(0)
            prenorm_apply(0)
            build_diag(0)
            build_diag(1)
            for tb in range(NTB):
                if tb > 0:
                    p.op("dve", "tensor_copy", out=G[:, :, 0:PAD], in_=G[:, :, TB:TB + PAD], reads=[hG], writes=[hG])
                for c in range(8):
                    psa, hpsa = next_bank()
                    for kk in range(8):
                        p.op("pe", "matmul", psa[:], lhsT=wav[:, kk, c * 128:(c + 1) * 128], rhs=hbuf[:, kk, :], start=(kk == 0), stop=(kk == 7), reads=[hwa, hh], writes=[hpsa], inc=(kk == 7))
                    psb, hpsb = next_bank()
                    for kk in range(8):
                        p.op("pe", "matmul", psb[:], lhsT=wbv[:, kk, c * 128:(c + 1) * 128], rhs=hbuf[:, kk, :], start=(kk == 0), stop=(kk == 7), reads=[hwb, hh], writes=[hpsb], inc=(kk == 7))
                    s_, hs_ = sg[c % 2], hsg[c % 2]
                    p.op("act", "activation", out=s_, in_=psb[:], func=AF.Sigmoid, bias=col(f"conv{j}_b_pw1", 8 + c), reads=[hpsb, hcv], writes=[hs_])
                    p.op("dve", "scalar_tensor_tensor", out=G[:, c, PAD:PAD + TB], in0=psa[:], scalar=col(f"conv{j}_b_pw1", c), in1=s_, op0=ALU.add, op1=ALU.mult, reads=[hpsa, hs_, hcv], writes=[hG])
                for c in range(8):
                    dg, hdg = diag[di % 2], hdiag[di % 2]
                    ps, hps = next_bank()
                    for kk in range(31):
                        o = PAD - 30 + kk
                        p.op("pe", "matmul", ps[:], lhsT=dg[:, kk, :], rhs=G[:, c, o:o + TB], start=(kk == 0), stop=(kk == 30), reads=[hdg, hG], writes=[hps], inc=(kk == 30))
                    build_diag(di + 2)
                    di += 1
                    if c == 0 and tb + 1 < NTB:
                        prenorm_stats(tb + 1)
                    bdw = col(f"conv{j}_b_dw", c)
                    p.op("act", "activation", out=cvf[:, c, :], in_=ps[:], func=AF.Identity, bias=bdw, reads=[hps, hcv], writes=[hcvf])
                    p.op("act", "activation", out=cvb[:, c, :], in_=ps[:], func=AF.Identity, bias=bdw, reads=[hps, hcv], writes=[hcvb])
                    p.op("act", "activation", out=sq[:, c, :], in_=ps[:], func=AF.Square, bias=bdw, reads=[hps, hcv], writes=[hsq])
                sm, hsm = next_sbank()
                for c in range(8):
                    p.op("pe", "matmul", sm[:], lhsT=ones, rhs=cvb[:, c, :], start=(c == 0), stop=(c == 7), reads=[hcvb, hcm], writes=[hsm], inc=(c == 7))
                s2, hs2 = next_sbank()
                for c in range(8):
                    p.op("pe", "matmul", s2[:], lhsT=ones, rhs=sq[:, c, :], start=(c == 0), stop=(c == 7), reads=[hsq, hcm], writes=[hs2], inc=(c == 7))
                p.op("act", "activation", out=msq, in_=sm[:], func=AF.Square, scale=1.0 / D, reads=[hsm], writes=[hmsq])
                p.op("dve", "scalar_tensor_tensor", out=var, in0=s2[:], scalar=1.0 / D, in1=msq, op0=ALU.mult, op1=ALU.subtract, reads=[hs2, hmsq], writes=[hvar])
                p.op("act", "activation", out=lnt, in_=var, func=AF.Ln, bias=col("eps"), reads=[hvar, hcv], writes=[hlnt])
                p.op("act", "activation", out=rstd, in_=lnt, func=AF.Exp, scale=-0.5, reads=[hlnt], writes=[hrstd])
                for c in range(8):
                    t_, ht_ = tmpf[c % 2], htmp[c % 2]
                    p.op("dve", "scalar_tensor_tensor", out=t_, in0=sm[:], scalar=-1.0 / D, in1=cvf[:, c, :], op0=ALU.mult, op1=ALU.add, reads=[hsm, hcvf], writes=[ht_])
                    p.op("dve", "tensor_tensor", out=t_, in0=t_, in1=rstd, op=ALU.mult, reads=[ht_, hrstd], writes=[ht_])
                    p.op("act", "activation", out=vbuf[:, c, :], in_=t_, func=AF.Silu, scale=col(f"conv{j}_ln_g", c), bias=col(f"conv{j}_ln_b", c), reads=[ht_, hcv], writes=[hv])
                if tb + 1 < NTB:
                    prenorm_apply(tb + 1)
                for oc in range(8):
                    ps, hps = next_bank()
                    for kk in range(8):
                        p.op("pe", "matmul", ps[:], lhsT=w2v[:, kk, oc * 128:(oc + 1) * 128], rhs=vbuf[:, kk, :], start=(kk == 0), stop=(kk == 7), reads=[hw2, hv], writes=[hps], inc=(kk == 7))
                    evac_y(ps, hps, oc, ysb, hysb, sq, hsq, bias=col(f"conv{j}_b_pw2", oc))
                postnorm(tb, f"norm_mix_post{i}", ysb, hysb, sq, hsq, rstd, hrstd, lnt, hlnt)


        def mla_mixer(i):
            SC = 96 ** -0.5
            R0, R1, R2 = 0, 8192, 16384
            P = slice(64, 96)
            PI = 3.1415925
            TWO_PI = 2.0 * math.pi
            C1 = 6.28125
            C2 = TWO_PI - C1

            def v3(ap, a):
                return ap.rearrange("p (a b) -> p a b", a=a)

            apos[0] = R1
            cqn = v3(carve_bf(3 * S), 3)
            hcqn = H("cqn")
            ckvn = v3(carve_bf(2 * S), 2)
            hckvn = H("ckvn")
            kpe = carve_bf(S)
            hkpe = H("kpe")
            cosT = carve_bf(S)
            sinT = carve_bf(S)
            htab = H("tab")
            assert apos[0] == R2
            w_in, hw_in = load_w("mla_in")
            w_uq, hw_uq = load_w("mla_uq")
            w_kv, hw_kv = load_w("mla_ukv")
            winv = v3(w_in[:], 8)
            wuqv = w_uq[:, 0:3 * 2048].rearrange("p (a b) -> p a b", a=3)
            wkvv = w_kv[:, 0:2 * 2048].rearrange("p (a b) -> p a b", a=2)

            p.barrier()
            apos[0] = R0
            posi = carve_f32(S).bitcast(I32)
            ang = carve_f32(S)
            t1 = carve_f32(S)
            nf = carve_f32(S)
            apos[0] = R2
            ni = carve_f32(S).bitcast(I32)
            hA = H("ropeA")
            p.dma("sp", posi[P, :], pos_d.partition_broadcast(32), writes=[hA])
            p.op("dve", "tensor_copy", out=t1[P, :], in_=posi[P, :], reads=[hA], writes=[hA])
            p.op("dve", "tensor_scalar", out=ang[P, :], in0=t1[P, :], scalar1=col("rope_inv")[P, :], scalar2=None, op0=ALU.mult, reads=[hA, hcv], writes=[hA])
            for which in ("sin", "cos"):
                if which == "cos":
                    p.op("dve", "tensor_scalar", out=ang[P, :], in0=ang[P, :], scalar1=0.5 * math.pi, scalar2=None, op0=ALU.add, reads=[hA], writes=[hA])
                p.op("dve", "tensor_scalar", out=t1[P, :], in0=ang[P, :], scalar1=1.0 / TWO_PI, scalar2=None, op0=ALU.mult, reads=[hA], writes=[hA])
                p.op("dve", "tensor_copy", out=ni[P, :], in_=t1[P, :], reads=[hA], writes=[hA])
                p.op("dve", "tensor_copy", out=nf[P, :], in_=ni[P, :], reads=[hA], writes=[hA])
                p.op("dve", "scalar_tensor_tensor", out=t1[P, :], in0=nf[P, :], scalar=-C1, in1=ang[P, :], op0=ALU.mult, op1=ALU.add, reads=[hA], writes=[hA])
                p.op("dve", "scalar_tensor_tensor", out=t1[P, :], in0=nf[P, :], scalar=-C2, in1=t1[P, :], op0=ALU.mult, op1=ALU.add, reads=[hA], writes=[hA])
                p.op("dve", "tensor_scalar", out=nf[P, :], in0=t1[P, :], scalar1=PI, scalar2=-TWO_PI, op0=ALU.is_gt, op1=ALU.mult, reads=[hA], writes=[hA])
                p.op("dve", "tensor_tensor", out=t1[P, :], in0=t1[P, :], in1=nf[P, :], op=ALU.add, reads=[hA], writes=[hA])
                p.op("dve", "tensor_scalar", out=nf[P, :], in0=t1[P, :], scalar1=-PI, scalar2=TWO_PI, op0=ALU.is_lt, op1=ALU.mult, reads=[hA], writes=[hA])
                p.op("dve", "tensor_tensor", out=t1[P, :], in0=t1[P, :], in1=nf[P, :], op=ALU.add, reads=[hA], writes=[hA])
                p.op("dve", "tensor_scalar", out=t1[P, :], in0=t1[P, :], scalar1=PI, scalar2=-PI, op0=ALU.min, op1=ALU.max, reads=[hA], writes=[hA])
                if which == "sin":
                    p.op("act", "activation", out=nf[P, :], in_=t1[P, :], func=AF.Sin, reads=[hA], writes=[hA])
                    p.op("dve", "tensor_scalar", out=sinT[P, :], in0=nf[P, :], scalar1=col("rope_sign")[P, :], scalar2=None, op0=ALU.mult, reads=[hA, hcv], writes=[htab])
                else:
                    p.op("act", "activation", out=cosT[P, :], in_=t1[P, :], func=AF.Sin, reads=[hA], writes=[htab])

            p.barrier()
            apos[0] = R0
            hbuf = v3(carve_bf(8 * 512), 8)
            hh = H("h")
            sq = v3(carve_bf(8 * 512), 8)
            hsq = H("sq")
            raw = v3(carve_f32(5 * 512), 5)
            hraw = H("raw")
            rstdq = carve_f32(512)
            hrq = H("rstdq")
            rstdk = carve_f32(512)
            hrk = H("rstdk")
            lnt = carve_f32(512)
            hlnt = H("lnt")
            apos[0] = R2
            rtA = carve_f32(512)
            rtB = carve_f32(512)
            hrt = H("ropetmp")
            for tb in range(NTB):
                tbs = slice(tb * TB, (tb + 1) * TB)
                prenorm(tb, f"norm_mix_pre{i}", hbuf, hh, sq, hsq, rstdq, hrq, lnt, hlnt)
                for m in range(5):
                    ps, hps = next_bank()
                    for kk in range(8):
                        p.op("pe", "matmul", ps[:], lhsT=winv[:, kk, m * 128:(m + 1) * 128], rhs=hbuf[:, kk, :], start=(kk == 0), stop=(kk == 7), reads=[hw_in, hh], writes=[hps], inc=(kk == 7))
                    p.op("act", "activation", out=raw[:, m, :], in_=ps[:], func=AF.Copy, reads=[hps], writes=[hraw])
                    p.op("act", "activation", out=sq[:, m, :], in_=ps[:], func=AF.Square, reads=[hps], writes=[hsq])
                psr, hpsr = next_bank()
                for kk in range(8):
                    p.op("pe", "matmul", psr[P, :], lhsT=winv[:, kk, 640:672], rhs=hbuf[:, kk, :], start=(kk == 0), stop=(kk == 7), reads=[hw_in, hh], writes=[hpsr], inc=(kk == 7))
                pss, hpss = next_bank()
                for kk in range(8):
                    p.op("pe", "matmul", pss[P, :], lhsT=winv[:, kk, 672:704], rhs=hbuf[:, kk, :], start=(kk == 0), stop=(kk == 7), reads=[hw_in, hh], writes=[hpss], inc=(kk == 7))
                p.op("dve", "tensor_tensor", out=rtA[P, :], in0=psr[P, :], in1=cosT[P, tbs], op=ALU.mult, reads=[hpsr, htab], writes=[hrt])
                p.op("dve", "tensor_tensor", out=rtB[P, :], in0=pss[P, :], in1=sinT[P, tbs], op=ALU.mult, reads=[hpss, htab], writes=[hrt])
                p.op("dve", "tensor_tensor", out=kpe[P, tbs], in0=rtA[P, :], in1=rtB[P, :], op=ALU.add, reads=[hrt], writes=[hkpe])
                rstd_from_sq(sq[:, 0:3, :], hsq, 3, 384, rstdq, hrq, lnt, hlnt)
                for m in range(3):
                    p.op("dve", "scalar_tensor_tensor", out=cqn[:, m, tbs], in0=raw[:, m, :], scalar=col("mla_q_norm", m), in1=rstdq, op0=ALU.mult, op1=ALU.mult, reads=[hraw, hrq, hcv], writes=[hcqn])
                rstd_from_sq(sq[:, 3:5, :], hsq, 2, 256, rstdk, hrk, lnt, hlnt)
                for m in range(2):
                    p.op("dve", "scalar_tensor_tensor", out=ckvn[:, m, tbs], in0=raw[:, 3 + m, :], scalar=col("mla_kv_norm", m), in1=rstdk, op0=ALU.mult, op1=ALU.mult, reads=[hraw, hrk, hcv], writes=[hckvn])

            p.barrier()
            apos[0] = R0
            attnT = v3(carve_bf(8 * S), 8)
            hattnT = H("attnT")
            apos[0] = R2
            scr = w_in
            Qt = [carve_bf(S), scr[:, 0:S]]
            Kt = [carve_bf(S), scr[:, S:2 * S]]
            Vt = [v3(carve_bf(16 * 66), 16), v3(scr[:, 2 * S:2 * S + 16 * 66], 16)]
            hQ = [[H(f"Q{q}_{t}") for t in range(NTB)] for q in range(2)]
            hK = [H("K0"), H("K1")]
            hV = [H("V0"), H("V1")]
            pairb = [v3(carve_bf(4 * 64), 4) for _ in range(2)]
            hpair = [H("pair0"), H("pair1")]
            Eb = [carve_bf(512) for _ in range(4)]
            hE = [H(f"E{q}") for q in range(4)]
            rtA = [carve_f32(512), scr[:, 5632:6656].bitcast(F32)]
            rtB = [carve_f32(512), scr[:, 6656:7680].bitcast(F32)]
            hrt = [H("ropetmp0"), H("ropetmp1")]
            rc = carve_f32(8)
            hrc = H("rc")
            for q in range(2):
                p.op("dve", "memset", Vt[q][:, :, 64:66], 1.0, writes=[hV[q]])
                p.op("dve", "memset", Qt[q][96:128, :], 0.0, writes=hQ[q])
                p.op("dve", "memset", Kt[q][96:128, :], 0.0, writes=[hK[q]])
            pj = [0]

            def nb67():
                q_ = 6 + pj[0] % 2
                pj[0] += 1
                return banks[q_], hb[q_]

            ei = 0
            pi_ = 0
            ri_ = 0
            stb = [0, 1, 4, 5]
            sti = [0]
            qbc = [0]
            hacc = [[H(f"acc{q}_{j}") for j in range(4)] for q in range(2)]
            for h in range(16):
                hp, hq = h // 2, h % 2
                sb_ = h % 2
                Q_, K_, V_ = Qt[sb_], Kt[sb_], Vt[sb_]
                for tb in range(NTB):
                    tbs = slice(tb * TB, (tb + 1) * TB)
                    hQ_ = hQ[sb_][tb]
                    psx, hpsx = nb67()
                    for kk in range(3):
                        p.op("pe", "matmul", psx[0:96, :], lhsT=wuqv[:, kk, h * 128:h * 128 + 96], rhs=cqn[:, kk, tbs], start=(kk == 0), stop=(kk == 2), reads=[hw_uq, hcqn], writes=[hpsx], inc=(kk == 2))
                    psy, hpsy = nb67()
                    for kk in range(3):
                        p.op("pe", "matmul", psy[P, :], lhsT=wuqv[:, kk, h * 128 + 96:h * 128 + 128], rhs=cqn[:, kk, tbs], start=(kk == 0), stop=(kk == 2), reads=[hw_uq, hcqn], writes=[hpsy], inc=(kk == 2))
                    ra, rb_, hr_ = rtA[ri_ % 2], rtB[ri_ % 2], hrt[ri_ % 2]
                    ri_ += 1
                    p.op("act", "activation", out=Q_[0:64, tbs], in_=psx[0:64, :], func=AF.Copy, reads=[hpsx], writes=[hQ_])
                    p.op("dve", "tensor_tensor", out=ra[P, :], in0=psx[P, :], in1=cosT[P, tbs], op=ALU.mult, reads=[hpsx, htab], writes=[hr_])
                    p.op("dve", "tensor_tensor", out=rb_[P, :], in0=psy[P, :], in1=sinT[P, tbs], op=ALU.mult, reads=[hpsy, htab], writes=[hr_])
                    p.op("dve", "tensor_tensor", out=Q_[P, tbs], in0=ra[P, :], in1=rb_[P, :], op=ALU.add, reads=[hr_], writes=[hQ_])
                    psk, hpsk = nb67()
                    for kk in range(2):
                        p.op("pe", "matmul", psk[0:64, :], lhsT=wkvv[:, kk, h * 128:h * 128 + 64], rhs=ckvn[:, kk, tbs], start=(kk == 0), stop=(kk == 1), reads=[hw_kv, hckvn], writes=[hpsk], inc=(kk == 1))
                    p.op("act", "activation", out=K_[0:64, tbs], in_=psk[0:64, :], func=AF.Copy, reads=[hpsk], writes=[hK[sb_]])
                    p.op("dve", "tensor_copy", out=K_[P, tbs], in_=kpe[P, tbs], reads=[hkpe], writes=[hK[sb_]])
                    psv, hpsv = nb67()
                    for tt in range(4):
                        tok = slice(tb * TB + tt * 128, tb * TB + (tt + 1) * 128)
                        for kk in range(2):
                            p.op("pe", "matmul", psv[:, tt * 64:(tt + 1) * 64], lhsT=ckvn[:, kk, tok], rhs=wkvv[:, kk, h * 128 + 64:h * 128 + 128], start=(kk == 0), stop=(kk == 1), reads=[hw_kv, hckvn], writes=[hpsv], inc=(kk == 1 and tt == 3))
                    p.op("act", "activation", out=V_[:, tb * 4:(tb + 1) * 4, 0:64], in_=psv[:, 0:256].rearrange("p (a b) -> p a b", a=4), func=AF.Copy, reads=[hpsv], writes=[hV[sb_]])
                for qb in range(4):
                    qs = slice(qb * TB, (qb + 1) * TB)
                    nk = 4 * (qb + 1)
                    abk = 2 + (qbc[0] % 2)
                    hacc_ = hacc[qbc[0] % 2]
                    qbc[0] += 1
                    accb = banks[abk]
                    for kt in range(nk):
                        sbk = stb[sti[0] % 4]
                        sti[0] += 1
                        p.op("pe", "matmul", banks[sbk][:], lhsT=K_[:, kt * 128:(kt + 1) * 128], rhs=Q_[:, qs], start=True, stop=True, reads=[hK[sb_], hQ[sb_][qb]], writes=[hb[sbk]])
                        E_, hE_ = Eb[ei % 4], hE[ei % 4]
                        ei += 1
                        p.op("act", "activation", out=E_, in_=banks[sbk][:], func=AF.Exp, scale=SC, reads=[hb[sbk]], writes=[hE_])
                        kl = kt - 4 * qb
                        if kl >= 0:
                            p.op("dve", "tensor_tensor", out=E_[:, kl * 128:(kl + 1) * 128], in0=E_[:, kl * 128:(kl + 1) * 128], in1=cmask, op=ALU.mult, reads=[hE_, hcm], writes=[hE_])
                        for j in range(4):
                            qt = 4 * qb + j
                            if kt <= qt:
                                first = (kt == 0 and j == 0)
                                last = (kt == nk - 1 and j == 3)
                                p.op("pe", "matmul", accb[:, j * 66:j * 66 + 65], lhsT=E_[:, j * 128:(j + 1) * 128], rhs=V_[:, kt, 0:65], start=first, stop=last, skip_group_check=True,
                                     reads=[hE_, hV[sb_]], writes=[hb[abk]], inc=(kt == qt))
                    pr, hpr = pairb[pi_ % 2], hpair[pi_ % 2]
                    pi_ += 1
                    for j in range(4):
                        p.op("dve", "reciprocal", out=rc[:, j:j + 1], in_=accb[:, j * 66 + 64:j * 66 + 65], reads=[hb[abk]], writes=[hrc])
                        p.op("act", "activation", out=pr[:, j, :], in_=accb[:, j * 66:j * 66 + 64], func=AF.Copy, scale=rc[:, j:j + 1], reads=[hb[abk], hrc], writes=[hpr])
                    pst_b, hpst = nb67()
                    pst = pst_b[:].bitcast(BF16)
                    prow = slice(hq * 64, (hq + 1) * 64)
                    for j in range(4):
                        p.op("pe", "transpose", out=pst[prow, j * 128:(j + 1) * 128], in_=pr[:, j, :], identity=ident, reads=[hpr, hcm], writes=[hpst], inc=(j == 3))
                    p.op("dve", "tensor_copy", out=attnT[prow, hp, qs], in_=pst[prow, 0:512], reads=[hpst], writes=[hattnT])

            p.barrier()
            w_o, hw_o = load_w("mla_o")
            wov = v3(w_o[:], 8)
            apos[0] = R1
            ysb = v3(carve_f32(8 * 512), 8)
            hysb = H("ysb")
            sq = v3(carve_bf(8 * 512), 8)
            hsq = H("sq")
            rstd = carve_f32(512)
            hrstd = H("rstd")
            lnt = carve_f32(512)
            hlnt = H("lnt")
            for tb in range(NTB):
                tbs = slice(tb * TB, (tb + 1) * TB)
                for oc in range(8):
                    ps, hps = next_bank()
                    for kk in range(8):
                        p.op("pe", "matmul", ps[:], lhsT=wov[:, kk, oc * 128:(oc + 1) * 128], rhs=attnT[:, kk, tbs], start=(kk == 0), stop=(kk == 7), reads=[hw_o, hattnT], writes=[hps], inc=(kk == 7))
                    evac_y(ps, hps, oc, ysb, hysb, sq, hsq)
                postnorm(tb, f"norm_mix_post{i}", ysb, hysb, sq, hsq, rstd, hrstd, lnt, hlnt)
            p.barrier()


        def ssm_mixer(i):
            NB = 256
            r45 = [0]

            def nb45():
                q_ = 4 + r45[0] % 2
                r45[0] += 1
                return banks[q_], hb[q_]
            NTB2 = S // NB
            HALO = 4

            def v3(ap, a):
                return ap.rearrange("p (a b) -> p a b", a=a)

            apos[0] = 0
            hbuf = v3(carve_bf(8 * (HALO + NB)), 8)
            hh = H("h")
            sq = v3(carve_bf(8 * NB), 8)
            hsq = H("sq")
            rstd = carve_f32(NB)
            hrstd = H("rstd")
            lnt = carve_f32(NB)
            hlnt = H("lnt")
            XC = v3(carve_bf(16 * NB), 16)
            hXC = H("XC")
            xs_tok = v3(carve_bf(2 * 2048), 2)
            hxs = H("xs_tok")
            B_tok = v3(carve_bf(2 * 1024), 2)
            hBt = H("B_tok")
            gT = v3(carve_bf(16 * NB), 16)
            hgT = H("gT")
            Sst = carve_f32(2048)
            hS = H("S")
            Sbf = carve_bf(2048)
            hSbf = H("Sbf")
            abc = carve_f32(32)
            habc = H("a_bc")
            small = [carve_f32(32) for _ in range(8)]
            hsm = [H(f"small{q}") for q in range(8)]
            rawc = [carve_bf(HALO + NB) for _ in range(2)]
            hraw = [H("raw0"), H("raw1")]
            dg = [v3(carve_bf(4 * 128), 4) for _ in range(2)]
            hdg = [H("dg0"), H("dg1")]
            xsTc = [carve_bf(NB) for _ in range(2)]
            hxsT = [H("xsT0"), H("xsT1")]
            rhsb = [carve_f32(128) for _ in range(4)]
            hrhs = [H(f"rhs{q}") for q in range(4)]
            dec = [carve_bf(512) for _ in range(2)]
            hdec = [H("dec0"), H("dec1")]
            Mb = [v3(carve_bf(512), 4) for _ in range(2)]
            hM = [H("M0"), H("M1")]
            CBm = v3(carve_bf(8 * 128), 8)
            hCB = H("CBm")
            yb = carve_f32(1024)
            hyb = H("yb")
            gn = carve_bf(1024)
            hgn = H("gn")
            szb = [carve_bf(512) for _ in range(2)]
            hsz = [H("sz0"), H("sz1")]
            junk = carve_bf(256)
            hjunk = H("junk")
            ss = carve_f32(4)
            hss = H("ss")
            rs4 = carve_f32(4)
            hrs4 = H("rs4")
            alias0 = apos[0]
            hAl = H("aliasA")
            hBl = H("aliasB")
            xdt = carve_bf(2048)
            hxdt = hAl
            xw = carve_bf(2048)
            hxw = hAl
            xsD = carve_bf(2048)
            hxsD = hBl
            alias1 = apos[0]
            apos[0] = alias0
            ysb = v3(carve_f32(8 * NB), 8)
            hysb = hAl
            sq2 = v3(carve_bf(8 * NB), 8)
            hsq2 = hBl
            assert apos[0] <= alias1
            apos[0] = alias1

            cw = col("ssm_conv_w", 0, 128).rearrange("p (c k) -> p c k", c=32)
            dbc = col("ssm_d", 0, 32)
            p.op("act", "activation", out=abc, in_=col("ssm_alog", 0, 32), func=AF.Exp, reads=[hcv], writes=[habc])
            p.op("dve", "tensor_scalar", out=abc, in0=abc, scalar1=-1.0, scalar2=None, op0=ALU.mult, reads=[habc], writes=[habc])
            p.op("dve", "memset", hbuf[:, :, 0:HALO], 0.0, writes=[hh])
            p.op("dve", "memset", Sst, 0.0, writes=[hS])
            p.op("dve", "memset", Sbf, 0.0, writes=[hSbf])
            wdtv = v3(carve_bf(8 * 32), 8)
            hw_dt = H("w_dt")
            p.dma("pool", wdtv, w_d[L.tile_ids["ssm_dt"]].rearrange("p (a b) -> p a b", a=8)[:, :, 0:32], writes=[hw_dt])
            load_w2 = load_w

            ri = 0
            for tb in range(NTB2):
                tsl = slice(tb * NB, (tb + 1) * NB)
                hxh = hx[tb // 2]
                xs = x_sb[:, :, tsl]
                hcur = hbuf[:, :, HALO:HALO + NB]
                if tb > 0:
                    p.op("dve", "tensor_copy", out=hbuf[:, :, 0:HALO], in_=hbuf[:, :, NB:NB + HALO], reads=[hh], writes=[hh])
                for c in range(8):
                    p.op("act", "activation", out=sq[:, c, :], in_=xs[:, c, :], func=AF.Square, reads=[hxh], writes=[hsq])
                rstd_from_sq(sq, hsq, 8, D, rstd, hrstd, lnt, hlnt, n=NB)
                for c in range(8):
                    p.op("dve", "scalar_tensor_tensor", out=hcur[:, c, :], in0=xs[:, c, :], scalar=col(f"norm_mix_pre{i}", c), in1=rstd, op0=ALU.mult, op1=ALU.mult, reads=[hxh, hrstd, hcv], writes=[hh])
                ws = None
                for c in range(32):
                    if c % 8 == 0:
                        ws, hws = load_w2(f"ssm_xbc_{c // 8}")
                        wv = v3(ws[:], 8)
                    ps, hps = next_bank()
                    for kk in range(8):
                        p.op("pe", "matmul", ps[:, 0:HALO + NB], lhsT=wv[:, kk, (c % 8) * 128:(c % 8 + 1) * 128], rhs=hbuf[:, kk, :], start=(kk == 0), stop=(kk == 7), reads=[hws, hh], writes=[hps], inc=(kk == 7))
                    r_, hr_ = rawc[ri % 2], hraw[ri % 2]
                    d_, hd_ = dg[ri % 2], hdg[ri % 2]
                    xt_, hxt_ = xsTc[ri % 2], hxsT[ri % 2]
                    ri += 1
                    p.op("act", "activation", out=r_, in_=ps[:, 0:HALO + NB], func=AF.Copy, reads=[hps], writes=[hr_])
                    p.op("dve", "tensor_tensor", out=d_, in0=ident.unsqueeze(1).to_broadcast([128, 4, 128]), in1=cw[:, c, :].unsqueeze(2).to_broadcast([128, 4, 128]), op=ALU.mult, reads=[hcm, hcv], writes=[hd_])
                    ps2, hps2 = next_bank()
                    for kk in range(4):
                        p.op("pe", "matmul", ps2[:, 0:NB], lhsT=d_[:, kk, :], rhs=r_[:, 1 + kk:1 + kk + NB], start=(kk == 0), stop=(kk == 3), reads=[hd_, hr_], writes=[hps2], inc=(kk == 3))
                    cb = col("ssm_conv_b", c)
                    if c < 24:
                        dst, hdst = xt_, hxt_
                    if c >= 16:
                        dst2, hdst2 = XC[:, c - 16, :], hXC
                    if c < 16:
                        p.op("act", "activation", out=xt_, in_=ps2[:, 0:NB], func=AF.Silu, bias=cb, reads=[hps2, hcv], writes=[hxt_])
                        src_t, hsrc_t = xt_, hxt_
                    else:
                        p.op("act", "activation", out=XC[:, c - 16, :], in_=ps2[:, 0:NB], func=AF.Silu, bias=cb, reads=[hps2, hcv], writes=[hXC])
                        src_t, hsrc_t = XC[:, c - 16, :], hXC
                    if c < 24:
                        bk = 6 + (c % 2)
                        pst = banks[bk][:].bitcast(BF16)
                        for tt in range(2):
                            p.op("pe", "transpose", out=pst[:, tt * 128:(tt + 1) * 128], in_=src_t[:, tt * 128:(tt + 1) * 128], identity=ident, reads=[hsrc_t, hcm], writes=[hb[bk]], inc=(tt == 1))
                        if c < 16:
                            p.op("dve", "tensor_copy", out=xs_tok[:, :, c * 128:(c + 1) * 128], in_=pst[:, 0:256].rearrange("p (a b) -> p a b", a=2), reads=[hb[bk]], writes=[hxs])
                        else:
                            p.op("dve", "tensor_copy", out=B_tok[:, :, (c - 16) * 128:(c - 15) * 128], in_=pst[:, 0:256].rearrange("p (a b) -> p a b", a=2), reads=[hb[bk]], writes=[hBt])
                wz = []
                for j in range(2):
                    wzj, hwzj = load_w2(f"ssm_z_{j}")
                    wz.append((v3(wzj[:], 8), hwzj))
                for tt in range(2):
                    tg = tb * 2 + tt
                    tok = slice(HALO + tt * 128, HALO + (tt + 1) * 128)
                    tk = slice(tt * 128, (tt + 1) * 128)
                    xu, dtv, adt, acs, tot_, ev, dte, cdv = small
                    hxu, hdt, hadt, hacs, htot, hev, hdte, hcd = hsm
                    ps, hps = nb45()
                    for kk in range(8):
                        p.op("pe", "matmul", ps[:, 0:32], lhsT=hbuf[:, kk, tok], rhs=wdtv[:, kk, 0:32], start=(kk == 0), stop=(kk == 7), reads=[hw_dt, hh], writes=[hps], inc=(kk == 7))
                    p.op("dve", "tensor_tensor", out=xu, in0=ps[:, 0:32], in1=col("ssm_dtb", 0, 32), op=ALU.add, reads=[hps, hcv], writes=[hxu])
                    p.op("act", "activation", out=adt, in_=xu, func=AF.Abs, reads=[hxu], writes=[hadt])
                    p.op("act", "activation", out=adt, in_=adt, func=AF.Exp, scale=-1.0, reads=[hadt], writes=[hadt])
                    p.op("act", "activation", out=adt, in_=adt, func=AF.Ln, bias=col("one"), reads=[hadt, hcv], writes=[hadt])
                    p.op("dve", "scalar_tensor_tensor", out=dtv, in0=xu, scalar=0.0, in1=adt, op0=ALU.max, op1=ALU.add, reads=[hxu, hadt], writes=[hdt])
                    p.op("dve", "tensor_tensor", out=adt, in0=dtv, in1=abc, op=ALU.mult, reads=[hdt, habc], writes=[hadt])
                    ps, hps = nb45()
                    p.op("pe", "matmul", ps[:, 0:32], lhsT=tri32, rhs=adt, start=True, stop=True, reads=[hadt, hcm], writes=[hps], inc=False)
                    p.op("pe", "matmul", ps[:, 32:64], lhsT=ones32, rhs=adt, start=True, stop=True, reads=[hadt, hcm], writes=[hps])
                    p.op("act", "activation", out=acs, in_=ps[:, 0:32], func=AF.Copy, reads=[hps], writes=[hacs])
                    p.op("act", "activation", out=ev, in_=ps[:, 0:32], func=AF.Exp, reads=[hps], writes=[hev])
                    p.op("act", "activation", out=cdv, in_=ps[:, 32:64], func=AF.Exp, reads=[hps], writes=[hcd])
                    p.op("dve", "tensor_tensor", out=tot_, in0=ps[:, 32:64], in1=acs, op=ALU.subtract, reads=[hps, hacs], writes=[htot])
                    p.op("act", "activation", out=dte, in_=tot_, func=AF.Exp, reads=[htot], writes=[hdte])
                    xs3 = xs_tok[:, tt, :].rearrange("p (h q) -> p h q", h=32)
                    p.op("dve", "tensor_tensor", out=xdt.rearrange("p (h q) -> p h q", h=32), in0=xs3, in1=dtv.unsqueeze(2).to_broadcast([128, 32, 64]), op=ALU.mult, reads=[hxs, hdt], writes=[hxdt])
                    p.op("dve", "tensor_tensor", out=xsD.rearrange("p (h q) -> p h q", h=32), in0=xs3, in1=dbc.unsqueeze(2).to_broadcast([128, 32, 64]), op=ALU.mult, reads=[hxs, hcv], writes=[hxsD])
                    p.op("dve", "tensor_tensor", out=xw.rearrange("p (h q) -> p h q", h=32), in0=xdt.rearrange("p (h q) -> p h q", h=32), in1=dte.unsqueeze(2).to_broadcast([128, 32, 64]), op=ALU.mult, reads=[hxdt, hdte], writes=[hxw])
                    for gh in range(2):
                        ps, hps = nb45()
                        for g4 in range(4):
                            g = gh * 4 + g4
                            p.op("pe", "matmul", ps[:, g4 * 128:(g4 + 1) * 128], lhsT=XC[:, g, tk], rhs=XC[:, 8 + g, tk], start=True, stop=True, reads=[hXC], writes=[hps], inc=(g4 == 3))
                        p.op("dve", "tensor_tensor", out=CBm[:, gh * 4:(gh + 1) * 4, :], in0=ps[:].rearrange("p (a b) -> p a b", a=4), in1=tri_bf.unsqueeze(1).to_broadcast([128, 4, 128]), op=ALU.mult, reads=[hps, hcm], writes=[hCB])
                    for hf in range(2):
                        yd = [(banks[0], hb[0]), (banks[1], hb[1])]
                        for q in range(2):
                            cols = slice(hf * 1024 + q * 512, hf * 1024 + (q + 1) * 512)
                            p.op("pe", "matmul", yd[q][0][:], lhsT=ident, rhs=xsD[:, cols], start=True, stop=False, reads=[hxsD, hcm], writes=[yd[q][1]], inc=False)
                        for g4 in range(4):
                            g = hf * 4 + g4
                            psd, hpsd = nb45()
                            for r in range(4):
                                hd = g * 4 + r
                                rb, hrb = rhsb[r], hrhs[r]
                                p.op("dve", "tensor_scalar", out=rb, in0=tri32, scalar1=adt[:, hd:hd + 1], scalar2=None, op0=ALU.mult, reads=[hcm, hadt], writes=[hrb])
                                p.op("pe", "matmul", psd[:, r * 128:(r + 1) * 128], lhsT=mstrict32, rhs=rb, start=True, stop=True, reads=[hrb, hcm], writes=[hpsd], inc=(r == 3))
                            dc, hdc = dec[g4 % 2], hdec[g4 % 2]
                            M_, hM_ = Mb[g4 % 2], hM[g4 % 2]
                            p.op("act", "activation", out=dc, in_=psd[:], func=AF.Exp, reads=[hpsd], writes=[hdc])
                            p.op("dve", "tensor_tensor", out=M_, in0=dc.rearrange("p (a b) -> p a b", a=4), in1=CBm[:, g, :].unsqueeze(1).to_broadcast([128, 4, 128]), op=ALU.mult, reads=[hdc, hCB], writes=[hM_])
                            for r in range(4):
                                hd = g * 4 + r
                                hl = hd - hf * 16
                                q, cq_ = hl // 8, hl % 8
                                last = (cq_ == 7)
                                p.op("pe", "matmul", yd[q][0][:, cq_ * 64:(cq_ + 1) * 64], lhsT=M_[:, r, :], rhs=xdt[:, hd * 64:(hd + 1) * 64], start=False, stop=last, reads=[hM_, hxdt], writes=[yd[q][1]], inc=last)
                        if tg > 0:
                            yo = [(banks[2], hb[2]), (banks[3], hb[3])]
                            for g4 in range(4):
                                g = hf * 4 + g4
                                q, cg = g4 // 2, g4 % 2
                                p.op("pe", "matmul", yo[q][0][:, cg * 256:(cg + 1) * 256], lhsT=XC[:, 8 + g, tk], rhs=Sbf[:, g * 256:(g + 1) * 256], start=True, stop=True, reads=[hXC, hSbf], writes=[yo[q][1]], inc=(cg == 1))
                            for q in range(2):
                                hsl = slice(hf * 16 + q * 8, hf * 16 + (q + 1) * 8)
                                ysl = yb[:, q * 512:(q + 1) * 512]
                                p.op("dve", "tensor_tensor", out=ysl.rearrange("p (h q) -> p h q", h=8), in0=yo[q][0][:].rearrange("p (h q) -> p h q", h=8), in1=ev[:, hsl].unsqueeze(2).to_broadcast([128, 8, 64]), op=ALU.mult, reads=[yo[q][1], hev], writes=[hyb])
                                p.op("dve", "tensor_tensor", out=ysl, in0=ysl, in1=yd[q][0][:], op=ALU.add, reads=[hyb, yd[q][1]], writes=[hyb])
                        else:
                            for q in range(2):
                                p.op("act", "activation", out=yb[:, q * 512:(q + 1) * 512], in_=yd[q][0][:], func=AF.Copy, reads=[yd[q][1]], writes=[hyb])
                        wzv, hwz = wz[hf]
                        for q in range(2):
                            psz, hpsz = nb45()
                            for kk in range(8):
                                p.op("pe", "matmul", psz[:], lhsT=hbuf[:, kk, tok], rhs=wzv[:, kk, q * 512:(q + 1) * 512], start=(kk == 0), stop=(kk == 7), reads=[hwz, hh], writes=[hpsz], inc=(kk == 7))
                            sz_, hsz_ = szb[q], hsz[q]
                            p.op("act", "activation", out=sz_, in_=psz[:], func=AF.Silu, reads=[hpsz], writes=[hsz_])
                            ysl = yb[:, q * 512:(q + 1) * 512]
                            p.op("dve", "tensor_tensor", out=ysl, in0=ysl, in1=sz_, op=ALU.mult, reads=[hyb, hsz_], writes=[hyb])
                            for g2 in range(2):
                                p.op("act", "activation", out=junk, in_=ysl[:, g2 * 256:(g2 + 1) * 256], func=AF.Square, accum_out=ss[:, q * 2 + g2:q * 2 + g2 + 1], reads=[hyb], writes=[hjunk, hss])
                        p.op("act", "activation", out=rs4, in_=ss, func=AF.Ln, scale=1.0 / 256, bias=col("eps"), reads=[hss, hcv], writes=[hrs4])
                        p.op("act", "activation", out=rs4, in_=rs4, func=AF.Exp, scale=-0.5, reads=[hrs4], writes=[hrs4])
                        p.op("dve", "tensor_tensor", out=gn.rearrange("p (a b) -> p a b", a=4), in0=yb.rearrange("p (a b) -> p a b", a=4), in1=rs4.unsqueeze(2).to_broadcast([128, 4, 256]), op=ALU.mult, reads=[hyb, hrs4], writes=[hgn])
                        for qg in range(2):
                            bk = 6 + (qg % 2)
                            pst = banks[bk][:].bitcast(BF16)
                            for j in range(4):
                                cc = qg * 4 + j
                                p.op("pe", "transpose", out=pst[:, j * 128:(j + 1) * 128], in_=gn[:, cc * 128:(cc + 1) * 128], identity=ident, reads=[hgn, hcm], writes=[hb[bk]], inc=(j == 3))
                            for j in range(4):
                                cc = hf * 8 + qg * 4 + j
                                p.op("act", "activation", out=gT[:, cc, tk], in_=pst[:, j * 128:(j + 1) * 128], func=AF.Copy, scale=col("ssm_norm_w", cc), reads=[hb[bk], hcv], writes=[hgT])
                    for gh in range(4):
                        ps, hps = nb45()
                        for g2 in range(2):
                            g = gh * 2 + g2
                            p.op("pe", "matmul", ps[:, g2 * 256:(g2 + 1) * 256], lhsT=B_tok[:, tt, g * 128:(g + 1) * 128], rhs=xw[:, g * 256:(g + 1) * 256], start=True, stop=True, reads=[hBt, hxw], writes=[hps], inc=(g2 == 1))
                        for r in range(8):
                            hd = gh * 8 + r
                            p.op("dve", "scalar_tensor_tensor", out=Sst[:, hd * 64:(hd + 1) * 64], in0=Sst[:, hd * 64:(hd + 1) * 64], scalar=cdv[:, hd:hd + 1], in1=ps[:, r * 64:(r + 1) * 64], op0=ALU.mult, op1=ALU.add, reads=[hS, hcd, hps], writes=[hS])
                    p.op("act", "activation", out=Sbf, in_=Sst, func=AF.Copy, reads=[hS], writes=[hSbf])
                wo = []
                for j in range(2):
                    woj, hwoj = load_w2(f"ssm_out_{j}")
                    wo.append((v3(woj[:], 8), hwoj))
                for oc in range(8):
                    ps, hps = next_bank()
                    for kk in range(16):
                        wv_, hwv_ = wo[kk // 8]
                        p.op("pe", "matmul", ps[:, 0:NB], lhsT=wv_[:, kk % 8, oc * 128:(oc + 1) * 128], rhs=gT[:, kk, :], start=(kk == 0), stop=(kk == 15), reads=[hwv_, hgT], writes=[hps], inc=(kk == 15))
                    p.op("act", "activation", out=ysb[:, oc, :], in_=ps[:, 0:NB], func=AF.Copy, reads=[hps], writes=[hysb])
                    p.op("act", "activation", out=sq2[:, oc, :], in_=ps[:, 0:NB], func=AF.Square, reads=[hps], writes=[hsq2])
                rstd_from_sq(sq2, hsq2, 8, D, rstd, hrstd, lnt, hlnt, n=NB)
                for c in range(8):
                    p.op("dve", "scalar_tensor_tensor", out=ysb[:, c, :], in0=ysb[:, c, :], scalar=col(f"norm_mix_post{i}", c), in1=rstd, op0=ALU.mult, op1=ALU.mult, reads=[hysb, hrstd, hcv], writes=[hysb])
                    p.op("dve", "tensor_tensor", out=xs[:, c, :], in0=xs[:, c, :], in1=ysb[:, c, :], op=ALU.add, reads=[hysb, hxh], writes=[hxh])

        for st in stages:
            kind, i = st
            p.cur_reorder = (kind != "conv") or os.environ.get("K_CONV_REORDER", "0") == "1"
            p.barrier()
            if kind == "ffn":
                ffn(i)
            elif kind == "conv":
                conv_mixer(i, i // 3)
            elif kind == "mla":
                mla_mixer(i)
            elif kind == "ssm":
                ssm_mixer(i)
            else:
                raise ValueError(st)

        ov = o_d.rearrange("(c q) t -> q c t", q=128)
        toks = []
        for tb in range(NTB):
            toks.append(p.dma("sp", ov[:, :, tb * TB:(tb + 1) * TB], x_sb[:, :, tb * TB:(tb + 1) * TB], reads=[hx[tb]]))
        for t in toks:
            p.wait_tok("sp", t)
        p.replay(block, sems)
        nc._dbg_prog = p
    return nc


ALL_STAGES = [("conv", 0), ("ffn", 0), ("ssm", 1), ("ffn", 1), ("mla", 2), ("ffn", 2), ("conv", 3), ("ffn", 3)]


def add_const_cols(L):
    L.add_cols("eps", np.full((128, 1), EPS, np.float32))
    L.add_cols("one", np.full((128, 1), 1.0, np.float32))
    jj = np.arange(128) % 32
    inv = (np.float32(10000.0) ** (-(np.arange(16, dtype=np.float32) / np.float32(16.0)))).astype(np.float32)
    L.add_cols("rope_inv", inv[jj % 16].reshape(128, 1))
    L.add_cols("rope_sign", np.where(jj < 16, -1.0, 1.0).astype(np.float32).reshape(128, 1))


def const_mats():
    kk_, qq_ = np.meshgrid(np.arange(128), np.arange(128), indexing="ij")
    cmask = ((kk_ // 64) <= (qq_ // 64)).astype(np.float32)
    tri = (kk_ <= qq_).astype(np.float32)
    mstrict = (kk_ > qq_).astype(np.float32)
    cmat = np.stack([np.eye(128, dtype=np.float32), np.ones((128, 128), np.float32), cmask, tri], 1)
    cmat32 = np.stack([tri, mstrict, np.ones((128, 128), np.float32)], 1)
    return cmat, cmat32


def run(inputs, stages, trace=False):
    inp = {k: np.asarray(v) for k, v in inputs.items()}
    L = make_layout(inp)
    add_const_cols(L)
    wts = np.stack(L.tiles, 0)
    cvec = np.concatenate(L.cols, 1)
    cmat, cmat32 = const_mats()
    nc = build_program(L, stages)
    x = inp["x"]
    in_maps = []
    for b in range(8):
        in_maps.append({"x": np.ascontiguousarray(x[b].T), "wts": wts, "cvec": cvec, "cmat": cmat, "cmat32": cmat32,
                        "pos": np.ascontiguousarray(inp["positions"][b].reshape(1, S).astype(np.int32))})
    res = run_bass_kernel_spmd(nc, in_maps, core_ids=list(range(8)), trace=trace)
    out = np.stack([res.results[b]["out"].T for b in range(8)], 0)
    return np.ascontiguousarray(out.astype(np.float32)), res


def kernel(**inputs):
    out, _ = run(inputs, ALL_STAGES)
    return out
```
